# Optimizing a Trainium2 kernel written in Bass

```python
import math
import numpy as np
import jax, jax.numpy as jnp
from jax import lax

D_MODEL = 2048
BATCH = 1
SEQ = 8192
DEPTH = 1

N_SUB = 3
DIFF_HEADS = 8
DIFF_QK_DIM = 64
DIFF_V_DIM = 2 * DIFF_QK_DIM
DN_HEADS = 8
DN_K_DIM = 128
DN_V_DIM = 128
DN_CONV = 4
DN_CHUNK = 64
DT_MIN = 1e-3
DT_MAX = 1e-1
D_FF = ((8 * D_MODEL // 3 + 255) // 256) * 256
REL_BUCKETS = 32
REL_MAX_DIST = 128
Q_BLOCK = 128
DEEPNORM_ALPHA = (2 * DEPTH) ** 0.25
DEEPNORM_BETA = (8 * DEPTH) ** -0.25
LN_EPS = 1e-5
RMS_EPS = 1e-6
IN_SIZES = (DIFF_HEADS * 2 * DIFF_QK_DIM, DIFF_HEADS * 2 * DIFF_QK_DIM, DIFF_HEADS * DIFF_V_DIM,
            DN_HEADS * DN_K_DIM, DN_HEADS * DN_K_DIM, DN_HEADS * DN_V_DIM, DN_HEADS * DN_V_DIM,
            DN_HEADS, DN_HEADS, D_MODEL, D_MODEL)
D_IN = sum(IN_SIZES)

kernel_name = 'hybrid_diffattn_gdn_macaron_deepnorm_adaln'


def layer_norm(x, g, b):
    xf = x.astype(jnp.float32)
    mu = xf.mean(-1, keepdims=True)
    var = jnp.square(xf - mu).mean(-1, keepdims=True)
    return ((xf - mu) * lax.rsqrt(var + LN_EPS) * g.astype(jnp.float32) + b.astype(jnp.float32)).astype(x.dtype)


def rms_norm(x, w, eps):
    xf = x.astype(jnp.float32)
    return (xf * lax.rsqrt(jnp.mean(xf * xf, -1, keepdims=True) + eps) * w.astype(jnp.float32)).astype(x.dtype)


def l2_normalize(x):
    return x * lax.rsqrt(jnp.sum(x * x, -1, keepdims=True) + RMS_EPS)


def split_cols(t, sizes):
    return jnp.split(t, np.cumsum(sizes)[:-1].tolist(), axis=-1)


def swiglu_ffn(h, w_in, w_out):
    g, u = jnp.split(h @ w_in, 2, axis=-1)
    return (jax.nn.silu(g) * u) @ w_out


def t5_bucket(rel):
    n = jnp.maximum(rel, 0)
    max_exact = REL_BUCKETS // 2
    nf = jnp.maximum(n, max_exact).astype(jnp.float32)
    large = max_exact + (jnp.log(nf / max_exact) / math.log(REL_MAX_DIST / max_exact)
                         * (REL_BUCKETS - max_exact)).astype(jnp.int32)
    large = jnp.minimum(large, REL_BUCKETS - 1)
    return jnp.where(n < max_exact, n, large)


def diff_attention(q, k, v, lam, rel_bias):
    B, S, H, _, dk = q.shape
    nblk = S // Q_BLOCK
    qb = q.reshape(B, nblk, Q_BLOCK, H, 2, dk).swapaxes(0, 1)
    kpos = jnp.arange(S, dtype=jnp.int32)
    scale = DIFF_QK_DIM ** -0.5

    def block(args):
        qi, bi = args
        qpos = bi * Q_BLOCK + jnp.arange(Q_BLOCK, dtype=jnp.int32)
        rel = qpos[:, None] - kpos[None, :]
        bias = rel_bias[t5_bucket(rel)].astype(jnp.float32).transpose(2, 0, 1)
        s = jnp.einsum('bqhmd,bkhmd->bhmqk', qi, k).astype(jnp.float32) * scale
        s = s + bias[None, :, None]
        s = jnp.where(rel >= 0, s, -jnp.inf)
        p = jax.nn.softmax(s, axis=-1)
        p = p[:, :, 0] - lam * p[:, :, 1]
        return jnp.einsum('bhqk,bkhd->bqhd', p.astype(v.dtype), v)

    o = lax.map(block, (qb, jnp.arange(nblk, dtype=jnp.int32)))
    return o.swapaxes(0, 1).reshape(B, S, H, v.shape[-1])


def gated_delta_rule(q, k, v, g, beta):
    B, S, H, dk = q.shape
    dv = v.shape[-1]
    C = DN_CHUNK
    N = S // C
    c5 = lambda t: t.reshape(B, N, C, H, t.shape[-1]).transpose(1, 0, 3, 2, 4)
    c4 = lambda t: t.reshape(B, N, C, H).transpose(1, 0, 3, 2)
    qc, kc, vc = c5(q), c5(k), c5(v)
    gc, bc = c4(g), c4(beta)
    G = jnp.cumsum(gc, axis=-1)
    tril = jnp.tril(jnp.ones((C, C), bool))
    strict = jnp.tril(jnp.ones((C, C), bool), -1)
    gdiff = G[..., :, None] - G[..., None, :]
    decay = jnp.where(tril, jnp.exp(jnp.where(tril, gdiff, 0.0)), 0.0)
    kb = kc * bc[..., None]
    A = jnp.where(strict, jnp.einsum('nbhid,nbhjd->nbhij', kb, kc) * decay, 0.0)
    eye = jnp.eye(C, dtype=jnp.float32)
    T = lax.linalg.triangular_solve(A + eye, jnp.broadcast_to(eye, A.shape), left_side=True,
                                    lower=True, unit_diagonal=True)
    u = T @ (vc * bc[..., None])
    w = T @ (kb * jnp.exp(G)[..., None])
    qk = jnp.where(tril, jnp.einsum('nbhid,nbhjd->nbhij', qc, kc) * decay, 0.0)
    q_dec = qc * jnp.exp(G)[..., None]
    k_dec = kc * jnp.exp(G[..., -1:] - G)[..., None]
    g_last = jnp.exp(G[..., -1])

    def step(state, inp):
        u_c, w_c, qd_c, kd_c, qk_c, gl_c = inp
        v_new = u_c - w_c @ state
        o = qd_c @ state + qk_c @ v_new
        state = state * gl_c[..., None, None] + jnp.einsum('bhcd,bhce->bhde', kd_c, v_new)
        return state, o

    s0 = jnp.zeros((B, H, dk, dv), jnp.float32)
    _, o = lax.scan(step, s0, (u, w, q_dec, k_dec, qk, g_last))
    return o.transpose(1, 0, 3, 2, 4).reshape(B, S, H, dv)


def hybrid_mixer(h, w_in, conv_w, a_log, dt_bias, dn_norm_w, lam_params, subln_w, rel_bias,
                 w_a, w_b, w_o, lam_init):
    B, S, _ = h.shape
    dq, dk_, dv_, nq, nk, nv, nz, nb, na, ga, gb = split_cols(h @ w_in, IN_SIZES)
    lp = lam_params.astype(jnp.float32)
    lam = jnp.exp(jnp.sum(lp[0] * lp[1])) - jnp.exp(jnp.sum(lp[2] * lp[3])) + lam_init
    ya = diff_attention(dq.reshape(B, S, DIFF_HEADS, 2, DIFF_QK_DIM),
                        dk_.reshape(B, S, DIFF_HEADS, 2, DIFF_QK_DIM),
                        dv_.reshape(B, S, DIFF_HEADS, DIFF_V_DIM), lam, rel_bias)
    ya = rms_norm(ya, subln_w, LN_EPS) * (1.0 - lam_init)
    ya = ya.reshape(B, S, DIFF_HEADS * DIFF_V_DIM) @ w_a
    qkv = jnp.concatenate([nq, nk, nv], axis=-1)
    ch = qkv.shape[-1]
    qkv = lax.conv_general_dilated(qkv, conv_w[:, None, :], window_strides=(1,),
                                   padding=((DN_CONV - 1, 0),),
                                   dimension_numbers=('NWC', 'WIO', 'NWC'), feature_group_count=ch)
    qkv = jax.nn.silu(qkv).astype(jnp.float32)
    cq, ck, cv = split_cols(qkv, (DN_HEADS * DN_K_DIM, DN_HEADS * DN_K_DIM, DN_HEADS * DN_V_DIM))
    q = l2_normalize(cq.reshape(B, S, DN_HEADS, DN_K_DIM)) * (DN_K_DIM ** -0.5)
    k = l2_normalize(ck.reshape(B, S, DN_HEADS, DN_K_DIM))
    v = cv.reshape(B, S, DN_HEADS, DN_V_DIM)
    beta = jax.nn.sigmoid(nb.astype(jnp.float32))
    g = -jnp.exp(a_log.astype(jnp.float32)) * jax.nn.softplus(na.astype(jnp.float32) + dt_bias.astype(jnp.float32))
    o = gated_delta_rule(q, k, v, g, beta)
    o = rms_norm(o, dn_norm_w, RMS_EPS) * jax.nn.silu(nz.reshape(B, S, DN_HEADS, DN_V_DIM).astype(jnp.float32))
    yb = o.reshape(B, S, DN_HEADS * DN_V_DIM).astype(h.dtype) @ w_b
    merged = jax.nn.sigmoid(ga) * ya + jax.nn.sigmoid(gb) * yb
    return merged @ w_o


def setup_inputs(seed: int = 0) -> dict:
    key = jax.random.key(seed)
    ks = jax.random.split(key, 20)
    f32 = jnp.float32
    nrm = lambda kk, shape, s: jax.random.normal(kk, shape, f32) * s
    diff_v_w = DIFF_HEADS * DIFF_V_DIM
    dn_v_w = DN_HEADS * DN_V_DIM
    conv_ch = 2 * DN_HEADS * DN_K_DIM + dn_v_w
    dt = jnp.exp(jax.random.uniform(ks[11], (DEPTH, DN_HEADS), f32, math.log(DT_MIN), math.log(DT_MAX)))
    return {
        'x': nrm(ks[0], (BATCH, SEQ, D_MODEL), 1.0),
        'c': nrm(ks[1], (BATCH, D_MODEL), 1.0),
        'w_ada': nrm(ks[2], (DEPTH, D_MODEL, N_SUB * 3 * D_MODEL), 0.5 * D_MODEL ** -0.5),
        'b_ada': nrm(ks[3], (DEPTH, N_SUB * 3 * D_MODEL), 0.02),
        'ln_g': 1.0 + nrm(ks[4], (DEPTH, N_SUB, D_MODEL), 0.02),
        'ln_b': nrm(ks[5], (DEPTH, N_SUB, D_MODEL), 0.02),
        'w_ffn_in': nrm(ks[6], (DEPTH, 2, D_MODEL, 2 * D_FF), D_MODEL ** -0.5),
        'w_ffn_out': nrm(ks[7], (DEPTH, 2, D_FF, D_MODEL), DEEPNORM_BETA * D_FF ** -0.5),
        'w_in': nrm(ks[8], (DEPTH, D_MODEL, D_IN), D_MODEL ** -0.5),
        'conv_w': nrm(ks[9], (DEPTH, DN_CONV, conv_ch), DN_CONV ** -0.5),
        'dn_a_log': jnp.log(jax.random.uniform(ks[10], (DEPTH, DN_HEADS), f32, 1.0, 16.0)),
        'dn_dt_bias': dt + jnp.log(-jnp.expm1(-dt)),
        'dn_norm_w': 1.0 + nrm(ks[12], (DEPTH, DN_V_DIM), 0.02),
        'diff_lambda': nrm(ks[13], (DEPTH, 4, DIFF_QK_DIM), 0.1),
        'diff_subln_w': 1.0 + nrm(ks[14], (DEPTH, DIFF_V_DIM), 0.02),
        'rel_bias': nrm(ks[15], (REL_BUCKETS, DIFF_HEADS), 0.5),
        'w_branch_a': nrm(ks[16], (DEPTH, diff_v_w, D_MODEL), diff_v_w ** -0.5),
        'w_branch_b': nrm(ks[17], (DEPTH, dn_v_w, D_MODEL), dn_v_w ** -0.5),
        'w_out': nrm(ks[18], (DEPTH, D_MODEL, D_MODEL), DEEPNORM_BETA * D_MODEL ** -0.5),
    }


def reference(x, c, w_ada, b_ada, ln_g, ln_b, w_ffn_in, w_ffn_out, w_in, conv_w, dn_a_log,
              dn_dt_bias, dn_norm_w, diff_lambda, diff_subln_w, rel_bias, w_branch_a, w_branch_b, w_out):
    B = x.shape[0]
    for l in range(DEPTH):
        lam_init = 0.8 - 0.6 * math.exp(-0.3 * l)
        ada = (jax.nn.silu(c) @ w_ada[l] + b_ada[l]).reshape(B, N_SUB, 3, D_MODEL)
        shift, scale, gate = ada[:, :, 0, None], ada[:, :, 1, None], ada[:, :, 2, None]
        y = swiglu_ffn(x * (1.0 + scale[:, 0]) + shift[:, 0], w_ffn_in[l, 0], w_ffn_out[l, 0])
        x = layer_norm(DEEPNORM_ALPHA * x + 0.5 * gate[:, 0] * y, ln_g[l, 0], ln_b[l, 0])
        y = hybrid_mixer(x * (1.0 + scale[:, 1]) + shift[:, 1], w_in[l], conv_w[l], dn_a_log[l],
                         dn_dt_bias[l], dn_norm_w[l], diff_lambda[l], diff_subln_w[l], rel_bias,
                         w_branch_a[l], w_branch_b[l], w_out[l], lam_init)
        x = layer_norm(DEEPNORM_ALPHA * x + gate[:, 1] * y, ln_g[l, 1], ln_b[l, 1])
        y = swiglu_ffn(x * (1.0 + scale[:, 2]) + shift[:, 2], w_ffn_in[l, 1], w_ffn_out[l, 1])
        x = layer_norm(DEEPNORM_ALPHA * x + 0.5 * gate[:, 2] * y, ln_g[l, 2], ln_b[l, 2])
    return x
```

```python
import math
import numpy as np
import ml_dtypes
import concourse.bass as bass
import concourse.mybir as mybir
from concourse.bass_utils import run_bass_kernel_spmd

F32 = mybir.dt.float32
BF16 = mybir.dt.bfloat16
AF = mybir.ActivationFunctionType
ALU = mybir.AluOpType
AX = mybir.AxisListType

NCORES = 8
D = 2048
S = 8192
TOK = S // NCORES
KC = D // 128
DFF = 5632
FFC = DFF // 128
ALPHA = 2.0 ** 0.25
LN_EPS = 1e-5
RMS_EPS = 1e-6
NDMA_SEM = 12


class Op:
    __slots__ = ("eng", "fn", "reads", "writes", "dma", "deps", "inc", "val", "sem_i", "barrier", "cost")

    def __init__(self, eng, fn, reads, writes, dma):
        self.eng, self.fn, self.reads, self.writes, self.dma = eng, fn, tuple(reads), tuple(writes), dma
        self.deps = []
        self.inc = False
        self.val = 0
        self.sem_i = 0
        self.barrier = False
        self.cost = None


class Prog:
    ENGS = ("pe", "act", "dve", "pool", "sp")

    def __init__(self, nc):
        self.nc = nc
        self.ops = []

    def add(self, eng, fn, reads=(), writes=(), dma=False, cost=None):
        op = Op(eng, fn, reads, writes, dma)
        op.cost = cost
        self.ops.append(op)

    def pe(self, fn, reads=(), writes=(), cost=None):
        self.add("pe", fn, reads, writes, cost=cost)

    def act(self, fn, reads=(), writes=()):
        self.add("act", fn, reads, writes)

    def dve(self, fn, reads=(), writes=()):
        self.add("dve", fn, reads, writes)

    def pool(self, fn, reads=(), writes=()):
        self.add("pool", fn, reads, writes)

    def dma(self, q, out, in_, reads=(), writes=()):
        self.add(q, lambda e: e.dma_start(out=out, in_=in_), reads, writes, dma=True)

    COST = {"pe": 0.3, "act": 0.45, "dve": 0.4, "pool": 0.5, "sp": 0.1}

    def interleave(self, i0, i1, i2):
        streams = [self.ops[i0:i1], self.ops[i1:i2]]
        HOP = 0.9
        deps = []
        for ops in streams:
            last_w, readers, dl = {}, {}, []
            for i, op in enumerate(ops):
                d = set()
                for r in op.reads:
                    if r in last_w:
                        d.add(last_w[r])
                for w in op.writes:
                    if w in last_w:
                        d.add(last_w[w])
                    d.update(readers.get(w, ()))
                d.discard(i)
                dl.append(d)
                for r in op.reads:
                    readers.setdefault(r, []).append(i)
                for w in op.writes:
                    last_w[w] = i
                    readers[w] = []
            deps.append(dl)
        fin = [[0.0] * len(st) for st in streams]
        ptr = [0, 0]
        free = {e: 0.0 for e in self.ENGS}
        out = []

        def start_time(si):
            i = ptr[si]
            op = streams[si][i]
            t = free[op.eng]
            for j in deps[si][i]:
                lat = HOP if streams[si][j].eng != op.eng or streams[si][j].dma else 0.15
                t = max(t, fin[si][j] + lat)
            return t
        while ptr[0] < len(streams[0]) or ptr[1] < len(streams[1]):
            cands = [si for si in (0, 1) if ptr[si] < len(streams[si])]
            best = min(cands, key=lambda si: (start_time(si), si))
            i = ptr[best]
            op = streams[best][i]
            t0 = start_time(best)
            cost = getattr(op, "cost", None) or self.COST[op.eng]
            if op.dma:
                free[op.eng] = t0 + 0.1
                fin[best][i] = t0 + 2.0
            else:
                free[op.eng] = t0 + cost
                fin[best][i] = t0 + cost
            out.append(op)
            ptr[best] += 1
        self.ops[i0:i2] = out

    def barrier(self, scratch):
        op = Op("dve", lambda e: e.memset(scratch, 0.0), (), (), False)
        op.barrier = True
        self.ops.append(op)

    def analyse(self):
        last_w = {}
        readers = {}
        ops = self.ops
        last_on = {}
        dma_hist = {e: [] for e in self.ENGS}
        pending = {}
        for i, op in enumerate(ops):
            if op.barrier:
                for e2, j in last_on.items():
                    if e2 != "dve" or True:
                        if j != i:
                            op.deps.append(j)
                            ops[j].inc = True
                for e2 in self.ENGS:
                    for j in dma_hist[e2][-NDMA_SEM:]:
                        op.deps.append(j)
                op.inc = True
                for e2 in self.ENGS:
                    pending[e2] = i
                pending.pop("dve")
                last_w.clear()
                readers.clear()
                last_on["dve"] = i
                continue
            if op.eng in pending:
                op.deps.append(pending.pop(op.eng))
            if op.dma:
                dma_hist[op.eng].append(i)
            else:
                last_on[op.eng] = i
            deps = {}
            for r in op.reads:
                j = last_w.get(r)
                if j is not None:
                    deps[j] = "raw"
            for w in op.writes:
                j = last_w.get(w)
                if j is not None and j not in deps:
                    deps[j] = "waw"
                for j in readers.get(w, ()):
                    if j not in deps:
                        deps[j] = "war"
            for j, kind in deps.items():
                if j == i:
                    continue
                a = ops[j]
                if a.dma:
                    op.deps.append(j)
                elif a.eng == op.eng:
                    if op.eng == "pe" and not op.dma:
                        continue
                    if op.dma or kind in ("raw", "waw"):
                        op.deps.append(j)
                        a.inc = True
                else:
                    op.deps.append(j)
                    a.inc = True
            for r in op.reads:
                readers.setdefault(r, []).append(i)
            for w in op.writes:
                last_w[w] = i
                readers[w] = []
        cnt = {e: 0 for e in self.ENGS}
        dcnt = {e: 0 for e in self.ENGS}
        self.dma_prev = {}
        hist = {e: [] for e in self.ENGS}
        for i, op in enumerate(ops):
            if op.dma:
                n = dcnt[op.eng]
                dcnt[op.eng] += 1
                op.sem_i = n % NDMA_SEM
                op.val = 16 * (n // NDMA_SEM + 1)
                hist[op.eng].append(i)
                if n >= NDMA_SEM:
                    op.deps.append(hist[op.eng][n - NDMA_SEM])
            elif op.inc:
                cnt[op.eng] += 1
                op.val = cnt[op.eng]

    def emit(self):
        self.analyse()
        nc = self.nc
        ops = self.ops
        import contextlib
        with contextlib.ExitStack() as st:
            esem = {e: st.enter_context(nc.semaphore("c_" + e)) for e in self.ENGS}
            dsem = {e: [st.enter_context(nc.semaphore("d_%s%d" % (e, k))) for k in range(NDMA_SEM)]
                    for e in ("sp", "pool", "act")}
            block = st.enter_context(nc.Block())

            def run(engname, eng):
                waited = {}
                for op in ops:
                    if op.eng != engname:
                        continue
                    for j in op.deps:
                        a = ops[j]
                        if a.dma:
                            sem, key = dsem[a.eng][a.sem_i], (a.eng, a.sem_i)
                        else:
                            sem, key = esem[a.eng], a.eng
                        if waited.get(key, 0) >= a.val:
                            continue
                        waited[key] = a.val
                        eng.wait_ge(sem, a.val)
                    if op.fn is None:
                        continue
                    inst = op.fn(eng)
                    if op.dma:
                        inst.then_inc(dsem[op.eng][op.sem_i], 16)
                    elif op.inc:
                        inst.then_inc(esem[op.eng], 1)

            @block.tensor
            def _(e):
                run("pe", e)

            @block.scalar
            def _(e):
                run("act", e)

            @block.vector
            def _(e):
                run("dve", e)

            @block.gpsimd
            def _(e):
                run("pool", e)

            @block.sync
            def _(e):
                run("sp", e)


class Arena:
    def __init__(self, t, nbytes):
        self.t = t
        self.nbytes = nbytes
        self.off = 0

    def alloc(self, shape_free, dtype, off=None):
        n = int(np.prod(shape_free))
        esz = 2 if dtype == BF16 else 4
        nb = (n * esz + 31) // 32 * 32
        if off is None:
            off = self.off
            self.off += nb
            assert self.off <= self.nbytes, ("SBUF arena overflow", self.off, self.nbytes)
        v = self.t[:, off // 4:(off + nb) // 4]
        if dtype == BF16:
            v = v.bitcast(BF16)
        v = v[:, 0:n]
        if len(shape_free) == 2:
            v = v.rearrange("p (a b) -> p a b", a=shape_free[0])
        elif len(shape_free) == 3:
            v = v.rearrange("p (a b c) -> p a b c", a=shape_free[0], b=shape_free[1])
        return v


def build_l0():
    nc = bass.Bass("TRN2", target_bir_lowering=False)
    c_d = nc.dram_tensor("c", [128, KC], F32, kind="ExternalInput").ap()
    wada = nc.dram_tensor("wada", [18, 128, KC, 128], F32, kind="ExternalInput").ap()
    bada_d = nc.dram_tensor("bada", [128, 18], F32, kind="ExternalInput").ap()
    out_d = nc.dram_tensor("ada", [128, 18], F32, kind="ExternalOutput").ap()
    import contextlib
    with contextlib.ExitStack() as st:
        NB = 64 * 1024
        arena_t = st.enter_context(nc.sbuf_tensor("arena", [128, NB // 4], F32))
        ps = st.enter_context(nc.psum_tensor("ps0", [128, 512], F32))
        A = Arena(arena_t, NB)
        P = Prog(nc)
        c_sb = A.alloc([KC], F32)
        c_bf = A.alloc([KC], BF16)
        ada = A.alloc([18], F32)
        bada = A.alloc([18], F32)
        wb = [A.alloc([KC, 128], BF16) for _ in range(4)]
        P.dma("sp", c_sb, c_d, writes=["c"])
        P.dma("sp", bada, bada_d, writes=["bada"])
        P.act(lambda e: e.activation(out=c_bf, in_=c_sb, func=AF.Silu), reads=["c"], writes=["cbf"])
        for t in range(18):
            slot = t % 4
            P.dma("pool", wb[slot], wada[t], writes=[("wb", slot)])

            def mm(e, t=t, slot=slot):
                inst = None
                for kc in range(KC):
                    inst = e.matmul(ps[:, t:t + 1], lhsT=wb[slot][:, kc, :], rhs=c_bf[:, kc:kc + 1],
                                    start=(kc == 0), stop=(kc == KC - 1))
                return inst
            P.pe(mm, reads=[("wb", slot), "cbf"], writes=["ps"])
        P.dve(lambda e: e.tensor_tensor(out=ada, in0=ps[:, 0:18], in1=bada, op=ALU.add),
              reads=["bada"], writes=["ps", "ada"])
        P.dma("sp", out_d, ada, reads=["ada"], writes=["out"])
        P.add("sp", None, reads=["out"])
        P.emit()
    return nc


def emit_ffn_in(P, A, PS, hT, actT, win_dram, wbuf, sg, tag):
    NT = TOK // 512
    n_in_tiles = FFC // 2
    for t in range(n_in_tiles):
        slot = t % 2
        wb = wbuf[slot]
        P.dma("pool", wb, win_dram[t], writes=[("wbuf", slot)])
        for b2 in range(2):
            ffb = 2 * t + b2
            for th in range(NT):
                i = (ffb * NT + th) % 2
                pg, pu = PS[i], PS[2 + i]
                tsl = slice(th * 512, (th + 1) * 512)

                def mm(e, wb=wb, b2=b2, tsl=tsl, pg=pg, pu=pu):
                    inst = None
                    for (pp, off) in ((pg, b2 * 256), (pu, b2 * 256 + 128)):
                        for kc in range(KC):
                            inst = e.matmul(pp[:, :], lhsT=wb[:, kc, off:off + 128], rhs=hT[:, kc, tsl],
                                            start=(kc == 0), stop=(kc == KC - 1))
                    return inst
                P.pe(mm, reads=[("wbuf", slot), tag + "hT"], writes=[("ps", i), ("ps", 2 + i)])
                P.act(lambda e, pg=pg, i=i: e.activation(out=sg[i], in_=pg[:, :], func=AF.Silu),
                      writes=[("ps", i), (tag + "sg", i)])
                P.dve(lambda e, pu=pu, i=i, ffb=ffb, tsl=tsl: e.tensor_tensor(
                    out=actT[:, ffb, tsl], in0=sg[i], in1=pu[:, :], op=ALU.mult),
                    reads=[(tag + "sg", i)], writes=[("ps", 2 + i), (tag + "actT", ffb, tsl.start)])


def emit_proj_ln(P, A, PS, rhsT, nK, rhs_keys, vT, vkeys_extra, wout_dram, wobuf, x_res, gate_vec, gate_keys, lng, lnb,
                 ones_mat, consume, tag, small_off, alias_keys):
    NT = TOK // 512
    o = [small_off]

    def al(shape):
        v = A.alloc(shape, F32, off=o[0])
        o[0] += int(np.prod(shape)) * 4
        return v
    xs = [al([512]) for _ in range(2)]
    tmp = [al([512]) for _ in range(2)]
    sq = [al([512]) for _ in range(2)]
    mean_sb = al([512])
    rstd = al([512])
    t1 = [al([512]) for _ in range(2)]
    small_keys = [(tag + k, i) for k in ("xs", "tmp", "sq", "t1") for i in range(2)] + [tag + "mean", tag + "rstd"]
    if alias_keys:
        P.dve(lambda e: e.memset(rstd[:, 0:1], 0.0), writes=list(alias_keys) + small_keys)
    n = 0
    for th in range(NT):
        tsl = slice(th * 512, (th + 1) * 512)
        for fb in range(KC):
            slot = n % 2
            n += 1
            wo = wobuf[slot]
            P.dma("pool", wo, wout_dram[fb], writes=[("wobuf", slot)])
            P.dma("sp", xs[slot], x_res[fb * 128:(fb + 1) * 128, tsl], reads=[(tag + "xres", fb, th)], writes=[(tag + "xs", slot)])
            py = PS[4 + slot]

            def mm(e, wo=wo, tsl=tsl, py=py):
                inst = None
                for kc in range(nK):
                    inst = e.matmul(py[:, :], lhsT=wo[:, kc, :], rhs=rhsT[:, kc, tsl],
                                    start=(kc == 0), stop=(kc == nK - 1))
                return inst
            P.pe(mm, reads=[("wobuf", slot)] + rhs_keys(th), writes=[("ps", 4 + slot)])
            P.act(lambda e, py=py, slot=slot, fb=fb: e.activation(out=tmp[slot], in_=py[:, :], func=AF.Identity,
                                                                 scale=gate_vec[:, fb:fb + 1]),
                  reads=gate_keys, writes=[("ps", 4 + slot), (tag + "tmp", slot)])
            P.dve(lambda e, fb=fb, slot=slot: e.scalar_tensor_tensor(
                out=vT[:, fb, :], in0=xs[slot], scalar=ALPHA, in1=tmp[slot], op0=ALU.mult, op1=ALU.add),
                reads=[(tag + "tmp", slot), (tag + "xs", slot)], writes=[(tag + "v", fb)] + vkeys_extra)
        pm, pq = PS[6], PS[7]
        for kc in range(KC):
            i = kc % 2
            P.act(lambda e, i=i, kc=kc: e.activation(out=sq[i], in_=vT[:, kc, :], func=AF.Square),
                  reads=[(tag + "v", kc)], writes=[(tag + "sq", i)])
            P.pe(lambda e, kc=kc, pm=pm: e.matmul(
                pm[:, :], lhsT=ones_mat, rhs=vT[:, kc, :], start=(kc == 0), stop=(kc == KC - 1)),
                reads=[(tag + "v", kc), "ones"], writes=[("ps", 6)])
            P.pe(lambda e, kc=kc, i=i, pq=pq: e.matmul(
                pq[:, :], lhsT=ones_mat, rhs=sq[i], start=(kc == 0), stop=(kc == KC - 1)),
                reads=[(tag + "sq", i), "ones"], writes=[("ps", 7)])
        P.act(lambda e, pm=pm: e.activation(out=mean_sb, in_=pm[:, :], func=AF.Identity),
              writes=[("ps", 6), tag + "mean"])
        P.dve(lambda e: e.tensor_tensor(out=rstd, in0=mean_sb, in1=mean_sb, op=ALU.mult),
              reads=[tag + "mean"], writes=[tag + "rstd"])
        P.dve(lambda e, pq=pq: e.tensor_tensor(out=rstd, in0=pq[:, :], in1=rstd, op=ALU.subtract),
              reads=[tag + "rstd"], writes=[("ps", 7), tag + "rstd"])
        P.dve(lambda e: e.tensor_scalar(out=rstd, in0=rstd, scalar1=LN_EPS, scalar2=None, op0=ALU.add),
              reads=[tag + "rstd"], writes=[tag + "rstd"])
        P.act(lambda e: e.activation(out=rstd, in_=rstd, func=AF.Sqrt), reads=[tag + "rstd"], writes=[tag + "rstd"])
        P.dve(lambda e: e.reciprocal(out=rstd, in_=rstd), reads=[tag + "rstd"], writes=[tag + "rstd"])
        for kc in range(KC):
            i = kc % 2
            P.dve(lambda e, i=i, kc=kc: e.tensor_tensor(out=t1[i], in0=vT[:, kc, :], in1=mean_sb, op=ALU.subtract),
                  reads=[(tag + "v", kc), tag + "mean"], writes=[(tag + "t1", i)])
            P.dve(lambda e, i=i: e.tensor_tensor(out=t1[i], in0=t1[i], in1=rstd, op=ALU.mult),
                  reads=[(tag + "t1", i), tag + "rstd"], writes=[(tag + "t1", i)])
            P.act(lambda e, i=i, kc=kc: e.activation(out=vT[:, kc, :], in_=t1[i], func=AF.Identity,
                                                    scale=lng[:, kc:kc + 1], bias=lnb[:, kc:kc + 1]),
                  reads=[(tag + "t1", i), tag + "ln"], writes=[(tag + "v", kc)])
            consume(kc, th, vT[:, kc, :], (tag + "v", kc))


def build_l1():
    nc = bass.Bass("TRN2", target_bir_lowering=False)
    xT = nc.dram_tensor("xT", [D, TOK], F32, kind="ExternalInput").ap()
    ada_d = nc.dram_tensor("ada", [128, 9 * KC], F32, kind="ExternalInput").ap()
    win = nc.dram_tensor("win", [FFC // 2, 128, KC, 512], F32, kind="ExternalInput").ap()
    wout = nc.dram_tensor("wout", [KC, 128, FFC, 128], F32, kind="ExternalInput").ap()
    lngb = nc.dram_tensor("lngb", [128, 2 * KC], F32, kind="ExternalInput").ap()
    x1T = nc.dram_tensor("x1T", [D, TOK], F32, kind="ExternalOutput").ap()
    h1T = nc.dram_tensor("h1T", [D, TOK], BF16, kind="ExternalOutput").ap()
    import contextlib
    with contextlib.ExitStack() as st:
        NB = 188 * 1024
        arena_t = st.enter_context(nc.sbuf_tensor("arena", [128, NB // 4], F32))
        PS = [st.enter_context(nc.psum_tensor("ps%d" % i, [128, 512], F32)) for i in range(8)]
        A = Arena(arena_t, NB)
        P = Prog(nc)
        wbuf_off = A.off
        wbuf = [A.alloc([KC, 512], BF16) for _ in range(2)]
        wobuf = [A.alloc([FFC, 128], BF16) for _ in range(2)]
        hT_off = A.off
        hT = A.alloc([KC, TOK], BF16)
        vT = A.alloc([KC, 512], F32, off=hT_off)
        act_off = A.off
        actT = A.alloc([FFC, TOK], BF16)
        ones_mat = A.alloc([128], F32)
        ln_sb = A.alloc([2 * KC], F32)
        P.dve(lambda e: e.memset(ones_mat, 1.0 / D), writes=["ones"])
        P.dma("sp", ln_sb, lngb, writes=["0ln"])
        ada = A.alloc([9 * KC], F32)
        P.dma("sp", ada, ada_d, writes=["0ada"])
        der = A.alloc([3 * KC], F32)
        P.dve(lambda e: e.tensor_scalar(out=der[:, 0:KC], in0=ada[:, KC:2 * KC], scalar1=1.0, scalar2=None, op0=ALU.add),
              reads=["0ada"], writes=["0der"])
        P.dve(lambda e: e.tensor_scalar(out=der[:, KC:2 * KC], in0=ada[:, 2 * KC:3 * KC], scalar1=0.5, scalar2=None,
                                        op0=ALU.mult), reads=["0ada"], writes=["0der"])
        P.dve(lambda e: e.tensor_scalar(out=der[:, 2 * KC:3 * KC], in0=ada[:, 4 * KC:5 * KC], scalar1=1.0, scalar2=None,
                                        op0=ALU.add), reads=["0ada"], writes=["0der"])
        xst = A.alloc([KC, TOK], F32, off=act_off)
        xT_v = xT.rearrange("(kc p) t -> p kc t", p=128)
        for g in range(4):
            P.dma("sp", xst[:, 4 * g:4 * g + 4, :], xT_v[:, 4 * g:4 * g + 4, :], writes=[("xst", g)])
        for kc in range(KC):
            P.dve(lambda e, kc=kc: e.tensor_scalar(
                out=hT[:, kc, :], in0=xst[:, kc, :], scalar1=der[:, kc:kc + 1],
                scalar2=ada[:, kc:kc + 1], op0=ALU.mult, op1=ALU.add),
                reads=[("xst", kc // 4), "0der", "0ada"], writes=["0hT"])
        hout = [A.alloc([512], BF16) for _ in range(2)]
        cnt = [0]

        def consume(kc, th, xn, key):
            tsl = slice(th * 512, (th + 1) * 512)
            i = cnt[0] % 2
            cnt[0] += 1
            P.dma("sp", x1T[kc * 128:(kc + 1) * 128, tsl], xn, reads=[key], writes=[("out", "x1", kc, th)])
            P.dve(lambda e, i=i, kc=kc, xn=xn: e.tensor_scalar(
                out=hout[i], in0=xn, scalar1=der[:, 2 * KC + kc:2 * KC + kc + 1],
                scalar2=ada[:, 3 * KC + kc:3 * KC + kc + 1], op0=ALU.mult, op1=ALU.add),
                reads=[key, "0der", "0ada"], writes=[("hout", i)])
            P.dma("sp", h1T[kc * 128:(kc + 1) * 128, tsl], hout[i], reads=[("hout", i)], writes=[("out", "h1", kc, th)])

        sg = [A.alloc([512], F32) for _ in range(2)]
        emit_ffn_in(P, A, PS, hT, actT, win, wbuf, sg, "0")
        act_keys = lambda th: [("0actT", ffb, th * 512) for ffb in range(FFC)]
        emit_proj_ln(P, A, PS, actT, FFC, act_keys, vT, ["0hT"], wout, wobuf, xT, der[:, KC:2 * KC], ["0ada", "0der"],
                     ln_sb[:, 0:KC], ln_sb[:, KC:2 * KC], ones_mat, consume, "0", wbuf_off, [("wbuf", 0), ("wbuf", 1)])
        outs = [("out", n, kc, th) for n in ("x1", "h1") for kc in range(KC) for th in range(TOK // 512)]
        P.add("sp", None, reads=outs)
        P.emit()
    return nc


def fm(v):
    return np.ascontiguousarray(v.reshape(-1, 128).T)


def tile_w(w, fw):
    K, F = w.shape
    return np.ascontiguousarray(w.reshape(K // 128, 128, F // fw, fw).transpose(2, 1, 0, 3))


def ffn_in_tiles(w):
    g = w[:, :DFF].reshape(KC, 128, FFC, 128)
    u = w[:, DFF:].reshape(KC, 128, FFC, 128)
    gu = np.stack([g, u], axis=3)
    gu = gu.reshape(KC, 128, FFC // 2, 512)
    return np.ascontiguousarray(gu.transpose(2, 1, 0, 3))


def ada_tiles(w_ada, b_ada, vec_ids):
    cols = np.concatenate([np.arange(v * D, (v + 1) * D) for v in vec_ids])
    w = w_ada[:, cols]
    return tile_w(w, 512), fm(b_ada[cols])


NBLK = S // 512
WQC = 898


def build_l2(nblk=NBLK):
    nc = bass.Bass("TRN2", target_bir_lowering=False)
    hT_d = nc.dram_tensor("hT", [D, S], BF16, kind="ExternalInput").ap()
    wq_d = nc.dram_tensor("wq", [128, KC, WQC], F32, kind="ExternalInput").ap()
    cw_d = nc.dram_tensor("convw", [128, 12], F32, kind="ExternalInput").ap()
    sc_d = nc.dram_tensor("scal", [128, 8], F32, kind="ExternalInput").ap()
    lam_d = nc.dram_tensor("lamp", [128, 256], F32, kind="ExternalInput").ap()
    bias_d = nc.dram_tensor("biasT", [128, 256], F32, kind="ExternalInput").ap()
    cst_d = nc.dram_tensor("consts", [128, 384], F32, kind="ExternalInput").ap()
    attn_d = nc.dram_tensor("attnT", [128, S], BF16, kind="ExternalOutput").ap()
    dn_d = nc.dram_tensor("dnT", [128, S], BF16, kind="ExternalOutput").ap()
    hT_v = hT_d.rearrange("(kc p) t -> p kc t", p=128)
    import contextlib
    with contextlib.ExitStack() as st:
        NB = 188 * 1024
        arena_t = st.enter_context(nc.sbuf_tensor("arena", [128, NB // 4], F32))
        PS = [st.enter_context(nc.psum_tensor("ps%d" % i, [128, 512], F32)) for i in range(8)]
        A = Arena(arena_t, NB)
        P = Prog(nc)
        al = A.alloc
        wq = al([KC, WQC], BF16)
        cw = al([12], F32)
        sc = al([8], F32)
        lamp = al([256], F32)
        biasT = al([256], F32)
        cst = al([384], F32)
        ident = cst[:, 0:128]
        U = cst[:, 128:256]
        SL = cst[:, 256:384]
        identb = al([128], BF16)
        onesF = al([128], F32)
        negones = al([128], F32)
        ones128 = al([128], F32)
        onesb = al([128], BF16)
        one_col = al([1], F32)
        P.dma("pool", wq, wq_d, writes=["wq"])
        P.dma("sp", cw, cw_d, writes=["par"])
        P.dma("sp", sc, sc_d, writes=["par"])
        P.dma("sp", lamp, lam_d, writes=["par"])
        P.dma("sp", biasT, bias_d, writes=["par"])
        P.dma("sp", cst, cst_d, writes=["par"])
        P.dve(lambda e: e.memset(onesF, 1.0), writes=["cst2"])
        P.dve(lambda e: e.memset(negones, -1.0), writes=["cst2"])
        P.dve(lambda e: e.memset(ones128, 1.0 / 128), writes=["cst2"])
        P.dve(lambda e: e.memset(onesb, 1.0), writes=["cst2"])
        P.dve(lambda e: e.memset(one_col, 1.0), writes=["cst2"])
        zerob = al([128], BF16)
        zero512 = al([512], BF16)
        P.dve(lambda e: e.memset(zerob, 0.0), writes=["cst2"])
        P.dve(lambda e: e.memset(zero512, 0.0), writes=["cst2"])
        der = al([8], F32)
        P.act(lambda e: e.activation(out=der[:, 0:1], in_=sc[:, 0:1], func=AF.Exp), reads=["par"], writes=["der"])
        P.dve(lambda e: e.tensor_scalar(out=der[:, 0:1], in0=der[:, 0:1], scalar1=-1.0, scalar2=None, op0=ALU.mult),
              reads=["der"], writes=["der"])
        P.dve(lambda e: e.tensor_scalar(out=der[:, 1:2], in0=sc[:, 4:5], scalar1=0.8, scalar2=None, op0=ALU.mult),
              reads=["par"], writes=["der"])
        lt = al([128], F32)
        P.dve(lambda e: e.tensor_tensor(out=lt[:, 0:64], in0=lamp[:, 0:64], in1=lamp[:, 64:128], op=ALU.mult),
              reads=["par"], writes=["lt"])
        P.dve(lambda e: e.tensor_tensor(out=lt[:, 64:128], in0=lamp[:, 128:192], in1=lamp[:, 192:256], op=ALU.mult),
              reads=["par", "lt"], writes=["lt"])
        P.dve(lambda e: e.reduce_sum(out=der[:, 2:3], in_=lt[:, 0:64], axis=AX.X), reads=["lt", "der"], writes=["der"])
        P.dve(lambda e: e.reduce_sum(out=der[:, 3:4], in_=lt[:, 64:128], axis=AX.X), reads=["lt", "der"], writes=["der"])
        P.act(lambda e: e.activation(out=der[:, 2:4], in_=der[:, 2:4], func=AF.Exp), reads=["der"], writes=["der"])
        P.dve(lambda e: e.tensor_tensor(out=der[:, 4:5], in0=der[:, 3:4], in1=der[:, 2:3], op=ALU.subtract),
              reads=["der"], writes=["der"])
        P.dve(lambda e: e.tensor_scalar(out=der[:, 4:5], in0=der[:, 4:5], scalar1=-0.2, scalar2=None, op0=ALU.add),
              reads=["der"], writes=["der"])
        nea, subw, neglam = der[:, 0:1], der[:, 1:2], der[:, 4:5]
        dtb, b31, normw = sc[:, 1:2], sc[:, 2:3], sc[:, 3:4]
        qT = al([S], BF16)
        kT = al([S], BF16)
        Vtok = al([S // 128, 128], BF16)
        hblk = [al([KC, 512], BF16)]
        raw = [al([515], F32) for _ in range(3)]
        cs = [al([512], F32) for _ in range(3)]
        cs0 = [cs[0], al([512], F32)]
        zs2 = [al([512], F32) for _ in range(2)]
        ba2 = [al([512], F32) for _ in range(2)]
        cs1 = [cs[1], al([512], F32)]
        cs2 = [cs[2], al([512], F32)]
        tA = al([512], F32)
        tB = al([512], F32)
        vtmp = al([512], F32)
        Sst = al([128], F32)
        NCH = 4
        ch = []
        for c in range(NCH):
            d = {}
            for nm, w in (("kn", 128), ("v", 128), ("kbg", 128), ("kdec", 128), ("u", 128), ("o", 128),
                          ("wT", 128), ("sm", 16), ("gU", 128), ("dm", 256), ("X", 128), ("Y", 128),
                          ("X2", 128), ("Y2", 128), ("PT", 128), ("PT2", 128), ("qkT", 128)):
                d[nm] = al([w], F32)
            ch.append(d)
        seqt = [{nm: al([128], F32) for nm in ("vnew", "o1s")} for _ in range(2)]
        Pm = [[al([512], BF16) for _ in range(2)] for _ in range(2)]
        tb_ = [al([256], F32) for _ in range(2)]
        at0 = al([512], F32)
        at1 = al([512], F32)
        at2 = al([512], F32)
        ost = al([512], BF16)
        dst = al([512], BF16)
        for w in range(3):
            P.dve(lambda e, w=w: e.memset(raw[w][:, 0:3], 0.0), writes=[("raw", w)])
        P.dve(lambda e: e.memset(Sst, 0.0), writes=["S"])

        def K(c, nm):
            return ("ch", c, nm)

        def emit_ip(tb):
            tsl = slice(tb * 512, (tb + 1) * 512)
            csb = [cs0[tb % 2], cs1[tb % 2], cs2[tb % 2]]
            ba = ba2[tb % 2]
            zs = zs2[tb % 2]
            hb = hblk[0]
            hk = ("hblk", 0)
            P.dma("sp", hb, hT_v[:, :, tsl], writes=[hk])
            for ob in range(8):
                pa = PS[ob % 2]
                pk = ("ps", ob % 2)
                M = 128 if ob < 7 else 2

                def mm(e, ob=ob, pa=pa, M=M, hb=hb):
                    inst = None
                    for kc in range(KC):
                        inst = e.matmul(pa[0:M, :], lhsT=wq[:, kc, ob * 128:ob * 128 + M], rhs=hb[:, kc, :],
                                        start=(kc == 0), stop=(kc == KC - 1))
                    return inst
                P.pe(mm, reads=["wq", hk], writes=[pk], cost=3.6 if ob < 7 else 1.2)
                if ob == 0:
                    P.act(lambda e, pa=pa, tsl=tsl: e.activation(out=qT[:, tsl], in_=pa[:, :], func=AF.Identity),
                          writes=[pk, ("qT", tb)])
                elif ob == 1:
                    P.act(lambda e, pa=pa, tsl=tsl: e.activation(out=kT[:, tsl], in_=pa[:, :], func=AF.Identity),
                          writes=[pk, ("kT", tb)])
                elif ob == 2:
                    P.act(lambda e, pa=pa: e.activation(out=vtmp, in_=pa[:, :], func=AF.Identity), writes=[pk, "vtmp"])

                    def tr(e):
                        inst = None
                        for j in range(4):
                            inst = e.transpose(out=PS[2][:, j * 128:(j + 1) * 128], in_=vtmp[:, j * 128:(j + 1) * 128],
                                               identity=ident)
                        return inst
                    P.pe(tr, reads=["vtmp", "par"], writes=[("ps", 2)])
                    P.dve(lambda e, tb=tb: e.tensor_copy(
                        out=Vtok[:, tb * 4:(tb + 1) * 4, :].rearrange("p a b -> p (a b)"), in_=PS[2][:, 0:512]),
                        writes=[("ps", 2), ("V", tb)])
                elif ob in (3, 4, 5):
                    w = ob - 3
                    P.act(lambda e, pa=pa, w=w: e.activation(out=raw[w][:, 3:515], in_=pa[:, :], func=AF.Identity),
                          writes=[pk, ("raw", w)])
                    P.dve(lambda e, w=w: e.tensor_scalar(out=tA, in0=raw[w][:, 0:512], scalar1=cw[:, w * 4:w * 4 + 1],
                                                         scalar2=None, op0=ALU.mult),
                          reads=[("raw", w), "par"], writes=["tA"])
                    for i in range(1, 4):
                        P.dve(lambda e, w=w, i=i: e.scalar_tensor_tensor(
                            out=tA, in0=raw[w][:, i:i + 512], scalar=cw[:, w * 4 + i:w * 4 + i + 1], in1=tA,
                            op0=ALU.mult, op1=ALU.add), reads=[("raw", w), "par", "tA"], writes=["tA"])
                    P.dve(lambda e, w=w: e.tensor_copy(out=raw[w][:, 0:3], in_=raw[w][:, 512:515]),
                          reads=[("raw", w)], writes=[("raw", w)])
                    P.act(lambda e, w=w: e.activation(out=csb[w], in_=tA, func=AF.Silu), reads=["tA"], writes=[("cs", w, tb % 2)])
                    if w < 2:
                        P.act(lambda e, w=w: e.activation(out=tB, in_=csb[w], func=AF.Square), reads=[("cs", w, tb % 2)],
                              writes=["tB"])
                        P.pe(lambda e: e.matmul(PS[2][:, :], lhsT=onesF, rhs=tB, start=True, stop=True),
                             reads=["tB", "cst2"], writes=[("ps", 2)])
                        P.dve(lambda e: e.tensor_scalar(out=tB, in0=PS[2][:, :], scalar1=RMS_EPS, scalar2=None,
                                                        op0=ALU.add), writes=[("ps", 2), "tB"])
                        P.act(lambda e: e.activation(out=tB, in_=tB, func=AF.Sqrt), reads=["tB"], writes=["tB"])
                        P.dve(lambda e: e.reciprocal(out=tB, in_=tB), reads=["tB"], writes=["tB"])
                        sc_ = (128.0 ** -0.5) if w == 0 else 1.0
                        P.dve(lambda e, w=w, sc_=sc_: e.scalar_tensor_tensor(
                            out=csb[w], in0=csb[w], scalar=sc_, in1=tB, op0=ALU.mult, op1=ALU.mult),
                            reads=[("cs", w, tb % 2), "tB"], writes=[("cs", w, tb % 2)])
                elif ob == 6:
                    P.act(lambda e, pa=pa: e.activation(out=zs, in_=pa[:, :], func=AF.Silu), writes=[pk, ("zs", tb % 2)])
                else:
                    P.act(lambda e, pa=pa: e.activation(out=ba[0:2, :], in_=pa[0:2, :], func=AF.Identity),
                          writes=[pk, ("ba", tb % 2)])

        emit_ip(0)
        for tb in range(nblk):
            tsl = slice(tb * 512, (tb + 1) * 512)
            zs = zs2[tb % 2]
            zk = ("zs", tb % 2)
            qk_ = ("cs", 0, tb % 2)
            qn, kn_f, cv, ba = cs0[tb % 2], cs1[tb % 2], cs2[tb % 2], ba2[tb % 2]
            k1_, k2_, kb_ = ("cs", 1, tb % 2), ("cs", 2, tb % 2), ("ba", tb % 2)
            iP0 = len(P.ops)
            cl = [(c, c) for c in range(NCH)]
            NIT = 6

            def each(fn):
                for c, cb in cl:
                    fn(c, ch[c], slice(cb * 128, (cb + 1) * 128), PS[3 + c], ("ps", 3 + c),
                       PS[7][:, 16 * c:16 * c + 16], ("ps", 7))

            def s1(c, d, csl, ps, pk, ps2, pk2, qn=qn, kn_f=kn_f, cv=cv, ba=ba):
                def f(e):
                    e.transpose(out=ps[:, 0:128], in_=kn_f[:, csl], identity=ident)
                    e.transpose(out=ps[:, 128:256], in_=cv[:, csl], identity=ident)
                    e.matmul(ps[:, 256:384], lhsT=kn_f[:, csl], rhs=kn_f[:, csl], start=True, stop=True)
                    e.matmul(ps[:, 384:512], lhsT=kn_f[:, csl], rhs=qn[:, csl], start=True, stop=True)
                    return e.transpose(out=ps2[:, 4:6], in_=ba[0:2, csl], identity=ident[0:2, 0:2])
                P.pe(f, reads=[k1_, k2_, qk_, kb_, "par"], writes=[pk, pk2])
            each(s1)

            def s2(c, d, csl, ps, pk, ps2, pk2):
                sm = d["sm"]
                P.act(lambda e: e.activation(out=d["kn"], in_=ps[:, 0:128], func=AF.Identity), writes=[pk, K(c, "kn")])
                P.dve(lambda e: e.tensor_copy(out=d["v"], in_=ps[:, 128:256]), writes=[pk, K(c, "v")])
                P.act(lambda e: e.activation(out=sm[:, 0:1], in_=ps2[:, 4:5], func=AF.Sigmoid), writes=[pk2, K(c, "sm")])
                P.act(lambda e: e.activation(out=sm[:, 2:3], in_=ps2[:, 5:6], func=AF.Exp, bias=dtb),
                      reads=["par"], writes=[pk2, K(c, "sm")])
                P.act(lambda e: e.activation(out=sm[:, 2:3], in_=sm[:, 2:3], func=AF.Ln, bias=one_col),
                      reads=[K(c, "sm"), "cst2"], writes=[K(c, "sm")])
                P.dve(lambda e: e.tensor_tensor(out=sm[:, 3:4], in0=sm[:, 2:3], in1=nea, op=ALU.mult),
                      reads=[K(c, "sm"), "der"], writes=[K(c, "sm")])
                P.dve(lambda e: e.tensor_scalar(out=sm[:, 1:2], in0=sm[:, 0:1], scalar1=-1.0, scalar2=None, op0=ALU.mult),
                      reads=[K(c, "sm")], writes=[K(c, "sm")])
                P.dve(lambda e: e.tensor_scalar(out=d["gU"], in0=U, scalar1=sm[:, 3:4], scalar2=None, op0=ALU.mult),
                      reads=[K(c, "sm"), "par"], writes=[K(c, "gU")])
            each(s2)

            def s4(c, d, csl, ps, pk, ps2, pk2):
                g = d["sm"][:, 3:4]
                gU = d["gU"]

                def f(e):
                    e.matmul(ps2[:, 0:1], lhsT=U, rhs=g, start=True, stop=True)
                    e.matmul(ps2[:, 1:2], lhsT=SL, rhs=g, start=True, stop=True)
                    e.matmul(ps2[:, 2:3], lhsT=onesF, rhs=g, start=True, stop=True)
                    e.matmul(ps[:, 0:128], lhsT=gU, rhs=onesF, start=True, stop=False)
                    e.matmul(ps[:, 0:128], lhsT=negones, rhs=gU, start=False, stop=True)
                    e.matmul(ps[:, 128:256], lhsT=gU, rhs=negones, start=True, stop=False)
                    return e.matmul(ps[:, 128:256], lhsT=onesF, rhs=gU, start=False, stop=True)
                P.pe(f, reads=[K(c, "sm"), K(c, "gU"), "par", "cst2"], writes=[pk, pk2])
            each(s4)

            def s5(c, d, csl, ps, pk, ps2, pk2):
                sm = d["sm"]
                P.act(lambda e: e.activation(out=sm[:, 4:6], in_=ps2[:, 0:2], func=AF.Exp), writes=[pk2, K(c, "sm")])
                P.act(lambda e: e.activation(out=sm[:, 7:8], in_=ps2[:, 2:3], func=AF.Exp), writes=[pk2, K(c, "sm")])
                P.dve(lambda e: e.tensor_scalar(out=d["dm"], in0=ps[:, 0:256], scalar1=0.0, scalar2=None, op0=ALU.min),
                      writes=[pk, K(c, "dm")])
                P.act(lambda e: e.activation(out=d["dm"], in_=d["dm"], func=AF.Exp), reads=[K(c, "dm")], writes=[K(c, "dm")])
                P.dve(lambda e: e.tensor_tensor(out=d["dm"][:, 0:128], in0=d["dm"][:, 0:128], in1=SL, op=ALU.mult),
                      reads=[K(c, "dm"), "par"], writes=[K(c, "dm")])
                P.dve(lambda e: e.tensor_tensor(out=d["dm"][:, 128:256], in0=d["dm"][:, 128:256], in1=U, op=ALU.mult),
                      reads=[K(c, "dm"), "par"], writes=[K(c, "dm")])
                P.dve(lambda e: e.tensor_tensor(out=sm[:, 6:7], in0=sm[:, 0:1], in1=sm[:, 4:5], op=ALU.mult),
                      reads=[K(c, "sm")], writes=[K(c, "sm")])
                P.act(lambda e: e.activation(out=d["kbg"], in_=d["kn"], func=AF.Identity, scale=sm[:, 6:7]),
                      reads=[K(c, "sm"), K(c, "kn")], writes=[K(c, "kbg")])
                P.act(lambda e: e.activation(out=d["kdec"], in_=d["kn"], func=AF.Identity, scale=sm[:, 5:6]),
                      reads=[K(c, "sm"), K(c, "kn")], writes=[K(c, "kdec")])
                P.act(lambda e: e.activation(out=d["v"], in_=d["v"], func=AF.Identity, scale=sm[:, 0:1]),
                      reads=[K(c, "sm"), K(c, "v")], writes=[K(c, "v")])
            each(s5)

            def s6(c, d, csl, ps, pk, ps2, pk2):
                P.dve(lambda e: e.scalar_tensor_tensor(out=d["X"], in0=ps[:, 256:384], scalar=d["sm"][:, 1:2],
                                                       in1=d["dm"][:, 0:128], op0=ALU.mult, op1=ALU.mult),
                      reads=[K(c, "sm"), K(c, "dm")], writes=[pk, K(c, "X")])
                P.dve(lambda e: e.tensor_tensor(out=d["qkT"], in0=ps[:, 384:512], in1=d["dm"][:, 128:256], op=ALU.mult),
                      reads=[K(c, "dm")], writes=[pk, K(c, "qkT")])
            each(s6)

            def s8(c, d, csl, ps, pk, ps2, pk2):
                P.pe(lambda e: e.transpose(out=ps[:, 0:128], in_=d["X"], identity=ident), reads=[K(c, "X"), "par"],
                     writes=[pk])
                P.act(lambda e: e.activation(out=d["Y"], in_=ps[:, 0:128], func=AF.Identity), writes=[pk, K(c, "Y")])
                P.dve(lambda e: e.tensor_tensor(out=d["PT"], in0=ps[:, 0:128], in1=ident, op=ALU.add),
                      reads=["par"], writes=[pk, K(c, "PT")])
            each(s8)
            XB, YB, PB = ("X", "X2"), ("Y", "Y2"), ("PT", "PT2")

            def sa_pe(c, d, ps, pk, k):
                X, Y = d[XB[(k - 1) % 2]], d[YB[(k - 1) % 2]]

                def f(e):
                    inst = e.matmul(ps[:, 0:128], lhsT=Y, rhs=X, start=True, stop=True)
                    if k < NIT:
                        inst = e.matmul(ps[:, 128:256], lhsT=X, rhs=Y, start=True, stop=True)
                    return inst
                P.pe(f, reads=[K(c, XB[(k - 1) % 2]), K(c, YB[(k - 1) % 2])], writes=[pk])

            def sa_ev(c, d, ps, pk, k):
                P.act(lambda e: e.activation(out=d[XB[k % 2]], in_=ps[:, 0:128], func=AF.Identity),
                      writes=[pk, K(c, XB[k % 2])])
                if k < NIT:
                    P.dve(lambda e: e.tensor_copy(out=d[YB[k % 2]], in_=ps[:, 128:256]), writes=[pk, K(c, YB[k % 2])])

            def sb_pe(c, d, ps, pk, k):
                P.pe(lambda e: e.matmul(ps[:, 256:384], lhsT=d[XB[k % 2]], rhs=d[PB[(k - 1) % 2]], start=True, stop=True),
                     reads=[K(c, XB[k % 2]), K(c, PB[(k - 1) % 2])], writes=[pk])

            def sb_ev(c, d, ps, pk, k):
                P.dve(lambda e: e.tensor_tensor(out=d[PB[k % 2]], in0=ps[:, 256:384], in1=d[PB[(k - 1) % 2]], op=ALU.add),
                      reads=[K(c, PB[(k - 1) % 2])], writes=[pk, K(c, PB[k % 2])])
            each(lambda c, d, csl, ps, pk, ps2, pk2: sa_pe(c, d, ps, pk, 1))
            each(lambda c, d, csl, ps, pk, ps2, pk2: sa_ev(c, d, ps, pk, 1))
            for k in range(1, NIT + 1):
                def ph_pe(c, d, csl, ps, pk, ps2, pk2, k=k):
                    sb_pe(c, d, ps, pk, k)
                    if k < NIT:
                        sa_pe(c, d, ps, pk, k + 1)
                each(ph_pe)

                def ph_ev(c, d, csl, ps, pk, ps2, pk2, k=k):
                    sb_ev(c, d, ps, pk, k)
                    if k < NIT:
                        sa_ev(c, d, ps, pk, k + 1)
                each(ph_ev)
            PTF = PB[NIT % 2]

            def sf(c, d, csl, ps, pk, ps2, pk2):
                PT = d[PTF]

                def f(e):
                    e.matmul(ps[:, 0:128], lhsT=PT, rhs=d["v"], start=True, stop=True)
                    return e.matmul(ps[:, 128:256], lhsT=d["kbg"], rhs=PT, start=True, stop=True)
                P.pe(f, reads=[K(c, PTF), K(c, "v"), K(c, "kbg")], writes=[pk])
                P.act(lambda e: e.activation(out=d["u"], in_=ps[:, 0:128], func=AF.Identity), writes=[pk, K(c, "u")])
                P.dve(lambda e: e.tensor_copy(out=d["wT"], in_=ps[:, 128:256]), writes=[pk, K(c, "wT")])
            each(sf)
            iP1 = len(P.ops)
            if tb + 1 < nblk:
                emit_ip(tb + 1)
                P.interleave(iP0, iP1, len(P.ops))
            iA = len(P.ops)
            pa, pb, pt_ = PS[5][:, 0:256], PS[5][:, 256:512], PS[6]
            for c, cb in cl:
                d = ch[c]
                csl = slice(cb * 128, (cb + 1) * 128)
                sm = d["sm"]
                sq_ = seqt[c % 2]

                def f1(e, d=d, csl=csl, qn=qn):
                    e.matmul(pa[:, 0:128], lhsT=d["wT"], rhs=Sst, start=True, stop=True)
                    return e.matmul(pa[:, 128:256], lhsT=qn[:, csl], rhs=Sst, start=True, stop=True)
                P.pe(f1, reads=[K(c, "wT"), "S", qk_], writes=[("ps", 5)])
                P.dve(lambda e, d=d, sq_=sq_: e.tensor_tensor(out=sq_["vnew"], in0=d["u"], in1=pa[:, 0:128], op=ALU.subtract),
                      reads=[K(c, "u")], writes=[("ps", 5), ("vnew", c % 2)])
                P.dve(lambda e, sm=sm, sq_=sq_: e.tensor_scalar(out=sq_["o1s"], in0=pa[:, 128:256], scalar1=sm[:, 4:5],
                                                                scalar2=None, op0=ALU.mult),
                      reads=[K(c, "sm")], writes=[("ps", 5), ("o1s", c % 2)])

                def f2(e, d=d, sq_=sq_):
                    e.matmul(pb[:, 0:128], lhsT=d["qkT"], rhs=sq_["vnew"], start=True, stop=True)
                    return e.matmul(pb[:, 128:256], lhsT=d["kdec"], rhs=sq_["vnew"], start=True, stop=True)
                P.pe(f2, reads=[K(c, "qkT"), ("vnew", c % 2), K(c, "kdec")], writes=[("ps", 5)])
                P.dve(lambda e, sm=sm: e.scalar_tensor_tensor(out=Sst, in0=Sst, scalar=sm[:, 7:8], in1=pb[:, 128:256],
                                                              op0=ALU.mult, op1=ALU.add),
                      reads=[K(c, "sm"), "S"], writes=[("ps", 5), "S"])
                P.dve(lambda e, d=d, sq_=sq_: e.tensor_tensor(out=d["o"], in0=sq_["o1s"], in1=pb[:, 0:128], op=ALU.add),
                      reads=[("o1s", c % 2)], writes=[("ps", 5), K(c, "o")])
            def each1(fn):
                for c, cb in cl:
                    fn(c, ch[c])

            def e1(c, d):
                sm = d["sm"]
                P.dve(lambda e: e.memset(sm[:, 8:9], 0.0), reads=[K(c, "sm")], writes=[K(c, "sm")])
                P.act(lambda e: e.activation(out=d["kn"], in_=d["o"], func=AF.Square, accum_out=sm[:, 8:9]),
                      reads=[K(c, "o")], writes=[K(c, "kn"), K(c, "sm")])
            each1(e1)

            def e2(c, d):
                sm = d["sm"]
                P.dve(lambda e: e.tensor_scalar(out=sm[:, 8:9], in0=sm[:, 8:9], scalar1=1.0 / 128, scalar2=RMS_EPS,
                                                op0=ALU.mult, op1=ALU.add), reads=[K(c, "sm")], writes=[K(c, "sm")])
            each1(e2)

            def e3(c, d):
                sm = d["sm"]
                P.act(lambda e: e.activation(out=sm[:, 8:9], in_=sm[:, 8:9], func=AF.Sqrt), reads=[K(c, "sm")],
                      writes=[K(c, "sm")])
            each1(e3)

            def e4(c, d):
                sm = d["sm"]
                P.dve(lambda e: e.reciprocal(out=sm[:, 8:9], in_=sm[:, 8:9]), reads=[K(c, "sm")], writes=[K(c, "sm")])
            each1(e4)

            def e5(c, d):
                sm = d["sm"]
                P.act(lambda e: e.activation(out=d["o"], in_=d["o"], func=AF.Identity, scale=sm[:, 8:9]),
                      reads=[K(c, "sm"), K(c, "o")], writes=[K(c, "o")])
            each1(e5)

            def e6(c, d):
                P.pe(lambda e: e.transpose(out=pt_[:, c * 128:(c + 1) * 128], in_=d["o"], identity=ident),
                     reads=[K(c, "o"), "par"], writes=[("ps", 6)])
            each1(e6)
            P.dve(lambda e, zs=zs: e.scalar_tensor_tensor(out=dst, in0=pt_[:, :], scalar=normw, in1=zs, op0=ALU.mult,
                                                          op1=ALU.mult), reads=[zk, "par"], writes=[("ps", 6), "dst"])
            P.dma("sp", dn_d[:, tsl], dst, reads=["dst"], writes=[("out", "dn", tb)])
            iB = len(P.ops)
            nj = 4 * tb + 4
            p_s = (PS[0], PS[1])
            p_o = (PS[2], PS[3])
            k_s = (("ps", 0), ("ps", 1))
            k_o = (("ps", 2), ("ps", 3))
            PL = (PS[4], PS[7])
            KL = (("ps", 4), ("ps", 7))
            def att_front(j):
                a = j - 4 * tb
                q0 = max(a, 0) * 128
                jsl = slice(j * 128, (j + 1) * 128)
                qsl = slice(tb * 512 + q0, tb * 512 + 512)
                for m in range(2):
                    rows = slice(64 * m, 64 * m + 64)
                    pm_ = Pm[m][j % 2]
                    pmk = ("Pm", m, j % 2)
                    P.pe(lambda e, m=m, rows=rows, jsl=jsl, qsl=qsl, q0=q0: e.matmul(
                        p_s[m][:, q0:512], lhsT=kT[rows, jsl], rhs=qT[rows, qsl], start=True, stop=True),
                        reads=[("kT", j // 4), ("qT", tb)], writes=[k_s[m]])
                    band_lo = q0 if a >= -1 else 512
                    band_hi = min(512, (a + 2) * 128) if a >= -1 else 512
                    if a >= -1:
                        bc0 = 0 if a >= 0 else 128
                        wdt = band_hi - band_lo
                        P.dve(lambda e, m=m, band_lo=band_lo, band_hi=band_hi, bc0=bc0, wdt=wdt: e.scalar_tensor_tensor(
                            out=tb_[m][:, 0:wdt], in0=p_s[m][:, band_lo:band_hi], scalar=0.125,
                            in1=biasT[:, bc0:bc0 + wdt], op0=ALU.mult, op1=ALU.add),
                            reads=["par"], writes=[k_s[m], ("tb", m)])
                        P.act(lambda e, m=m, band_lo=band_lo, band_hi=band_hi, wdt=wdt, pm_=pm_: e.activation(
                            out=pm_[:, band_lo:band_hi], in_=tb_[m][:, 0:wdt], func=AF.Exp),
                            reads=[("tb", m)], writes=[pmk])
                    if band_hi < 512 or a < -1:
                        lo = band_hi if a >= -1 else 0
                        P.act(lambda e, m=m, lo=lo, pm_=pm_: e.activation(out=pm_[:, lo:512], in_=p_s[m][:, lo:512],
                                                                        func=AF.Exp, scale=0.125, bias=b31),
                              reads=["par"], writes=[k_s[m], pmk])

            def att_back(j):
                a = j - 4 * tb
                q0 = max(a, 0) * 128
                for m in range(2):
                    pm_ = Pm[m][j % 2]
                    pmk = ("Pm", m, j % 2)
                    P.pe(lambda e, m=m, j=j, q0=q0, pm_=pm_: e.matmul(
                        p_o[m][:, q0:512], lhsT=Vtok[:, j, :], rhs=pm_[:, q0:512], start=(j == 0), stop=False),
                        reads=[("V", j // 4), pmk], writes=[k_o[m]])
                    P.pe(lambda e, m=m, j=j, q0=q0, pm_=pm_: e.matmul(
                        PL[m][:, q0:512], lhsT=onesb, rhs=pm_[:, q0:512], start=(j == 0), stop=False),
                        reads=["cst2", pmk], writes=[KL[m]])
            for step in range(nj + 1):
                if step < nj:
                    att_front(step)
                if step >= 1:
                    att_back(step - 1)
            for m in range(2):
                P.pe(lambda e, m=m: e.matmul(p_o[m][:, :], lhsT=zerob, rhs=zero512, start=False, stop=True),
                     reads=["cst2"], writes=[k_o[m]])
                P.pe(lambda e, m=m: e.matmul(PL[m][:, :], lhsT=zerob, rhs=zero512, start=False, stop=True),
                     reads=["cst2"], writes=[KL[m]])
            for m, t, r in ((0, at0, at2), (1, at1, at2)):
                P.dve(lambda e, m=m, r=r: e.reciprocal(out=r, in_=PL[m][:, :]), writes=[KL[m], ("atr", 0)])
                P.dve(lambda e, m=m, t=t, r=r: e.tensor_tensor(out=t, in0=p_o[m][:, :], in1=r, op=ALU.mult),
                      reads=[("atr", 0)], writes=[k_o[m], ("at", m)])
            P.dve(lambda e: e.scalar_tensor_tensor(out=at0, in0=at1, scalar=neglam, in1=at0, op0=ALU.mult, op1=ALU.add),
                  reads=[("at", 1), "der"], writes=[("at", 0)])
            P.act(lambda e: e.activation(out=at2, in_=at0, func=AF.Square), reads=[("at", 0)], writes=[("atr", 0)])
            P.pe(lambda e: e.matmul(PS[0][:, :], lhsT=ones128, rhs=at2, start=True, stop=True),
                 reads=[("atr", 0), "cst2"], writes=[("ps", 0)])
            P.dve(lambda e: e.tensor_scalar(out=at2, in0=PS[0][:, :], scalar1=LN_EPS, scalar2=None, op0=ALU.add),
                  writes=[("ps", 0), ("atr", 0)])
            P.act(lambda e: e.activation(out=at2, in_=at2, func=AF.Sqrt), reads=[("atr", 0)], writes=[("atr", 0)])
            P.dve(lambda e: e.reciprocal(out=at2, in_=at2), reads=[("atr", 0)], writes=[("atr", 0)])
            P.dve(lambda e: e.tensor_tensor(out=at0, in0=at0, in1=at2, op=ALU.mult), reads=[("atr", 0), ("at", 0)],
                  writes=[("at", 0)])
            P.act(lambda e: e.activation(out=ost, in_=at0, func=AF.Identity, scale=subw), reads=[("at", 0), "der"],
                  writes=["ost"])
            P.dma("sp", attn_d[:, tsl], ost, reads=["ost"], writes=[("out", "at", tb)])
            iC = len(P.ops)
            P.interleave(iA, iB, iC)
        outs = [("out", n, tb) for n in ("dn", "at") for tb in range(nblk)]
        P.add("sp", None, reads=outs)
        P.emit()
    return nc


def t5_bucket_np(rel):
    n = np.maximum(rel, 0)
    max_exact = 16
    nf = np.maximum(n, max_exact).astype(np.float32)
    large = max_exact + (np.log(nf / np.float32(max_exact)) / np.float32(math.log(128 / max_exact))
                         * np.float32(32 - max_exact)).astype(np.int32)
    large = np.minimum(large, 31)
    return np.where(n < max_exact, n, large)


def l2_inputs(h, hT_all, w_in, conv_w, a_log, dt_bias, dn_norm_w, diff_lambda, subln_w, rel_bias):
    cols = np.concatenate([np.arange(o + h * 128, o + (h + 1) * 128) for o in range(0, 7 * 1024, 1024)]
                          + [np.array([7168 + h, 7176 + h])])
    wq = np.ascontiguousarray(w_in[:, cols].reshape(KC, 128, WQC).transpose(1, 0, 2))
    cw = np.zeros((128, 12), np.float32)
    for w in range(3):
        cw[:, w * 4:(w + 1) * 4] = conv_w[:, w * 1024 + h * 128:w * 1024 + (h + 1) * 128].T
    sc = np.zeros((128, 8), np.float32)
    sc[:, 0] = a_log[h]
    sc[:, 1] = dt_bias[h]
    sc[:, 2] = rel_bias[31, h]
    sc[:, 3] = dn_norm_w
    sc[:, 4] = subln_w
    lamp = np.ascontiguousarray(np.broadcast_to(diff_lambda.reshape(1, 256), (128, 256)))
    kk = np.arange(128)[:, None]
    qq = np.arange(256)[None, :]
    rel = qq - kk
    bt = np.where(rel >= 0, rel_bias[t5_bucket_np(rel), h], np.float32(-30000.0)).astype(np.float32)
    cst = np.zeros((128, 384), np.float32)
    cst[:, 0:128] = np.eye(128, dtype=np.float32)
    a = np.arange(128)
    cst[:, 128:256] = (a[:, None] <= a[None, :])
    cst[:, 256:384] = (a[:, None] > a[None, :])
    return {"hT": hT_all, "wq": wq, "convw": cw, "scal": sc, "lamp": lamp, "biasT": bt, "consts": cst}


def build_l3():
    nc = bass.Bass("TRN2", target_bir_lowering=False)
    ada_d = nc.dram_tensor("ada", [128, 9 * KC], F32, kind="ExternalInput").ap()
    aT_d = nc.dram_tensor("aT", [1024, TOK], BF16, kind="ExternalInput").ap()
    bT_d = nc.dram_tensor("bT", [1024, TOK], BF16, kind="ExternalInput").ap()
    h1_d = nc.dram_tensor("h1T", [D, TOK], BF16, kind="ExternalInput").ap()
    x1_d = nc.dram_tensor("x1T", [D, TOK], F32, kind="ExternalInput").ap()
    wg_d = nc.dram_tensor("wg", [KC, 128, KC, 256], F32, kind="ExternalInput").ap()
    wab_d = nc.dram_tensor("wab", [KC, 128, KC, 128], F32, kind="ExternalInput").ap()
    wo_d = nc.dram_tensor("wo", [KC, 128, KC, 128], F32, kind="ExternalInput").ap()
    win = nc.dram_tensor("win", [FFC // 2, 128, KC, 512], F32, kind="ExternalInput").ap()
    wout = nc.dram_tensor("wout", [KC, 128, FFC, 128], F32, kind="ExternalInput").ap()
    lngb = nc.dram_tensor("lngb", [128, 4 * KC], F32, kind="ExternalInput").ap()
    x2s = nc.dram_tensor("x2s", [D, TOK], F32).ap()
    outT = nc.dram_tensor("outT", [D, TOK], F32, kind="ExternalOutput").ap()
    import contextlib
    with contextlib.ExitStack() as st:
        NB = 188 * 1024
        arena_t = st.enter_context(nc.sbuf_tensor("arena", [128, NB // 4], F32))
        PS = [st.enter_context(nc.psum_tensor("ps%d" % i, [128, 512], F32)) for i in range(8)]
        A = Arena(arena_t, NB)
        P = Prog(nc)
        wbuf_off = A.off
        wbuf = [A.alloc([KC, 512], BF16) for _ in range(2)]
        wobuf_off = A.off
        wobuf = [A.alloc([FFC, 128], BF16) for _ in range(2)]
        hT_off = A.off
        h2T = A.alloc([KC, TOK], BF16)
        vT3 = A.alloc([KC, 512], F32, off=hT_off)
        act_off = A.off
        actT = A.alloc([FFC, TOK], BF16)
        ones_mat = A.alloc([128], F32)
        ln_sb = A.alloc([4 * KC], F32)
        ada = A.alloc([9 * KC], F32)
        der = A.alloc([2 * KC], F32)
        bscr = A.alloc([1], F32)
        sg = [A.alloc([512], F32) for _ in range(2)]
        h1T = A.alloc([KC, TOK], BF16, off=act_off)
        aT = A.alloc([8, TOK], BF16, off=act_off + 32 * 1024)
        bT = A.alloc([8, TOK], BF16, off=act_off + 48 * 1024)
        wgb = [A.alloc([KC, 256], BF16, off=act_off + 64 * 1024 + i * 8192) for i in range(2)]
        wabb = [A.alloc([KC, 128], BF16, off=act_off + 80 * 1024 + i * 4096) for i in range(2)]
        mergedT = A.alloc([KC, TOK], BF16, off=wbuf_off)
        vT2 = A.alloc([KC, 512], F32, off=act_off)
        P.dve(lambda e: e.memset(ones_mat, 1.0 / D), writes=["ones"])
        P.dma("sp", ln_sb, lngb, writes=["1ln", "2ln"])
        P.dma("sp", ada, ada_d, writes=["ada"])
        P.dve(lambda e: e.tensor_scalar(out=der[:, 0:KC], in0=ada[:, 7 * KC:8 * KC], scalar1=1.0, scalar2=None, op0=ALU.add),
              reads=["ada"], writes=["der"])
        P.dve(lambda e: e.tensor_scalar(out=der[:, KC:2 * KC], in0=ada[:, 8 * KC:9 * KC], scalar1=0.5, scalar2=None,
                                        op0=ALU.mult), reads=["ada"], writes=["der"])
        P.dma("sp", h1T, h1_d.rearrange("(kc p) t -> p kc t", p=128), writes=["h1T"])
        P.dma("sp", aT, aT_d.rearrange("(kc p) t -> p kc t", p=128), writes=["aT"])
        P.dma("sp", bT, bT_d.rearrange("(kc p) t -> p kc t", p=128), writes=["bT"])
        s1o = wobuf_off
        sga = [A.alloc([512], F32, off=s1o + i * 2048) for i in range(2)]
        m1 = [A.alloc([512], F32, off=s1o + 4096 + i * 2048) for i in range(2)]
        n = 0
        for fb in range(KC):
            slot = fb % 2
            P.dma("pool", wgb[slot], wg_d[fb], writes=[("wgb", slot)])
            P.dma("pool", wabb[slot], wab_d[fb], writes=[("wabb", slot)])
            for th in range(TOK // 512):
                tsl = slice(th * 512, (th + 1) * 512)
                i = n % 2
                n += 1
                pga, pgb, pya, pyb = PS[i], PS[2 + i], PS[4 + i], PS[6 + i]

                def mm(e, slot=slot, tsl=tsl, pga=pga, pgb=pgb, pya=pya, pyb=pyb):
                    inst = None
                    for kc in range(KC):
                        inst = e.matmul(pga[:, :], lhsT=wgb[slot][:, kc, 0:128], rhs=h1T[:, kc, tsl],
                                        start=(kc == 0), stop=(kc == KC - 1))
                    for kc in range(KC):
                        inst = e.matmul(pgb[:, :], lhsT=wgb[slot][:, kc, 128:256], rhs=h1T[:, kc, tsl],
                                        start=(kc == 0), stop=(kc == KC - 1))
                    for kc in range(8):
                        inst = e.matmul(pya[:, :], lhsT=wabb[slot][:, kc, :], rhs=aT[:, kc, tsl],
                                        start=(kc == 0), stop=(kc == 7))
                    for kc in range(8):
                        inst = e.matmul(pyb[:, :], lhsT=wabb[slot][:, 8 + kc, :], rhs=bT[:, kc, tsl],
                                        start=(kc == 0), stop=(kc == 7))
                    return inst
                P.pe(mm, reads=[("wgb", slot), ("wabb", slot), "h1T", "aT", "bT"],
                     writes=[("ps", i), ("ps", 2 + i), ("ps", 4 + i), ("ps", 6 + i)])
                P.act(lambda e, i=i, pga=pga: e.activation(out=sga[i], in_=pga[:, :], func=AF.Sigmoid),
                      writes=[("ps", i), ("sga", i)])
                P.dve(lambda e, i=i, pya=pya: e.tensor_tensor(out=m1[i], in0=sga[i], in1=pya[:, :], op=ALU.mult),
                      reads=[("sga", i)], writes=[("ps", 4 + i), ("m1", i)])
                P.act(lambda e, i=i, pgb=pgb: e.activation(out=sga[i], in_=pgb[:, :], func=AF.Sigmoid),
                      reads=[("m1", i)], writes=[("ps", 2 + i), ("sga", i)])
                P.dve(lambda e, i=i, pyb=pyb: e.tensor_tensor(out=sga[i], in0=sga[i], in1=pyb[:, :], op=ALU.mult),
                      reads=[("sga", i)], writes=[("ps", 6 + i), ("sga", i)])
                P.dve(lambda e, i=i, fb=fb, tsl=tsl: e.tensor_tensor(out=mergedT[:, fb, tsl], in0=m1[i], in1=sga[i],
                                                                      op=ALU.add),
                      reads=[("sga", i), ("m1", i)], writes=[("mg", fb, th)])
        P.barrier(bscr)
        hout = [A.alloc([512], F32, off=act_off + 64 * 1024 + i * 2048) for i in range(2)]
        cnt = [0]

        def consume2(kc, th, xn, key):
            tsl = slice(th * 512, (th + 1) * 512)
            P.dma("sp", x2s[kc * 128:(kc + 1) * 128, tsl], xn, reads=[key], writes=[("2xres", kc, th)])
            P.dve(lambda e, kc=kc, xn=xn, tsl=tsl: e.tensor_scalar(
                out=h2T[:, kc, tsl], in0=xn, scalar1=der[:, kc:kc + 1], scalar2=ada[:, 6 * KC + kc:6 * KC + kc + 1],
                op0=ALU.mult, op1=ALU.add), reads=[key, "der", "ada"], writes=["2hT"])
        mg_keys = lambda th: [("mg", fb, th) for fb in range(KC)]
        wo2 = [A.alloc([KC, 128], BF16, off=act_off + 80 * 1024 + i * 4096) for i in range(2)]
        emit_proj_ln(P, A, PS, mergedT, KC, mg_keys, vT2, [], wo_d, wo2, x1_d, ada[:, 5 * KC:6 * KC], ["ada"],
                     ln_sb[:, 0:KC], ln_sb[:, KC:2 * KC], ones_mat, consume2, "1", wobuf_off, [])
        P.barrier(bscr)
        def consume3(kc, th, xn, key):
            tsl = slice(th * 512, (th + 1) * 512)
            P.dma("sp", outT[kc * 128:(kc + 1) * 128, tsl], xn, reads=[key], writes=[("out", kc, th)])
        emit_ffn_in(P, A, PS, h2T, actT, win, wbuf, sg, "2")
        act_keys = lambda th: [("2actT", ffb, th * 512) for ffb in range(FFC)]
        emit_proj_ln(P, A, PS, actT, FFC, act_keys, vT3, ["2hT"], wout, wobuf, x2s, der[:, KC:2 * KC], ["der"],
                     ln_sb[:, 2 * KC:3 * KC], ln_sb[:, 3 * KC:4 * KC], ones_mat, consume3, "2", wbuf_off,
                     [("wbuf", 0), ("wbuf", 1)])
        outs = [("out", kc, th) for kc in range(KC) for th in range(TOK // 512)]
        P.add("sp", None, reads=outs)
        P.emit()
    return nc


_CACHE = {}


def _prog(name, fn):
    if name not in _CACHE:
        _CACHE[name] = fn()
    return _CACHE[name]


def kernel(x, c, w_ada, b_ada, ln_g, ln_b, w_ffn_in, w_ffn_out, w_in, conv_w, dn_a_log, dn_dt_bias, dn_norm_w,
           diff_lambda, diff_subln_w, rel_bias, w_branch_a, w_branch_b, w_out):
    f32 = lambda a: np.ascontiguousarray(np.asarray(a, dtype=np.float32))
    x, c, w_ada, b_ada, ln_g, ln_b = f32(x), f32(c), f32(w_ada), f32(b_ada), f32(ln_g), f32(ln_b)
    w_ffn_in, w_ffn_out, w_in, conv_w = f32(w_ffn_in), f32(w_ffn_out), f32(w_in), f32(conv_w)
    cores = list(range(NCORES))
    cfm = fm(c[0])
    wt = tile_w(w_ada[0], 128)
    bfm = fm(b_ada[0])
    res = run_bass_kernel_spmd(_prog("l0", build_l0), [
        {"c": cfm, "wada": np.ascontiguousarray(wt[18 * r:18 * (r + 1)]), "bada": np.ascontiguousarray(bfm[:, 18 * r:18 * (r + 1)])}
        for r in cores], core_ids=cores)
    ada = np.ascontiguousarray(np.concatenate([np.asarray(res.results[r]["ada"]) for r in cores], axis=1))
    del wt
    win0 = ffn_in_tiles(w_ffn_in[0, 0])
    wout0 = tile_w(w_ffn_out[0, 0], 128)
    lngb0 = np.concatenate([fm(ln_g[0, 0]), fm(ln_b[0, 0])], axis=1)
    xs = x[0]
    res = run_bass_kernel_spmd(_prog("l1", build_l1), [
        {"xT": np.ascontiguousarray(xs[r * TOK:(r + 1) * TOK].T), "ada": ada, "win": win0, "wout": wout0, "lngb": lngb0}
        for r in cores], core_ids=cores)
    x1T = [np.asarray(res.results[r]["x1T"]) for r in cores]
    h1T = [np.asarray(res.results[r]["h1T"]) for r in cores]
    del win0, wout0
    hT_all = np.ascontiguousarray(np.concatenate(h1T, axis=1))
    res = run_bass_kernel_spmd(_prog("l2", build_l2), [
        l2_inputs(h, hT_all, w_in[0], conv_w[0], f32(dn_a_log)[0], f32(dn_dt_bias)[0], f32(dn_norm_w)[0],
                  f32(diff_lambda)[0], f32(diff_subln_w)[0], f32(rel_bias)) for h in cores], core_ids=cores)
    aT_all = np.concatenate([np.asarray(res.results[h]["attnT"]) for h in cores], axis=0)
    bT_all = np.concatenate([np.asarray(res.results[h]["dnT"]) for h in cores], axis=0)
    del hT_all
    ga = w_in[0][:, 7184:7184 + D].reshape(KC, 128, KC, 128)
    gb = w_in[0][:, 7184 + D:7184 + 2 * D].reshape(KC, 128, KC, 128)
    wg = np.ascontiguousarray(np.concatenate([ga, gb], axis=3).transpose(2, 1, 0, 3))
    wab = tile_w(np.concatenate([f32(w_branch_a)[0], f32(w_branch_b)[0]], axis=0), 128)
    wo = tile_w(f32(w_out)[0], 128)
    win1 = ffn_in_tiles(w_ffn_in[0, 1])
    wout1 = tile_w(w_ffn_out[0, 1], 128)
    lngb1 = np.concatenate([fm(ln_g[0, 1]), fm(ln_b[0, 1]), fm(ln_g[0, 2]), fm(ln_b[0, 2])], axis=1)
    res = run_bass_kernel_spmd(_prog("l3", build_l3), [
        {"ada": ada, "aT": np.ascontiguousarray(aT_all[:, r * TOK:(r + 1) * TOK]),
         "bT": np.ascontiguousarray(bT_all[:, r * TOK:(r + 1) * TOK]), "h1T": h1T[r], "x1T": x1T[r],
         "wg": wg, "wab": wab, "wo": wo, "win": win1, "wout": wout1, "lngb": lngb1} for r in cores], core_ids=cores)
    out = np.concatenate([np.asarray(res.results[r]["outT"]).T for r in cores], axis=0)
    return np.ascontiguousarray(out.reshape(1, S, D).astype(np.float32))
```

```python
import math
import numpy as np
import ml_dtypes
import concourse.bass as bass
import concourse.mybir as mybir
from concourse.bass_utils import run_bass_kernel_spmd

F32 = mybir.dt.float32
BF16 = mybir.dt.bfloat16
AF = mybir.ActivationFunctionType
ALU = mybir.AluOpType
AX = mybir.AxisListType

NCORES = 8
D = 2048
S = 8192
TOK = S // NCORES
KC = D // 128
DFF = 5632
FFC = DFF // 128
ALPHA = 2.0 ** 0.25
LN_EPS = 1e-5
RMS_EPS = 1e-6
NDMA_SEM = 12


class Op:
    __slots__ = ("eng", "fn", "reads", "writes", "dma", "deps", "inc", "val", "sem_i", "barrier", "cost")

    def __init__(self, eng, fn, reads, writes, dma):
        self.eng, self.fn, self.reads, self.writes, self.dma = eng, fn, tuple(reads), tuple(writes), dma
        self.deps = []
        self.inc = False
        self.val = 0
        self.sem_i = 0
        self.barrier = False
        self.cost = None


class Prog:
    ENGS = ("pe", "act", "dve", "pool", "sp")

    def __init__(self, nc):
        self.nc = nc
        self.ops = []

    def add(self, eng, fn, reads=(), writes=(), dma=False, cost=None):
        op = Op(eng, fn, reads, writes, dma)
        op.cost = cost
        self.ops.append(op)

    def pe(self, fn, reads=(), writes=(), cost=None):
        self.add("pe", fn, reads, writes, cost=cost)

    def act(self, fn, reads=(), writes=()):
        self.add("act", fn, reads, writes)

    def dve(self, fn, reads=(), writes=()):
        self.add("dve", fn, reads, writes)

    def pool(self, fn, reads=(), writes=()):
        self.add("pool", fn, reads, writes)

    def dma(self, q, out, in_, reads=(), writes=()):
        self.add(q, lambda e: e.dma_start(out=out, in_=in_), reads, writes, dma=True)

    COST = {"pe": 0.3, "act": 0.45, "dve": 0.4, "pool": 0.5, "sp": 0.1}

    def interleave(self, i0, i1, i2):
        streams = [self.ops[i0:i1], self.ops[i1:i2]]
        HOP = 0.9
        deps = []
        for ops in streams:
            last_w, readers, dl = {}, {}, []
            for i, op in enumerate(ops):
                d = set()
                for r in op.reads:
                    if r in last_w:
                        d.add(last_w[r])
                for w in op.writes:
                    if w in last_w:
                        d.add(last_w[w])
                    d.update(readers.get(w, ()))
                d.discard(i)
                dl.append(d)
                for r in op.reads:
                    readers.setdefault(r, []).append(i)
                for w in op.writes:
                    last_w[w] = i
                    readers[w] = []
            deps.append(dl)
        fin = [[0.0] * len(st) for st in streams]
        ptr = [0, 0]
        free = {e: 0.0 for e in self.ENGS}
        out = []

        def start_time(si):
            i = ptr[si]
            op = streams[si][i]
            t = free[op.eng]
            for j in deps[si][i]:
                lat = HOP if streams[si][j].eng != op.eng or streams[si][j].dma else 0.15
                t = max(t, fin[si][j] + lat)
            return t
        while ptr[0] < len(streams[0]) or ptr[1] < len(streams[1]):
            cands = [si for si in (0, 1) if ptr[si] < len(streams[si])]
            best = min(cands, key=lambda si: (start_time(si), si))
            i = ptr[best]
            op = streams[best][i]
            t0 = start_time(best)
            cost = getattr(op, "cost", None) or self.COST[op.eng]
            if op.dma:
                free[op.eng] = t0 + 0.1
                fin[best][i] = t0 + 2.0
            else:
                free[op.eng] = t0 + cost
                fin[best][i] = t0 + cost
            out.append(op)
            ptr[best] += 1
        self.ops[i0:i2] = out

    def barrier(self, scratch):
        op = Op("dve", lambda e: e.memset(scratch, 0.0), (), (), False)
        op.barrier = True
        self.ops.append(op)

    def analyse(self):
        last_w = {}
        readers = {}
        ops = self.ops
        last_on = {}
        dma_hist = {e: [] for e in self.ENGS}
        pending = {}
        for i, op in enumerate(ops):
            if op.barrier:
                for e2, j in last_on.items():
                    if e2 != "dve" or True:
                        if j != i:
                            op.deps.append(j)
                            ops[j].inc = True
                for e2 in self.ENGS:
                    for j in dma_hist[e2][-NDMA_SEM:]:
                        op.deps.append(j)
                op.inc = True
                for e2 in self.ENGS:
                    pending[e2] = i
                pending.pop("dve")
                last_w.clear()
                readers.clear()
                last_on["dve"] = i
                continue
            if op.eng in pending:
                op.deps.append(pending.pop(op.eng))
            if op.dma:
                dma_hist[op.eng].append(i)
            else:
                last_on[op.eng] = i
            deps = {}
            for r in op.reads:
                j = last_w.get(r)
                if j is not None:
                    deps[j] = "raw"
            for w in op.writes:
                j = last_w.get(w)
                if j is not None and j not in deps:
                    deps[j] = "waw"
                for j in readers.get(w, ()):
                    if j not in deps:
                        deps[j] = "war"
            for j, kind in deps.items():
                if j == i:
                    continue
                a = ops[j]
                if a.dma:
                    op.deps.append(j)
                elif a.eng == op.eng:
                    if op.eng == "pe" and not op.dma:
                        continue
                    op.deps.append(j)
                    a.inc = True
                else:
                    op.deps.append(j)
                    a.inc = True
            for r in op.reads:
                readers.setdefault(r, []).append(i)
            for w in op.writes:
                last_w[w] = i
                readers[w] = []
        cnt = {e: 0 for e in self.ENGS}
        dcnt = {e: 0 for e in self.ENGS}
        self.dma_prev = {}
        hist = {e: [] for e in self.ENGS}
        for i, op in enumerate(ops):
            if op.dma:
                n = dcnt[op.eng]
                dcnt[op.eng] += 1
                op.sem_i = n % NDMA_SEM
                op.val = 16 * (n // NDMA_SEM + 1)
                hist[op.eng].append(i)
                if n >= NDMA_SEM:
                    op.deps.append(hist[op.eng][n - NDMA_SEM])
            elif op.inc:
                cnt[op.eng] += 1
                op.val = cnt[op.eng]

    def emit(self):
        self.analyse()
        nc = self.nc
        ops = self.ops
        import contextlib
        with contextlib.ExitStack() as st:
            esem = {e: st.enter_context(nc.semaphore("c_" + e)) for e in self.ENGS}
            dsem = {e: [st.enter_context(nc.semaphore("d_%s%d" % (e, k))) for k in range(NDMA_SEM)]
                    for e in ("sp", "pool", "act")}
            block = st.enter_context(nc.Block())

            def run(engname, eng):
                waited = {}
                for op in ops:
                    if op.eng != engname:
                        continue
                    for j in op.deps:
                        a = ops[j]
                        if a.dma:
                            sem, key = dsem[a.eng][a.sem_i], (a.eng, a.sem_i)
                        else:
                            sem, key = esem[a.eng], a.eng
                        if waited.get(key, 0) >= a.val:
                            continue
                        waited[key] = a.val
                        eng.wait_ge(sem, a.val)
                    if op.fn is None:
                        continue
                    inst = op.fn(eng)
                    if op.dma:
                        inst.then_inc(dsem[op.eng][op.sem_i], 16)
                    elif op.inc:
                        inst.then_inc(esem[op.eng], 1)

            @block.tensor
            def _(e):
                run("pe", e)

            @block.scalar
            def _(e):
                run("act", e)

            @block.vector
            def _(e):
                run("dve", e)

            @block.gpsimd
            def _(e):
                run("pool", e)

            @block.sync
            def _(e):
                run("sp", e)


class Arena:
    def __init__(self, t, nbytes):
        self.t = t
        self.nbytes = nbytes
        self.off = 0

    def alloc(self, shape_free, dtype, off=None):
        n = int(np.prod(shape_free))
        esz = 2 if dtype == BF16 else 4
        nb = (n * esz + 31) // 32 * 32
        if off is None:
            off = self.off
            self.off += nb
            assert self.off <= self.nbytes, ("SBUF arena overflow", self.off, self.nbytes)
        v = self.t[:, off // 4:(off + nb) // 4]
        if dtype == BF16:
            v = v.bitcast(BF16)
        v = v[:, 0:n]
        if len(shape_free) == 2:
            v = v.rearrange("p (a b) -> p a b", a=shape_free[0])
        elif len(shape_free) == 3:
            v = v.rearrange("p (a b c) -> p a b c", a=shape_free[0], b=shape_free[1])
        return v


def build_l0():
    nc = bass.Bass("TRN2", target_bir_lowering=False)
    c_d = nc.dram_tensor("c", [128, KC], F32, kind="ExternalInput").ap()
    wada = nc.dram_tensor("wada", [18, 128, KC, 128], F32, kind="ExternalInput").ap()
    bada_d = nc.dram_tensor("bada", [128, 18], F32, kind="ExternalInput").ap()
    out_d = nc.dram_tensor("ada", [128, 18], F32, kind="ExternalOutput").ap()
    import contextlib
    with contextlib.ExitStack() as st:
        NB = 64 * 1024
        arena_t = st.enter_context(nc.sbuf_tensor("arena", [128, NB // 4], F32))
        ps = st.enter_context(nc.psum_tensor("ps0", [128, 512], F32))
        A = Arena(arena_t, NB)
        P = Prog(nc)
        c_sb = A.alloc([KC], F32)
        c_bf = A.alloc([KC], BF16)
        ada = A.alloc([18], F32)
        bada = A.alloc([18], F32)
        wb = [A.alloc([KC, 128], BF16) for _ in range(4)]
        P.dma("sp", c_sb, c_d, writes=["c"])
        P.dma("sp", bada, bada_d, writes=["bada"])
        P.act(lambda e: e.activation(out=c_bf, in_=c_sb, func=AF.Silu), reads=["c"], writes=["cbf"])
        for t in range(18):
            slot = t % 4
            P.dma("pool", wb[slot], wada[t], writes=[("wb", slot)])

            def mm(e, t=t, slot=slot):
                inst = None
                for kc in range(KC):
                    inst = e.matmul(ps[:, t:t + 1], lhsT=wb[slot][:, kc, :], rhs=c_bf[:, kc:kc + 1],
                                    start=(kc == 0), stop=(kc == KC - 1))
                return inst
            P.pe(mm, reads=[("wb", slot), "cbf"], writes=["ps"])
        P.dve(lambda e: e.tensor_tensor(out=ada, in0=ps[:, 0:18], in1=bada, op=ALU.add),
              reads=["bada"], writes=["ps", "ada"])
        P.dma("sp", out_d, ada, reads=["ada"], writes=["out"])
        P.add("sp", None, reads=["out"])
        P.emit()
    return nc


def emit_ffn_in(P, A, PS, hT, actT, win_dram, wbuf, sg, tag):
    NT = TOK // 512
    n_in_tiles = FFC // 2
    for t in range(n_in_tiles):
        slot = t % 2
        wb = wbuf[slot]
        P.dma("pool", wb, win_dram[t], writes=[("wbuf", slot)])
        for b2 in range(2):
            ffb = 2 * t + b2
            for th in range(NT):
                i = (ffb * NT + th) % 2
                pg, pu = PS[i], PS[2 + i]
                tsl = slice(th * 512, (th + 1) * 512)

                def mm(e, wb=wb, b2=b2, tsl=tsl, pg=pg, pu=pu):
                    inst = None
                    for (pp, off) in ((pg, b2 * 256), (pu, b2 * 256 + 128)):
                        for kc in range(KC):
                            inst = e.matmul(pp[:, :], lhsT=wb[:, kc, off:off + 128], rhs=hT[:, kc, tsl],
                                            start=(kc == 0), stop=(kc == KC - 1))
                    return inst
                P.pe(mm, reads=[("wbuf", slot), tag + "hT"], writes=[("ps", i), ("ps", 2 + i)])
                P.act(lambda e, pg=pg, i=i: e.activation(out=sg[i], in_=pg[:, :], func=AF.Silu),
                      writes=[("ps", i), (tag + "sg", i)])
                P.dve(lambda e, pu=pu, i=i, ffb=ffb, tsl=tsl: e.tensor_tensor(
                    out=actT[:, ffb, tsl], in0=sg[i], in1=pu[:, :], op=ALU.mult),
                    reads=[(tag + "sg", i)], writes=[("ps", 2 + i), (tag + "actT", ffb, tsl.start)])


def emit_proj_ln(P, A, PS, rhsT, nK, rhs_keys, vT, vkeys_extra, wout_dram, wobuf, x_res, gate_vec, gate_keys, lng, lnb,
                 ones_mat, consume, tag, small_off, alias_keys):
    NT = TOK // 512
    o = [small_off]

    def al(shape):
        v = A.alloc(shape, F32, off=o[0])
        o[0] += int(np.prod(shape)) * 4
        return v
    xs = [al([512]) for _ in range(2)]
    tmp = [al([512]) for _ in range(2)]
    sq = [al([512]) for _ in range(2)]
    mean_sb = al([512])
    rstd = al([512])
    t1 = [al([512]) for _ in range(2)]
    small_keys = [(tag + k, i) for k in ("xs", "tmp", "sq", "t1") for i in range(2)] + [tag + "mean", tag + "rstd"]
    if alias_keys:
        P.dve(lambda e: e.memset(rstd[:, 0:1], 0.0), writes=list(alias_keys) + small_keys)
    n = 0
    for th in range(NT):
        tsl = slice(th * 512, (th + 1) * 512)
        for fb in range(KC):
            slot = n % 2
            n += 1
            wo = wobuf[slot]
            P.dma("pool", wo, wout_dram[fb], writes=[("wobuf", slot)])
            P.dma("sp", xs[slot], x_res[fb * 128:(fb + 1) * 128, tsl], reads=[(tag + "xres", fb, th)], writes=[(tag + "xs", slot)])
            py = PS[4 + slot]

            def mm(e, wo=wo, tsl=tsl, py=py):
                inst = None
                for kc in range(nK):
                    inst = e.matmul(py[:, :], lhsT=wo[:, kc, :], rhs=rhsT[:, kc, tsl],
                                    start=(kc == 0), stop=(kc == nK - 1))
                return inst
            P.pe(mm, reads=[("wobuf", slot)] + rhs_keys(th), writes=[("ps", 4 + slot)])
            P.act(lambda e, py=py, slot=slot, fb=fb: e.activation(out=tmp[slot], in_=py[:, :], func=AF.Identity,
                                                                 scale=gate_vec[:, fb:fb + 1]),
                  reads=gate_keys, writes=[("ps", 4 + slot), (tag + "tmp", slot)])
            P.dve(lambda e, fb=fb, slot=slot: e.scalar_tensor_tensor(
                out=vT[:, fb, :], in0=xs[slot], scalar=ALPHA, in1=tmp[slot], op0=ALU.mult, op1=ALU.add),
                reads=[(tag + "tmp", slot), (tag + "xs", slot)], writes=[(tag + "v", fb)] + vkeys_extra)
        pm, pq = PS[6], PS[7]
        for kc in range(KC):
            i = kc % 2
            P.act(lambda e, i=i, kc=kc: e.activation(out=sq[i], in_=vT[:, kc, :], func=AF.Square),
                  reads=[(tag + "v", kc)], writes=[(tag + "sq", i)])
            P.pe(lambda e, kc=kc, pm=pm: e.matmul(
                pm[:, :], lhsT=ones_mat, rhs=vT[:, kc, :], start=(kc == 0), stop=(kc == KC - 1)),
                reads=[(tag + "v", kc), "ones"], writes=[("ps", 6)])
            P.pe(lambda e, kc=kc, i=i, pq=pq: e.matmul(
                pq[:, :], lhsT=ones_mat, rhs=sq[i], start=(kc == 0), stop=(kc == KC - 1)),
                reads=[(tag + "sq", i), "ones"], writes=[("ps", 7)])
        P.act(lambda e, pm=pm: e.activation(out=mean_sb, in_=pm[:, :], func=AF.Identity),
              writes=[("ps", 6), tag + "mean"])
        P.dve(lambda e: e.tensor_tensor(out=rstd, in0=mean_sb, in1=mean_sb, op=ALU.mult),
              reads=[tag + "mean"], writes=[tag + "rstd"])
        P.dve(lambda e, pq=pq: e.tensor_tensor(out=rstd, in0=pq[:, :], in1=rstd, op=ALU.subtract),
              reads=[tag + "rstd"], writes=[("ps", 7), tag + "rstd"])
        P.dve(lambda e: e.tensor_scalar(out=rstd, in0=rstd, scalar1=LN_EPS, scalar2=None, op0=ALU.add),
              reads=[tag + "rstd"], writes=[tag + "rstd"])
        P.act(lambda e: e.activation(out=rstd, in_=rstd, func=AF.Sqrt), reads=[tag + "rstd"], writes=[tag + "rstd"])
        P.dve(lambda e: e.reciprocal(out=rstd, in_=rstd), reads=[tag + "rstd"], writes=[tag + "rstd"])
        for kc in range(KC):
            i = kc % 2
            P.dve(lambda e, i=i, kc=kc: e.tensor_tensor(out=t1[i], in0=vT[:, kc, :], in1=mean_sb, op=ALU.subtract),
                  reads=[(tag + "v", kc), tag + "mean"], writes=[(tag + "t1", i)])
            P.dve(lambda e, i=i: e.tensor_tensor(out=t1[i], in0=t1[i], in1=rstd, op=ALU.mult),
                  reads=[(tag + "t1", i), tag + "rstd"], writes=[(tag + "t1", i)])
            P.act(lambda e, i=i, kc=kc: e.activation(out=vT[:, kc, :], in_=t1[i], func=AF.Identity,
                                                    scale=lng[:, kc:kc + 1], bias=lnb[:, kc:kc + 1]),
                  reads=[(tag + "t1", i), tag + "ln"], writes=[(tag + "v", kc)])
            consume(kc, th, vT[:, kc, :], (tag + "v", kc))


def build_l1():
    nc = bass.Bass("TRN2", target_bir_lowering=False)
    xT = nc.dram_tensor("xT", [D, TOK], F32, kind="ExternalInput").ap()
    ada_d = nc.dram_tensor("ada", [128, 9 * KC], F32, kind="ExternalInput").ap()
    win = nc.dram_tensor("win", [FFC // 2, 128, KC, 512], F32, kind="ExternalInput").ap()
    wout = nc.dram_tensor("wout", [KC, 128, FFC, 128], F32, kind="ExternalInput").ap()
    lngb = nc.dram_tensor("lngb", [128, 2 * KC], F32, kind="ExternalInput").ap()
    x1T = nc.dram_tensor("x1T", [D, TOK], F32, kind="ExternalOutput").ap()
    h1T = nc.dram_tensor("h1T", [D, TOK], BF16, kind="ExternalOutput").ap()
    import contextlib
    with contextlib.ExitStack() as st:
        NB = 188 * 1024
        arena_t = st.enter_context(nc.sbuf_tensor("arena", [128, NB // 4], F32))
        PS = [st.enter_context(nc.psum_tensor("ps%d" % i, [128, 512], F32)) for i in range(8)]
        A = Arena(arena_t, NB)
        P = Prog(nc)
        wbuf_off = A.off
        wbuf = [A.alloc([KC, 512], BF16) for _ in range(2)]
        wobuf = [A.alloc([FFC, 128], BF16) for _ in range(2)]
        hT_off = A.off
        hT = A.alloc([KC, TOK], BF16)
        vT = A.alloc([KC, 512], F32, off=hT_off)
        act_off = A.off
        actT = A.alloc([FFC, TOK], BF16)
        ones_mat = A.alloc([128], F32)
        ln_sb = A.alloc([2 * KC], F32)
        P.dve(lambda e: e.memset(ones_mat, 1.0 / D), writes=["ones"])
        P.dma("sp", ln_sb, lngb, writes=["0ln"])
        ada = A.alloc([9 * KC], F32)
        P.dma("sp", ada, ada_d, writes=["0ada"])
        der = A.alloc([3 * KC], F32)
        P.dve(lambda e: e.tensor_scalar(out=der[:, 0:KC], in0=ada[:, KC:2 * KC], scalar1=1.0, scalar2=None, op0=ALU.add),
              reads=["0ada"], writes=["0der"])
        P.dve(lambda e: e.tensor_scalar(out=der[:, KC:2 * KC], in0=ada[:, 2 * KC:3 * KC], scalar1=0.5, scalar2=None,
                                        op0=ALU.mult), reads=["0ada"], writes=["0der"])
        P.dve(lambda e: e.tensor_scalar(out=der[:, 2 * KC:3 * KC], in0=ada[:, 4 * KC:5 * KC], scalar1=1.0, scalar2=None,
                                        op0=ALU.add), reads=["0ada"], writes=["0der"])
        xst = A.alloc([KC, TOK], F32, off=act_off)
        xT_v = xT.rearrange("(kc p) t -> p kc t", p=128)
        for g in range(4):
            P.dma("sp", xst[:, 4 * g:4 * g + 4, :], xT_v[:, 4 * g:4 * g + 4, :], writes=[("xst", g)])
        for kc in range(KC):
            P.dve(lambda e, kc=kc: e.tensor_scalar(
                out=hT[:, kc, :], in0=xst[:, kc, :], scalar1=der[:, kc:kc + 1],
                scalar2=ada[:, kc:kc + 1], op0=ALU.mult, op1=ALU.add),
                reads=[("xst", kc // 4), "0der", "0ada"], writes=["0hT"])
        bscr = A.alloc([1], F32)
        P.barrier(bscr)
        hout = [A.alloc([512], BF16) for _ in range(2)]
        cnt = [0]

        def consume(kc, th, xn, key):
            tsl = slice(th * 512, (th + 1) * 512)
            i = cnt[0] % 2
            cnt[0] += 1
            P.dma("sp", x1T[kc * 128:(kc + 1) * 128, tsl], xn, reads=[key], writes=[("out", "x1", kc, th)])
            P.dve(lambda e, i=i, kc=kc, xn=xn: e.tensor_scalar(
                out=hout[i], in0=xn, scalar1=der[:, 2 * KC + kc:2 * KC + kc + 1],
                scalar2=ada[:, 3 * KC + kc:3 * KC + kc + 1], op0=ALU.mult, op1=ALU.add),
                reads=[key, "0der", "0ada"], writes=[("hout", i)])
            P.dma("sp", h1T[kc * 128:(kc + 1) * 128, tsl], hout[i], reads=[("hout", i)], writes=[("out", "h1", kc, th)])

        sg = [A.alloc([512], F32) for _ in range(2)]
        emit_ffn_in(P, A, PS, hT, actT, win, wbuf, sg, "0")
        act_keys = lambda th: [("0actT", ffb, th * 512) for ffb in range(FFC)]
        emit_proj_ln(P, A, PS, actT, FFC, act_keys, vT, ["0hT"], wout, wobuf, xT, der[:, KC:2 * KC], ["0ada", "0der"],
                     ln_sb[:, 0:KC], ln_sb[:, KC:2 * KC], ones_mat, consume, "0", wbuf_off, [("wbuf", 0), ("wbuf", 1)])
        outs = [("out", n, kc, th) for n in ("x1", "h1") for kc in range(KC) for th in range(TOK // 512)]
        P.add("sp", None, reads=outs)
        P.emit()
    return nc


def fm(v):
    return np.ascontiguousarray(v.reshape(-1, 128).T)


def tile_w(w, fw):
    K, F = w.shape
    return np.ascontiguousarray(w.reshape(K // 128, 128, F // fw, fw).transpose(2, 1, 0, 3))


def ffn_in_tiles(w):
    g = w[:, :DFF].reshape(KC, 128, FFC, 128)
    u = w[:, DFF:].reshape(KC, 128, FFC, 128)
    gu = np.stack([g, u], axis=3)
    gu = gu.reshape(KC, 128, FFC // 2, 512)
    return np.ascontiguousarray(gu.transpose(2, 1, 0, 3))


def ada_tiles(w_ada, b_ada, vec_ids):
    cols = np.concatenate([np.arange(v * D, (v + 1) * D) for v in vec_ids])
    w = w_ada[:, cols]
    return tile_w(w, 512), fm(b_ada[cols])


NBLK = S // 512
WQC = 898


def build_l2(nblk=NBLK):
    nc = bass.Bass("TRN2", target_bir_lowering=False)
    hT_d = nc.dram_tensor("hT", [D, S], BF16, kind="ExternalInput").ap()
    wq_d = nc.dram_tensor("wq", [128, KC, WQC], F32, kind="ExternalInput").ap()
    cw_d = nc.dram_tensor("convw", [128, 12], F32, kind="ExternalInput").ap()
    sc_d = nc.dram_tensor("scal", [128, 8], F32, kind="ExternalInput").ap()
    lam_d = nc.dram_tensor("lamp", [128, 256], F32, kind="ExternalInput").ap()
    bias_d = nc.dram_tensor("biasT", [128, 256], F32, kind="ExternalInput").ap()
    cst_d = nc.dram_tensor("consts", [128, 384], F32, kind="ExternalInput").ap()
    attn_d = nc.dram_tensor("attnT", [128, S], BF16, kind="ExternalOutput").ap()
    dn_d = nc.dram_tensor("dnT", [128, S], BF16, kind="ExternalOutput").ap()
    hT_v = hT_d.rearrange("(kc p) t -> p kc t", p=128)
    import contextlib
    with contextlib.ExitStack() as st:
        NB = 188 * 1024
        arena_t = st.enter_context(nc.sbuf_tensor("arena", [128, NB // 4], F32))
        PS = [st.enter_context(nc.psum_tensor("ps%d" % i, [128, 512], F32)) for i in range(8)]
        A = Arena(arena_t, NB)
        P = Prog(nc)
        al = A.alloc
        wq = al([KC, WQC], BF16)
        cw = al([12], F32)
        sc = al([8], F32)
        lamp = al([256], F32)
        biasT = al([256], F32)
        cst = al([384], F32)
        ident = cst[:, 0:128]
        U = cst[:, 128:256]
        SL = cst[:, 256:384]
        identb = al([128], BF16)
        onesF = al([128], F32)
        negones = al([128], F32)
        ones128 = al([128], F32)
        onesb = al([128], BF16)
        one_col = al([1], F32)
        P.dma("pool", wq, wq_d, writes=["wq"])
        P.dma("sp", cw, cw_d, writes=["par"])
        P.dma("sp", sc, sc_d, writes=["par"])
        P.dma("sp", lamp, lam_d, writes=["par"])
        P.dma("sp", biasT, bias_d, writes=["par"])
        P.dma("sp", cst, cst_d, writes=["par"])
        P.dve(lambda e: e.memset(onesF, 1.0), writes=["cst2"])
        P.dve(lambda e: e.memset(negones, -1.0), writes=["cst2"])
        P.dve(lambda e: e.memset(ones128, 1.0 / 128), writes=["cst2"])
        P.dve(lambda e: e.memset(onesb, 1.0), writes=["cst2"])
        P.dve(lambda e: e.memset(one_col, 1.0), writes=["cst2"])
        zerob = al([128], BF16)
        zero512 = al([512], BF16)
        P.dve(lambda e: e.memset(zerob, 0.0), writes=["cst2"])
        P.dve(lambda e: e.memset(zero512, 0.0), writes=["cst2"])
        der = al([8], F32)
        P.act(lambda e: e.activation(out=der[:, 0:1], in_=sc[:, 0:1], func=AF.Exp), reads=["par"], writes=["der"])
        P.dve(lambda e: e.tensor_scalar(out=der[:, 0:1], in0=der[:, 0:1], scalar1=-1.0, scalar2=None, op0=ALU.mult),
              reads=["der"], writes=["der"])
        P.dve(lambda e: e.tensor_scalar(out=der[:, 1:2], in0=sc[:, 4:5], scalar1=0.8, scalar2=None, op0=ALU.mult),
              reads=["par"], writes=["der"])
        lt = al([128], F32)
        P.dve(lambda e: e.tensor_tensor(out=lt[:, 0:64], in0=lamp[:, 0:64], in1=lamp[:, 64:128], op=ALU.mult),
              reads=["par"], writes=["lt"])
        P.dve(lambda e: e.tensor_tensor(out=lt[:, 64:128], in0=lamp[:, 128:192], in1=lamp[:, 192:256], op=ALU.mult),
              reads=["par", "lt"], writes=["lt"])
        P.dve(lambda e: e.reduce_sum(out=der[:, 2:3], in_=lt[:, 0:64], axis=AX.X), reads=["lt", "der"], writes=["der"])
        P.dve(lambda e: e.reduce_sum(out=der[:, 3:4], in_=lt[:, 64:128], axis=AX.X), reads=["lt", "der"], writes=["der"])
        P.act(lambda e: e.activation(out=der[:, 2:4], in_=der[:, 2:4], func=AF.Exp), reads=["der"], writes=["der"])
        P.dve(lambda e: e.tensor_tensor(out=der[:, 4:5], in0=der[:, 3:4], in1=der[:, 2:3], op=ALU.subtract),
              reads=["der"], writes=["der"])
        P.dve(lambda e: e.tensor_scalar(out=der[:, 4:5], in0=der[:, 4:5], scalar1=-0.2, scalar2=None, op0=ALU.add),
              reads=["der"], writes=["der"])
        nea, subw, neglam = der[:, 0:1], der[:, 1:2], der[:, 4:5]
        dtb, b31, normw = sc[:, 1:2], sc[:, 2:3], sc[:, 3:4]
        qT = al([S], BF16)
        kT = al([S], BF16)
        Vtok = al([S // 128, 128], BF16)
        hblk = [al([KC, 512], BF16)]
        raw = [al([515], F32) for _ in range(3)]
        cs = [al([512], F32) for _ in range(3)]
        cs0 = [cs[0], al([512], F32)]
        zs2 = [al([512], F32) for _ in range(2)]
        ba2 = [al([512], F32) for _ in range(2)]
        cs1 = [cs[1], al([512], F32)]
        cs2 = [cs[2], al([512], F32)]
        tA = al([512], F32)
        tB = al([512], F32)
        vtmp = al([512], F32)
        Sst = al([128], F32)
        NCH = 4
        ch = []
        for c in range(NCH):
            d = {}
            for nm, w in (("kn", 128), ("v", 128), ("kbg", 128), ("kdec", 128), ("u", 128), ("o", 128),
                          ("wT", 128), ("sm", 16), ("gU", 128), ("dm", 256), ("X", 128), ("Y", 128),
                          ("X2", 128), ("Y2", 128), ("PT", 128), ("PT2", 128), ("qkT", 128)):
                d[nm] = al([w], F32)
            ch.append(d)
        seqt = [{nm: al([128], F32) for nm in ("vnew", "o1s")} for _ in range(2)]
        Pm = [[al([512], BF16) for _ in range(2)] for _ in range(2)]
        tb_ = [al([256], F32) for _ in range(2)]
        at0 = al([512], F32)
        at1 = al([512], F32)
        at2 = al([512], F32)
        ost = al([512], BF16)
        dst = al([512], BF16)
        for w in range(3):
            P.dve(lambda e, w=w: e.memset(raw[w][:, 0:3], 0.0), writes=[("raw", w)])
        P.dve(lambda e: e.memset(Sst, 0.0), writes=["S"])

        def K(c, nm):
            return ("ch", c, nm)

        def emit_ip(tb):
            tsl = slice(tb * 512, (tb + 1) * 512)
            csb = [cs0[tb % 2], cs1[tb % 2], cs2[tb % 2]]
            ba = ba2[tb % 2]
            zs = zs2[tb % 2]
            hb = hblk[0]
            hk = ("hblk", 0)
            P.dma("pool", hb, hT_v[:, :, tsl], writes=[hk])
            for ob in range(8):
                pa = PS[ob % 2]
                pk = ("ps", ob % 2)
                M = 128 if ob < 7 else 2

                def mm(e, ob=ob, pa=pa, M=M, hb=hb):
                    inst = None
                    for kc in range(KC):
                        inst = e.matmul(pa[0:M, :], lhsT=wq[:, kc, ob * 128:ob * 128 + M], rhs=hb[:, kc, :],
                                        start=(kc == 0), stop=(kc == KC - 1))
                    return inst
                P.pe(mm, reads=["wq", hk], writes=[pk], cost=3.6 if ob < 7 else 1.2)
                if ob == 0:
                    P.act(lambda e, pa=pa, tsl=tsl: e.activation(out=qT[:, tsl], in_=pa[:, :], func=AF.Identity),
                          writes=[pk, ("qT", tb)])
                elif ob == 1:
                    P.act(lambda e, pa=pa, tsl=tsl: e.activation(out=kT[:, tsl], in_=pa[:, :], func=AF.Identity),
                          writes=[pk, ("kT", tb)])
                elif ob == 2:
                    P.act(lambda e, pa=pa: e.activation(out=vtmp, in_=pa[:, :], func=AF.Identity), writes=[pk, "vtmp"])

                    def tr(e):
                        inst = None
                        for j in range(4):
                            inst = e.transpose(out=PS[2][:, j * 128:(j + 1) * 128], in_=vtmp[:, j * 128:(j + 1) * 128],
                                               identity=ident)
                        return inst
                    P.pe(tr, reads=["vtmp", "par"], writes=[("ps", 2)])
                    P.dve(lambda e, tb=tb: e.tensor_copy(
                        out=Vtok[:, tb * 4:(tb + 1) * 4, :].rearrange("p a b -> p (a b)"), in_=PS[2][:, 0:512]),
                        writes=[("ps", 2), ("V", tb)])
                elif ob in (3, 4, 5):
                    w = ob - 3
                    P.act(lambda e, pa=pa, w=w: e.activation(out=raw[w][:, 3:515], in_=pa[:, :], func=AF.Identity),
                          writes=[pk, ("raw", w)])
                    P.dve(lambda e, w=w: e.tensor_scalar(out=tA, in0=raw[w][:, 0:512], scalar1=cw[:, w * 4:w * 4 + 1],
                                                         scalar2=None, op0=ALU.mult),
                          reads=[("raw", w), "par"], writes=["tA"])
                    for i in range(1, 4):
                        P.dve(lambda e, w=w, i=i: e.scalar_tensor_tensor(
                            out=tA, in0=raw[w][:, i:i + 512], scalar=cw[:, w * 4 + i:w * 4 + i + 1], in1=tA,
                            op0=ALU.mult, op1=ALU.add), reads=[("raw", w), "par", "tA"], writes=["tA"])
                    P.dve(lambda e, w=w: e.tensor_copy(out=raw[w][:, 0:3], in_=raw[w][:, 512:515]),
                          reads=[("raw", w)], writes=[("raw", w)])
                    P.act(lambda e, w=w: e.activation(out=csb[w], in_=tA, func=AF.Silu), reads=["tA"], writes=[("cs", w, tb % 2)])
                    if w < 2:
                        P.act(lambda e, w=w: e.activation(out=tB, in_=csb[w], func=AF.Square), reads=[("cs", w, tb % 2)],
                              writes=["tB"])
                        P.pe(lambda e: e.matmul(PS[2][:, :], lhsT=onesF, rhs=tB, start=True, stop=True),
                             reads=["tB", "cst2"], writes=[("ps", 2)])
                        P.dve(lambda e: e.tensor_scalar(out=tB, in0=PS[2][:, :], scalar1=RMS_EPS, scalar2=None,
                                                        op0=ALU.add), writes=[("ps", 2), "tB"])
                        P.act(lambda e: e.activation(out=tB, in_=tB, func=AF.Sqrt), reads=["tB"], writes=["tB"])
                        P.dve(lambda e: e.reciprocal(out=tB, in_=tB), reads=["tB"], writes=["tB"])
                        sc_ = (128.0 ** -0.5) if w == 0 else 1.0
                        P.dve(lambda e, w=w, sc_=sc_: e.scalar_tensor_tensor(
                            out=csb[w], in0=csb[w], scalar=sc_, in1=tB, op0=ALU.mult, op1=ALU.mult),
                            reads=[("cs", w, tb % 2), "tB"], writes=[("cs", w, tb % 2)])
                elif ob == 6:
                    P.act(lambda e, pa=pa: e.activation(out=zs, in_=pa[:, :], func=AF.Silu), writes=[pk, ("zs", tb % 2)])
                else:
                    P.act(lambda e, pa=pa: e.activation(out=ba[0:2, :], in_=pa[0:2, :], func=AF.Identity),
                          writes=[pk, ("ba", tb % 2)])

        emit_ip(0)
        for tb in range(nblk):
            tsl = slice(tb * 512, (tb + 1) * 512)
            zs = zs2[tb % 2]
            zk = ("zs", tb % 2)
            qk_ = ("cs", 0, tb % 2)
            qn, kn_f, cv, ba = cs0[tb % 2], cs1[tb % 2], cs2[tb % 2], ba2[tb % 2]
            k1_, k2_, kb_ = ("cs", 1, tb % 2), ("cs", 2, tb % 2), ("ba", tb % 2)
            iP0 = len(P.ops)
            cl = [(c, c) for c in range(NCH)]
            NIT = 6

            def each(fn):
                for c, cb in cl:
                    fn(c, ch[c], slice(cb * 128, (cb + 1) * 128), PS[3 + c], ("ps", 3 + c),
                       PS[7][:, 16 * c:16 * c + 16], ("ps", 7))

            def s1(c, d, csl, ps, pk, ps2, pk2, qn=qn, kn_f=kn_f, cv=cv, ba=ba):
                def f(e):
                    e.transpose(out=ps[:, 0:128], in_=kn_f[:, csl], identity=ident)
                    e.transpose(out=ps[:, 128:256], in_=cv[:, csl], identity=ident)
                    e.matmul(ps[:, 256:384], lhsT=kn_f[:, csl], rhs=kn_f[:, csl], start=True, stop=True)
                    e.matmul(ps[:, 384:512], lhsT=kn_f[:, csl], rhs=qn[:, csl], start=True, stop=True)
                    return e.transpose(out=ps2[:, 4:6], in_=ba[0:2, csl], identity=ident[0:2, 0:2])
                P.pe(f, reads=[k1_, k2_, qk_, kb_, "par"], writes=[pk, pk2])
            each(s1)

            def s2(c, d, csl, ps, pk, ps2, pk2):
                sm = d["sm"]
                P.act(lambda e: e.activation(out=d["kn"], in_=ps[:, 0:128], func=AF.Identity), writes=[pk, K(c, "kn")])
                P.dve(lambda e: e.tensor_copy(out=d["v"], in_=ps[:, 128:256]), writes=[pk, K(c, "v")])
                P.act(lambda e: e.activation(out=sm[:, 0:1], in_=ps2[:, 4:5], func=AF.Sigmoid), writes=[pk2, K(c, "sm")])
                P.act(lambda e: e.activation(out=sm[:, 2:3], in_=ps2[:, 5:6], func=AF.Exp, bias=dtb),
                      reads=["par"], writes=[pk2, K(c, "sm")])
                P.act(lambda e: e.activation(out=sm[:, 2:3], in_=sm[:, 2:3], func=AF.Ln, bias=one_col),
                      reads=[K(c, "sm"), "cst2"], writes=[K(c, "sm")])
                P.dve(lambda e: e.tensor_tensor(out=sm[:, 3:4], in0=sm[:, 2:3], in1=nea, op=ALU.mult),
                      reads=[K(c, "sm"), "der"], writes=[K(c, "sm")])
                P.dve(lambda e: e.tensor_scalar(out=sm[:, 1:2], in0=sm[:, 0:1], scalar1=-1.0, scalar2=None, op0=ALU.mult),
                      reads=[K(c, "sm")], writes=[K(c, "sm")])
                P.dve(lambda e: e.tensor_scalar(out=d["gU"], in0=U, scalar1=sm[:, 3:4], scalar2=None, op0=ALU.mult),
                      reads=[K(c, "sm"), "par"], writes=[K(c, "gU")])
            each(s2)

            def s4(c, d, csl, ps, pk, ps2, pk2):
                g = d["sm"][:, 3:4]
                gU = d["gU"]

                def f(e):
                    e.matmul(ps2[:, 0:1], lhsT=U, rhs=g, start=True, stop=True)
                    e.matmul(ps2[:, 1:2], lhsT=SL, rhs=g, start=True, stop=True)
                    e.matmul(ps2[:, 2:3], lhsT=onesF, rhs=g, start=True, stop=True)
                    e.matmul(ps[:, 0:128], lhsT=gU, rhs=onesF, start=True, stop=False)
                    e.matmul(ps[:, 0:128], lhsT=negones, rhs=gU, start=False, stop=True)
                    e.matmul(ps[:, 128:256], lhsT=gU, rhs=negones, start=True, stop=False)
                    return e.matmul(ps[:, 128:256], lhsT=onesF, rhs=gU, start=False, stop=True)
                P.pe(f, reads=[K(c, "sm"), K(c, "gU"), "par", "cst2"], writes=[pk, pk2])
            each(s4)

            def s5(c, d, csl, ps, pk, ps2, pk2):
                sm = d["sm"]
                P.act(lambda e: e.activation(out=sm[:, 4:6], in_=ps2[:, 0:2], func=AF.Exp), writes=[pk2, K(c, "sm")])
                P.act(lambda e: e.activation(out=sm[:, 7:8], in_=ps2[:, 2:3], func=AF.Exp), writes=[pk2, K(c, "sm")])
                P.dve(lambda e: e.tensor_scalar(out=d["dm"], in0=ps[:, 0:256], scalar1=0.0, scalar2=None, op0=ALU.min),
                      writes=[pk, K(c, "dm")])
                P.act(lambda e: e.activation(out=d["dm"], in_=d["dm"], func=AF.Exp), reads=[K(c, "dm")], writes=[K(c, "dm")])
                P.dve(lambda e: e.tensor_tensor(out=d["dm"][:, 0:128], in0=d["dm"][:, 0:128], in1=SL, op=ALU.mult),
                      reads=[K(c, "dm"), "par"], writes=[K(c, "dm")])
                P.dve(lambda e: e.tensor_tensor(out=d["dm"][:, 128:256], in0=d["dm"][:, 128:256], in1=U, op=ALU.mult),
                      reads=[K(c, "dm"), "par"], writes=[K(c, "dm")])
                P.dve(lambda e: e.tensor_tensor(out=sm[:, 6:7], in0=sm[:, 0:1], in1=sm[:, 4:5], op=ALU.mult),
                      reads=[K(c, "sm")], writes=[K(c, "sm")])
                P.act(lambda e: e.activation(out=d["kbg"], in_=d["kn"], func=AF.Identity, scale=sm[:, 6:7]),
                      reads=[K(c, "sm"), K(c, "kn")], writes=[K(c, "kbg")])
                P.act(lambda e: e.activation(out=d["kdec"], in_=d["kn"], func=AF.Identity, scale=sm[:, 5:6]),
                      reads=[K(c, "sm"), K(c, "kn")], writes=[K(c, "kdec")])
                P.act(lambda e: e.activation(out=d["v"], in_=d["v"], func=AF.Identity, scale=sm[:, 0:1]),
                      reads=[K(c, "sm"), K(c, "v")], writes=[K(c, "v")])
            each(s5)

            def s6(c, d, csl, ps, pk, ps2, pk2):
                P.dve(lambda e: e.scalar_tensor_tensor(out=d["X"], in0=ps[:, 256:384], scalar=d["sm"][:, 1:2],
                                                       in1=d["dm"][:, 0:128], op0=ALU.mult, op1=ALU.mult),
                      reads=[K(c, "sm"), K(c, "dm")], writes=[pk, K(c, "X")])
                P.dve(lambda e: e.tensor_tensor(out=d["qkT"], in0=ps[:, 384:512], in1=d["dm"][:, 128:256], op=ALU.mult),
                      reads=[K(c, "dm")], writes=[pk, K(c, "qkT")])
            each(s6)

            def s8(c, d, csl, ps, pk, ps2, pk2):
                P.pe(lambda e: e.transpose(out=ps[:, 0:128], in_=d["X"], identity=ident), reads=[K(c, "X"), "par"],
                     writes=[pk])
                P.act(lambda e: e.activation(out=d["Y"], in_=ps[:, 0:128], func=AF.Identity), writes=[pk, K(c, "Y")])
                P.dve(lambda e: e.tensor_tensor(out=d["PT"], in0=ps[:, 0:128], in1=ident, op=ALU.add),
                      reads=["par"], writes=[pk, K(c, "PT")])
            each(s8)
            XB, YB, PB = ("X", "X2"), ("Y", "Y2"), ("PT", "PT2")

            def sa_pe(c, d, ps, pk, k):
                X, Y = d[XB[(k - 1) % 2]], d[YB[(k - 1) % 2]]

                def f(e):
                    inst = e.matmul(ps[:, 0:128], lhsT=Y, rhs=X, start=True, stop=True)
                    if k < NIT:
                        inst = e.matmul(ps[:, 128:256], lhsT=X, rhs=Y, start=True, stop=True)
                    return inst
                P.pe(f, reads=[K(c, XB[(k - 1) % 2]), K(c, YB[(k - 1) % 2])], writes=[pk])

            def sa_ev(c, d, ps, pk, k):
                P.act(lambda e: e.activation(out=d[XB[k % 2]], in_=ps[:, 0:128], func=AF.Identity),
                      writes=[pk, K(c, XB[k % 2])])
                if k < NIT:
                    P.dve(lambda e: e.tensor_copy(out=d[YB[k % 2]], in_=ps[:, 128:256]), writes=[pk, K(c, YB[k % 2])])

            def sb_pe(c, d, ps, pk, k):
                P.pe(lambda e: e.matmul(ps[:, 256:384], lhsT=d[XB[k % 2]], rhs=d[PB[(k - 1) % 2]], start=True, stop=True),
                     reads=[K(c, XB[k % 2]), K(c, PB[(k - 1) % 2])], writes=[pk])

            def sb_ev(c, d, ps, pk, k):
                P.dve(lambda e: e.tensor_tensor(out=d[PB[k % 2]], in0=ps[:, 256:384], in1=d[PB[(k - 1) % 2]], op=ALU.add),
                      reads=[K(c, PB[(k - 1) % 2])], writes=[pk, K(c, PB[k % 2])])
            each(lambda c, d, csl, ps, pk, ps2, pk2: sa_pe(c, d, ps, pk, 1))
            each(lambda c, d, csl, ps, pk, ps2, pk2: sa_ev(c, d, ps, pk, 1))
            for k in range(1, NIT + 1):
                def ph_pe(c, d, csl, ps, pk, ps2, pk2, k=k):
                    sb_pe(c, d, ps, pk, k)
                    if k < NIT:
                        sa_pe(c, d, ps, pk, k + 1)
                each(ph_pe)

                def ph_ev(c, d, csl, ps, pk, ps2, pk2, k=k):
                    sb_ev(c, d, ps, pk, k)
                    if k < NIT:
                        sa_ev(c, d, ps, pk, k + 1)
                each(ph_ev)
            PTF = PB[NIT % 2]

            def sf(c, d, csl, ps, pk, ps2, pk2):
                PT = d[PTF]

                def f(e):
                    e.matmul(ps[:, 0:128], lhsT=PT, rhs=d["v"], start=True, stop=True)
                    return e.matmul(ps[:, 128:256], lhsT=d["kbg"], rhs=PT, start=True, stop=True)
                P.pe(f, reads=[K(c, PTF), K(c, "v"), K(c, "kbg")], writes=[pk])
                P.act(lambda e: e.activation(out=d["u"], in_=ps[:, 0:128], func=AF.Identity), writes=[pk, K(c, "u")])
                P.dve(lambda e: e.tensor_copy(out=d["wT"], in_=ps[:, 128:256]), writes=[pk, K(c, "wT")])
            each(sf)
            iP1 = len(P.ops)
            if tb + 1 < nblk:
                emit_ip(tb + 1)
                P.interleave(iP0, iP1, len(P.ops))
            iA = len(P.ops)
            pa, pb, pt_ = PS[5][:, 0:256], PS[5][:, 256:512], PS[6]
            for c, cb in cl:
                d = ch[c]
                csl = slice(cb * 128, (cb + 1) * 128)
                sm = d["sm"]
                sq_ = seqt[c % 2]

                def f1(e, d=d, csl=csl, qn=qn):
                    e.matmul(pa[:, 0:128], lhsT=d["wT"], rhs=Sst, start=True, stop=True)
                    return e.matmul(pa[:, 128:256], lhsT=qn[:, csl], rhs=Sst, start=True, stop=True)
                P.pe(f1, reads=[K(c, "wT"), "S", qk_], writes=[("ps", 5)])
                P.dve(lambda e, d=d, sq_=sq_: e.tensor_tensor(out=sq_["vnew"], in0=d["u"], in1=pa[:, 0:128], op=ALU.subtract),
                      reads=[K(c, "u")], writes=[("ps", 5), ("vnew", c % 2)])
                P.dve(lambda e, sm=sm, sq_=sq_: e.tensor_scalar(out=sq_["o1s"], in0=pa[:, 128:256], scalar1=sm[:, 4:5],
                                                                scalar2=None, op0=ALU.mult),
                      reads=[K(c, "sm")], writes=[("ps", 5), ("o1s", c % 2)])

                def f2(e, d=d, sq_=sq_):
                    e.matmul(pb[:, 0:128], lhsT=d["qkT"], rhs=sq_["vnew"], start=True, stop=True)
                    return e.matmul(pb[:, 128:256], lhsT=d["kdec"], rhs=sq_["vnew"], start=True, stop=True)
                P.pe(f2, reads=[K(c, "qkT"), ("vnew", c % 2), K(c, "kdec")], writes=[("ps", 5)])
                P.dve(lambda e, sm=sm: e.scalar_tensor_tensor(out=Sst, in0=Sst, scalar=sm[:, 7:8], in1=pb[:, 128:256],
                                                              op0=ALU.mult, op1=ALU.add),
                      reads=[K(c, "sm"), "S"], writes=[("ps", 5), "S"])
                P.dve(lambda e, d=d, sq_=sq_: e.tensor_tensor(out=d["o"], in0=sq_["o1s"], in1=pb[:, 0:128], op=ALU.add),
                      reads=[("o1s", c % 2)], writes=[("ps", 5), K(c, "o")])
            def each1(fn):
                for c, cb in cl:
                    fn(c, ch[c])

            def e1(c, d):
                sm = d["sm"]
                P.dve(lambda e: e.memset(sm[:, 8:9], 0.0), reads=[K(c, "sm")], writes=[K(c, "sm")])
                P.act(lambda e: e.activation(out=d["kn"], in_=d["o"], func=AF.Square, accum_out=sm[:, 8:9]),
                      reads=[K(c, "o")], writes=[K(c, "kn"), K(c, "sm")])
            each1(e1)

            def e2(c, d):
                sm = d["sm"]
                P.dve(lambda e: e.tensor_scalar(out=sm[:, 8:9], in0=sm[:, 8:9], scalar1=1.0 / 128, scalar2=RMS_EPS,
                                                op0=ALU.mult, op1=ALU.add), reads=[K(c, "sm")], writes=[K(c, "sm")])
            each1(e2)

            def e3(c, d):
                sm = d["sm"]
                P.act(lambda e: e.activation(out=sm[:, 8:9], in_=sm[:, 8:9], func=AF.Sqrt), reads=[K(c, "sm")],
                      writes=[K(c, "sm")])
            each1(e3)

            def e4(c, d):
                sm = d["sm"]
                P.dve(lambda e: e.reciprocal(out=sm[:, 8:9], in_=sm[:, 8:9]), reads=[K(c, "sm")], writes=[K(c, "sm")])
            each1(e4)

            def e5(c, d):
                sm = d["sm"]
                P.act(lambda e: e.activation(out=d["o"], in_=d["o"], func=AF.Identity, scale=sm[:, 8:9]),
                      reads=[K(c, "sm"), K(c, "o")], writes=[K(c, "o")])
            each1(e5)

            def e6(c, d):
                P.pe(lambda e: e.transpose(out=pt_[:, c * 128:(c + 1) * 128], in_=d["o"], identity=ident),
                     reads=[K(c, "o"), "par"], writes=[("ps", 6)])
            each1(e6)
            P.dve(lambda e, zs=zs: e.scalar_tensor_tensor(out=dst, in0=pt_[:, :], scalar=normw, in1=zs, op0=ALU.mult,
                                                          op1=ALU.mult), reads=[zk, "par"], writes=[("ps", 6), "dst"])
            P.dma("sp", dn_d[:, tsl], dst, reads=["dst"], writes=[("out", "dn", tb)])
            iB = len(P.ops)
            nj = 4 * tb + 4
            p_s = (PS[0], PS[1])
            p_o = (PS[2], PS[3])
            k_s = (("ps", 0), ("ps", 1))
            k_o = (("ps", 2), ("ps", 3))
            PL = (PS[4], PS[7])
            KL = (("ps", 4), ("ps", 7))
            def att_front(j):
                a = j - 4 * tb
                q0 = max(a, 0) * 128
                jsl = slice(j * 128, (j + 1) * 128)
                qsl = slice(tb * 512 + q0, tb * 512 + 512)
                for m in range(2):
                    rows = slice(64 * m, 64 * m + 64)
                    pm_ = Pm[m][j % 2]
                    pmk = ("Pm", m, j % 2)
                    P.pe(lambda e, m=m, rows=rows, jsl=jsl, qsl=qsl, q0=q0: e.matmul(
                        p_s[m][:, q0:512], lhsT=kT[rows, jsl], rhs=qT[rows, qsl], start=True, stop=True),
                        reads=[("kT", j // 4), ("qT", tb)], writes=[k_s[m]])
                    band_lo = q0 if a >= -1 else 512
                    band_hi = min(512, (a + 2) * 128) if a >= -1 else 512
                    if a >= -1:
                        bc0 = 0 if a >= 0 else 128
                        wdt = band_hi - band_lo
                        P.dve(lambda e, m=m, band_lo=band_lo, band_hi=band_hi, bc0=bc0, wdt=wdt: e.scalar_tensor_tensor(
                            out=tb_[m][:, 0:wdt], in0=p_s[m][:, band_lo:band_hi], scalar=0.125,
                            in1=biasT[:, bc0:bc0 + wdt], op0=ALU.mult, op1=ALU.add),
                            reads=["par"], writes=[k_s[m], ("tb", m)])
                        P.act(lambda e, m=m, band_lo=band_lo, band_hi=band_hi, wdt=wdt, pm_=pm_: e.activation(
                            out=pm_[:, band_lo:band_hi], in_=tb_[m][:, 0:wdt], func=AF.Exp),
                            reads=[("tb", m)], writes=[pmk])
                    if band_hi < 512 or a < -1:
                        lo = band_hi if a >= -1 else 0
                        P.act(lambda e, m=m, lo=lo, pm_=pm_: e.activation(out=pm_[:, lo:512], in_=p_s[m][:, lo:512],
                                                                        func=AF.Exp, scale=0.125, bias=b31),
                              reads=["par"], writes=[k_s[m], pmk])

            def att_back(j):
                a = j - 4 * tb
                q0 = max(a, 0) * 128
                for m in range(2):
                    pm_ = Pm[m][j % 2]
                    pmk = ("Pm", m, j % 2)
                    P.pe(lambda e, m=m, j=j, q0=q0, pm_=pm_: e.matmul(
                        p_o[m][:, q0:512], lhsT=Vtok[:, j, :], rhs=pm_[:, q0:512], start=(j == 0), stop=False),
                        reads=[("V", j // 4), pmk], writes=[k_o[m]])
                    P.pe(lambda e, m=m, j=j, q0=q0, pm_=pm_: e.matmul(
                        PL[m][:, q0:512], lhsT=onesb, rhs=pm_[:, q0:512], start=(j == 0), stop=False),
                        reads=["cst2", pmk], writes=[KL[m]])
            for step in range(nj + 1):
                if step < nj:
                    att_front(step)
                if step >= 1:
                    att_back(step - 1)
            for m in range(2):
                P.pe(lambda e, m=m: e.matmul(p_o[m][:, :], lhsT=zerob, rhs=zero512, start=False, stop=True),
                     reads=["cst2"], writes=[k_o[m]])
                P.pe(lambda e, m=m: e.matmul(PL[m][:, :], lhsT=zerob, rhs=zero512, start=False, stop=True),
                     reads=["cst2"], writes=[KL[m]])
            for m, t, r in ((0, at0, at2), (1, at1, at2)):
                P.dve(lambda e, m=m, r=r: e.reciprocal(out=r, in_=PL[m][:, :]), writes=[KL[m], ("atr", 0)])
                P.dve(lambda e, m=m, t=t, r=r: e.tensor_tensor(out=t, in0=p_o[m][:, :], in1=r, op=ALU.mult),
                      reads=[("atr", 0)], writes=[k_o[m], ("at", m)])
            P.dve(lambda e: e.scalar_tensor_tensor(out=at0, in0=at1, scalar=neglam, in1=at0, op0=ALU.mult, op1=ALU.add),
                  reads=[("at", 1), "der"], writes=[("at", 0)])
            P.act(lambda e: e.activation(out=at2, in_=at0, func=AF.Square), reads=[("at", 0)], writes=[("atr", 0)])
            P.pe(lambda e: e.matmul(PS[0][:, :], lhsT=ones128, rhs=at2, start=True, stop=True),
                 reads=[("atr", 0), "cst2"], writes=[("ps", 0)])
            P.dve(lambda e: e.tensor_scalar(out=at2, in0=PS[0][:, :], scalar1=LN_EPS, scalar2=None, op0=ALU.add),
                  writes=[("ps", 0), ("atr", 0)])
            P.act(lambda e: e.activation(out=at2, in_=at2, func=AF.Sqrt), reads=[("atr", 0)], writes=[("atr", 0)])
            P.dve(lambda e: e.reciprocal(out=at2, in_=at2), reads=[("atr", 0)], writes=[("atr", 0)])
            P.dve(lambda e: e.tensor_tensor(out=at0, in0=at0, in1=at2, op=ALU.mult), reads=[("atr", 0), ("at", 0)],
                  writes=[("at", 0)])
            P.act(lambda e: e.activation(out=ost, in_=at0, func=AF.Identity, scale=subw), reads=[("at", 0), "der"],
                  writes=["ost"])
            P.dma("sp", attn_d[:, tsl], ost, reads=["ost"], writes=[("out", "at", tb)])
            iC = len(P.ops)
            P.interleave(iA, iB, iC)
        outs = [("out", n, tb) for n in ("dn", "at") for tb in range(nblk)]
        P.add("sp", None, reads=outs)
        P.emit()
    return nc


def t5_bucket_np(rel):
    n = np.maximum(rel, 0)
    max_exact = 16
    nf = np.maximum(n, max_exact).astype(np.float32)
    large = max_exact + (np.log(nf / np.float32(max_exact)) / np.float32(math.log(128 / max_exact))
                         * np.float32(32 - max_exact)).astype(np.int32)
    large = np.minimum(large, 31)
    return np.where(n < max_exact, n, large)


def l2_inputs(h, hT_all, w_in, conv_w, a_log, dt_bias, dn_norm_w, diff_lambda, subln_w, rel_bias):
    cols = np.concatenate([np.arange(o + h * 128, o + (h + 1) * 128) for o in range(0, 7 * 1024, 1024)]
                          + [np.array([7168 + h, 7176 + h])])
    wq = np.ascontiguousarray(w_in[:, cols].reshape(KC, 128, WQC).transpose(1, 0, 2))
    cw = np.zeros((128, 12), np.float32)
    for w in range(3):
        cw[:, w * 4:(w + 1) * 4] = conv_w[:, w * 1024 + h * 128:w * 1024 + (h + 1) * 128].T
    sc = np.zeros((128, 8), np.float32)
    sc[:, 0] = a_log[h]
    sc[:, 1] = dt_bias[h]
    sc[:, 2] = rel_bias[31, h]
    sc[:, 3] = dn_norm_w
    sc[:, 4] = subln_w
    lamp = np.ascontiguousarray(np.broadcast_to(diff_lambda.reshape(1, 256), (128, 256)))
    kk = np.arange(128)[:, None]
    qq = np.arange(256)[None, :]
    rel = qq - kk
    bt = np.where(rel >= 0, rel_bias[t5_bucket_np(rel), h], np.float32(-30000.0)).astype(np.float32)
    cst = np.zeros((128, 384), np.float32)
    cst[:, 0:128] = np.eye(128, dtype=np.float32)
    a = np.arange(128)
    cst[:, 128:256] = (a[:, None] <= a[None, :])
    cst[:, 256:384] = (a[:, None] > a[None, :])
    return {"hT": hT_all, "wq": wq, "convw": cw, "scal": sc, "lamp": lamp, "biasT": bt, "consts": cst}


def build_l3():
    nc = bass.Bass("TRN2", target_bir_lowering=False)
    ada_d = nc.dram_tensor("ada", [128, 9 * KC], F32, kind="ExternalInput").ap()
    aT_d = nc.dram_tensor("aT", [1024, TOK], BF16, kind="ExternalInput").ap()
    bT_d = nc.dram_tensor("bT", [1024, TOK], BF16, kind="ExternalInput").ap()
    h1_d = nc.dram_tensor("h1T", [D, TOK], BF16, kind="ExternalInput").ap()
    x1_d = nc.dram_tensor("x1T", [D, TOK], F32, kind="ExternalInput").ap()
    wg_d = nc.dram_tensor("wg", [KC, 128, KC, 256], F32, kind="ExternalInput").ap()
    wab_d = nc.dram_tensor("wab", [KC, 128, KC, 128], F32, kind="ExternalInput").ap()
    wo_d = nc.dram_tensor("wo", [KC, 128, KC, 128], F32, kind="ExternalInput").ap()
    win = nc.dram_tensor("win", [FFC // 2, 128, KC, 512], F32, kind="ExternalInput").ap()
    wout = nc.dram_tensor("wout", [KC, 128, FFC, 128], F32, kind="ExternalInput").ap()
    lngb = nc.dram_tensor("lngb", [128, 4 * KC], F32, kind="ExternalInput").ap()
    x2s = nc.dram_tensor("x2s", [D, TOK], F32).ap()
    outT = nc.dram_tensor("outT", [D, TOK], F32, kind="ExternalOutput").ap()
    import contextlib
    with contextlib.ExitStack() as st:
        NB = 188 * 1024
        arena_t = st.enter_context(nc.sbuf_tensor("arena", [128, NB // 4], F32))
        PS = [st.enter_context(nc.psum_tensor("ps%d" % i, [128, 512], F32)) for i in range(8)]
        A = Arena(arena_t, NB)
        P = Prog(nc)
        wbuf_off = A.off
        wbuf = [A.alloc([KC, 512], BF16) for _ in range(2)]
        wobuf_off = A.off
        wobuf = [A.alloc([FFC, 128], BF16) for _ in range(2)]
        hT_off = A.off
        h2T = A.alloc([KC, TOK], BF16)
        vT3 = A.alloc([KC, 512], F32, off=hT_off)
        act_off = A.off
        actT = A.alloc([FFC, TOK], BF16)
        ones_mat = A.alloc([128], F32)
        ln_sb = A.alloc([4 * KC], F32)
        ada = A.alloc([9 * KC], F32)
        der = A.alloc([2 * KC], F32)
        bscr = A.alloc([1], F32)
        sg = [A.alloc([512], F32) for _ in range(2)]
        h1T = A.alloc([KC, TOK], BF16, off=act_off)
        aT = A.alloc([8, TOK], BF16, off=act_off + 32 * 1024)
        bT = A.alloc([8, TOK], BF16, off=act_off + 48 * 1024)
        wgb = [A.alloc([KC, 256], BF16, off=act_off + 64 * 1024 + i * 8192) for i in range(2)]
        wabb = [A.alloc([KC, 128], BF16, off=act_off + 80 * 1024 + i * 4096) for i in range(2)]
        mergedT = A.alloc([KC, TOK], BF16, off=wbuf_off)
        vT2 = A.alloc([KC, 512], F32, off=act_off)
        P.dve(lambda e: e.memset(ones_mat, 1.0 / D), writes=["ones"])
        P.dma("sp", ln_sb, lngb, writes=["1ln", "2ln"])
        P.dma("sp", ada, ada_d, writes=["ada"])
        P.dve(lambda e: e.tensor_scalar(out=der[:, 0:KC], in0=ada[:, 7 * KC:8 * KC], scalar1=1.0, scalar2=None, op0=ALU.add),
              reads=["ada"], writes=["der"])
        P.dve(lambda e: e.tensor_scalar(out=der[:, KC:2 * KC], in0=ada[:, 8 * KC:9 * KC], scalar1=0.5, scalar2=None,
                                        op0=ALU.mult), reads=["ada"], writes=["der"])
        P.dma("sp", h1T, h1_d.rearrange("(kc p) t -> p kc t", p=128), writes=["h1T"])
        P.dma("sp", aT, aT_d.rearrange("(kc p) t -> p kc t", p=128), writes=["aT"])
        P.dma("sp", bT, bT_d.rearrange("(kc p) t -> p kc t", p=128), writes=["bT"])
        s1o = wobuf_off
        sga = [A.alloc([512], F32, off=s1o + i * 2048) for i in range(2)]
        m1 = [A.alloc([512], F32, off=s1o + 4096 + i * 2048) for i in range(2)]
        n = 0
        for fb in range(KC):
            slot = fb % 2
            P.dma("pool", wgb[slot], wg_d[fb], writes=[("wgb", slot)])
            P.dma("pool", wabb[slot], wab_d[fb], writes=[("wabb", slot)])
            for th in range(TOK // 512):
                tsl = slice(th * 512, (th + 1) * 512)
                i = n % 2
                n += 1
                pga, pgb, pya, pyb = PS[i], PS[2 + i], PS[4 + i], PS[6 + i]

                def mm(e, slot=slot, tsl=tsl, pga=pga, pgb=pgb, pya=pya, pyb=pyb):
                    inst = None
                    for kc in range(KC):
                        inst = e.matmul(pga[:, :], lhsT=wgb[slot][:, kc, 0:128], rhs=h1T[:, kc, tsl],
                                        start=(kc == 0), stop=(kc == KC - 1))
                    for kc in range(KC):
                        inst = e.matmul(pgb[:, :], lhsT=wgb[slot][:, kc, 128:256], rhs=h1T[:, kc, tsl],
                                        start=(kc == 0), stop=(kc == KC - 1))
                    for kc in range(8):
                        inst = e.matmul(pya[:, :], lhsT=wabb[slot][:, kc, :], rhs=aT[:, kc, tsl],
                                        start=(kc == 0), stop=(kc == 7))
                    for kc in range(8):
                        inst = e.matmul(pyb[:, :], lhsT=wabb[slot][:, 8 + kc, :], rhs=bT[:, kc, tsl],
                                        start=(kc == 0), stop=(kc == 7))
                    return inst
                P.pe(mm, reads=[("wgb", slot), ("wabb", slot), "h1T", "aT", "bT"],
                     writes=[("ps", i), ("ps", 2 + i), ("ps", 4 + i), ("ps", 6 + i)])
                P.act(lambda e, i=i, pga=pga: e.activation(out=sga[i], in_=pga[:, :], func=AF.Sigmoid),
                      writes=[("ps", i), ("sga", i)])
                P.dve(lambda e, i=i, pya=pya: e.tensor_tensor(out=m1[i], in0=sga[i], in1=pya[:, :], op=ALU.mult),
                      reads=[("sga", i)], writes=[("ps", 4 + i), ("m1", i)])
                P.act(lambda e, i=i, pgb=pgb: e.activation(out=sga[i], in_=pgb[:, :], func=AF.Sigmoid),
                      reads=[("m1", i)], writes=[("ps", 2 + i), ("sga", i)])
                P.dve(lambda e, i=i, pyb=pyb: e.tensor_tensor(out=sga[i], in0=sga[i], in1=pyb[:, :], op=ALU.mult),
                      reads=[("sga", i)], writes=[("ps", 6 + i), ("sga", i)])
                P.dve(lambda e, i=i, fb=fb, tsl=tsl: e.tensor_tensor(out=mergedT[:, fb, tsl], in0=m1[i], in1=sga[i],
                                                                      op=ALU.add),
                      reads=[("sga", i), ("m1", i)], writes=[("mg", fb, th)])
        P.barrier(bscr)
        hout = [A.alloc([512], F32, off=act_off + 64 * 1024 + i * 2048) for i in range(2)]
        cnt = [0]

        def consume2(kc, th, xn, key):
            tsl = slice(th * 512, (th + 1) * 512)
            P.dma("sp", x2s[kc * 128:(kc + 1) * 128, tsl], xn, reads=[key], writes=[("2xres", kc, th)])
            P.dve(lambda e, kc=kc, xn=xn, tsl=tsl: e.tensor_scalar(
                out=h2T[:, kc, tsl], in0=xn, scalar1=der[:, kc:kc + 1], scalar2=ada[:, 6 * KC + kc:6 * KC + kc + 1],
                op0=ALU.mult, op1=ALU.add), reads=[key, "der", "ada"], writes=["2hT"])
        mg_keys = lambda th: [("mg", fb, th) for fb in range(KC)]
        wo2 = [A.alloc([KC, 128], BF16, off=act_off + 80 * 1024 + i * 4096) for i in range(2)]
        emit_proj_ln(P, A, PS, mergedT, KC, mg_keys, vT2, [], wo_d, wo2, x1_d, ada[:, 5 * KC:6 * KC], ["ada"],
                     ln_sb[:, 0:KC], ln_sb[:, KC:2 * KC], ones_mat, consume2, "1", wobuf_off, [])
        P.barrier(bscr)
        def consume3(kc, th, xn, key):
            tsl = slice(th * 512, (th + 1) * 512)
            P.dma("sp", outT[kc * 128:(kc + 1) * 128, tsl], xn, reads=[key], writes=[("out", kc, th)])
        emit_ffn_in(P, A, PS, h2T, actT, win, wbuf, sg, "2")
        act_keys = lambda th: [("2actT", ffb, th * 512) for ffb in range(FFC)]
        emit_proj_ln(P, A, PS, actT, FFC, act_keys, vT3, ["2hT"], wout, wobuf, x2s, der[:, KC:2 * KC], ["der"],
                     ln_sb[:, 2 * KC:3 * KC], ln_sb[:, 3 * KC:4 * KC], ones_mat, consume3, "2", wbuf_off,
                     [("wbuf", 0), ("wbuf", 1)])
        outs = [("out", kc, th) for kc in range(KC) for th in range(TOK // 512)]
        P.add("sp", None, reads=outs)
        P.emit()
    return nc


_CACHE = {}


def _prog(name, fn):
    if name not in _CACHE:
        _CACHE[name] = fn()
    return _CACHE[name]


def kernel(x, c, w_ada, b_ada, ln_g, ln_b, w_ffn_in, w_ffn_out, w_in, conv_w, dn_a_log, dn_dt_bias, dn_norm_w,
           diff_lambda, diff_subln_w, rel_bias, w_branch_a, w_branch_b, w_out):
    f32 = lambda a: np.ascontiguousarray(np.asarray(a, dtype=np.float32))
    x, c, w_ada, b_ada, ln_g, ln_b = f32(x), f32(c), f32(w_ada), f32(b_ada), f32(ln_g), f32(ln_b)
    w_ffn_in, w_ffn_out, w_in, conv_w = f32(w_ffn_in), f32(w_ffn_out), f32(w_in), f32(conv_w)
    cores = list(range(NCORES))
    cfm = fm(c[0])
    wt = tile_w(w_ada[0], 128)
    bfm = fm(b_ada[0])
    res = run_bass_kernel_spmd(_prog("l0", build_l0), [
        {"c": cfm, "wada": np.ascontiguousarray(wt[18 * r:18 * (r + 1)]), "bada": np.ascontiguousarray(bfm[:, 18 * r:18 * (r + 1)])}
        for r in cores], core_ids=cores)
    ada = np.ascontiguousarray(np.concatenate([np.asarray(res.results[r]["ada"]) for r in cores], axis=1))
    del wt
    win0 = ffn_in_tiles(w_ffn_in[0, 0])
    wout0 = tile_w(w_ffn_out[0, 0], 128)
    lngb0 = np.concatenate([fm(ln_g[0, 0]), fm(ln_b[0, 0])], axis=1)
    xs = x[0]
    res = run_bass_kernel_spmd(_prog("l1", build_l1), [
        {"xT": np.ascontiguousarray(xs[r * TOK:(r + 1) * TOK].T), "ada": ada, "win": win0, "wout": wout0, "lngb": lngb0}
        for r in cores], core_ids=cores)
    x1T = [np.asarray(res.results[r]["x1T"]) for r in cores]
    h1T = [np.asarray(res.results[r]["h1T"]) for r in cores]
    del win0, wout0
    hT_all = np.ascontiguousarray(np.concatenate(h1T, axis=1))
    res = run_bass_kernel_spmd(_prog("l2", build_l2), [
        l2_inputs(h, hT_all, w_in[0], conv_w[0], f32(dn_a_log)[0], f32(dn_dt_bias)[0], f32(dn_norm_w)[0],
                  f32(diff_lambda)[0], f32(diff_subln_w)[0], f32(rel_bias)) for h in cores], core_ids=cores)
    aT_all = np.concatenate([np.asarray(res.results[h]["attnT"]) for h in cores], axis=0)
    bT_all = np.concatenate([np.asarray(res.results[h]["dnT"]) for h in cores], axis=0)
    del hT_all
    ga = w_in[0][:, 7184:7184 + D].reshape(KC, 128, KC, 128)
    gb = w_in[0][:, 7184 + D:7184 + 2 * D].reshape(KC, 128, KC, 128)
    wg = np.ascontiguousarray(np.concatenate([ga, gb], axis=3).transpose(2, 1, 0, 3))
    wab = tile_w(np.concatenate([f32(w_branch_a)[0], f32(w_branch_b)[0]], axis=0), 128)
    wo = tile_w(f32(w_out)[0], 128)
    win1 = ffn_in_tiles(w_ffn_in[0, 1])
    wout1 = tile_w(w_ffn_out[0, 1], 128)
    lngb1 = np.concatenate([fm(ln_g[0, 1]), fm(ln_b[0, 1]), fm(ln_g[0, 2]), fm(ln_b[0, 2])], axis=1)
    res = run_bass_kernel_spmd(_prog("l3", build_l3), [
        {"ada": ada, "aT": np.ascontiguousarray(aT_all[:, r * TOK:(r + 1) * TOK]),
         "bT": np.ascontiguousarray(bT_all[:, r * TOK:(r + 1) * TOK]), "h1T": h1T[r], "x1T": x1T[r],
         "wg": wg, "wab": wab, "wo": wo, "win": win1, "wout": wout1, "lngb": lngb1} for r in cores], core_ids=cores)
    out = np.concatenate([np.asarray(res.results[r]["outT"]).T for r in cores], axis=0)
    return np.ascontiguousarray(out.reshape(1, S, D).astype(np.float32))
```

```python
import math
import numpy as np
import ml_dtypes
import concourse.bass as bass
import concourse.mybir as mybir
from concourse.bass_utils import run_bass_kernel_spmd

F32 = mybir.dt.float32
BF16 = mybir.dt.bfloat16
AF = mybir.ActivationFunctionType
ALU = mybir.AluOpType
AX = mybir.AxisListType

NCORES = 8
D = 2048
S = 8192
TOK = S // NCORES
KC = D // 128
DFF = 5632
FFC = DFF // 128
ALPHA = 2.0 ** 0.25
LN_EPS = 1e-5
RMS_EPS = 1e-6
NDMA_SEM = 12


class Op:
    __slots__ = ("eng", "fn", "reads", "writes", "dma", "deps", "inc", "val", "sem_i", "barrier", "cost")

    def __init__(self, eng, fn, reads, writes, dma):
        self.eng, self.fn, self.reads, self.writes, self.dma = eng, fn, tuple(reads), tuple(writes), dma
        self.deps = []
        self.inc = False
        self.val = 0
        self.sem_i = 0
        self.barrier = False
        self.cost = None


class Prog:
    ENGS = ("pe", "act", "dve", "pool", "sp")

    def __init__(self, nc):
        self.nc = nc
        self.ops = []

    def add(self, eng, fn, reads=(), writes=(), dma=False, cost=None):
        op = Op(eng, fn, reads, writes, dma)
        op.cost = cost
        self.ops.append(op)

    def pe(self, fn, reads=(), writes=(), cost=None):
        self.add("pe", fn, reads, writes, cost=cost)

    def act(self, fn, reads=(), writes=()):
        self.add("act", fn, reads, writes)

    def dve(self, fn, reads=(), writes=()):
        self.add("dve", fn, reads, writes)

    def pool(self, fn, reads=(), writes=()):
        self.add("pool", fn, reads, writes)

    def dma(self, q, out, in_, reads=(), writes=()):
        self.add(q, lambda e: e.dma_start(out=out, in_=in_), reads, writes, dma=True)

    COST = {"pe": 0.3, "act": 0.45, "dve": 0.4, "pool": 0.5, "sp": 0.1}

    def interleave(self, i0, i1, i2):
        streams = [self.ops[i0:i1], self.ops[i1:i2]]
        HOP = 0.9
        deps = []
        for ops in streams:
            last_w, readers, dl = {}, {}, []
            for i, op in enumerate(ops):
                d = set()
                for r in op.reads:
                    if r in last_w:
                        d.add(last_w[r])
                for w in op.writes:
                    if w in last_w:
                        d.add(last_w[w])
                    d.update(readers.get(w, ()))
                d.discard(i)
                dl.append(d)
                for r in op.reads:
                    readers.setdefault(r, []).append(i)
                for w in op.writes:
                    last_w[w] = i
                    readers[w] = []
            deps.append(dl)
        fin = [[0.0] * len(st) for st in streams]
        ptr = [0, 0]
        free = {e: 0.0 for e in self.ENGS}
        out = []

        def start_time(si):
            i = ptr[si]
            op = streams[si][i]
            t = free[op.eng]
            for j in deps[si][i]:
                lat = HOP if streams[si][j].eng != op.eng or streams[si][j].dma else 0.15
                t = max(t, fin[si][j] + lat)
            return t
        while ptr[0] < len(streams[0]) or ptr[1] < len(streams[1]):
            cands = [si for si in (0, 1) if ptr[si] < len(streams[si])]
            best = min(cands, key=lambda si: (start_time(si), si))
            i = ptr[best]
            op = streams[best][i]
            t0 = start_time(best)
            cost = getattr(op, "cost", None) or self.COST[op.eng]
            if op.dma:
                free[op.eng] = t0 + 0.1
                fin[best][i] = t0 + 2.0
            else:
                free[op.eng] = t0 + cost
                fin[best][i] = t0 + cost
            out.append(op)
            ptr[best] += 1
        self.ops[i0:i2] = out

    def barrier(self, scratch):
        op = Op("dve", lambda e: e.memset(scratch, 0.0), (), (), False)
        op.barrier = True
        self.ops.append(op)

    def analyse(self):
        last_w = {}
        readers = {}
        ops = self.ops
        last_on = {}
        dma_hist = {e: [] for e in self.ENGS}
        pending = {}
        for i, op in enumerate(ops):
            if op.barrier:
                for e2, j in last_on.items():
                    if e2 != "dve" or True:
                        if j != i:
                            op.deps.append(j)
                            ops[j].inc = True
                for e2 in self.ENGS:
                    for j in dma_hist[e2][-NDMA_SEM:]:
                        op.deps.append(j)
                op.inc = True
                for e2 in self.ENGS:
                    pending[e2] = i
                pending.pop("dve")
                last_w.clear()
                readers.clear()
                last_on["dve"] = i
                continue
            if op.eng in pending:
                op.deps.append(pending.pop(op.eng))
            if op.dma:
                dma_hist[op.eng].append(i)
            else:
                last_on[op.eng] = i
            deps = {}
            for r in op.reads:
                j = last_w.get(r)
                if j is not None:
                    deps[j] = "raw"
            for w in op.writes:
                j = last_w.get(w)
                if j is not None and j not in deps:
                    deps[j] = "waw"
                for j in readers.get(w, ()):
                    if j not in deps:
                        deps[j] = "war"
            for j, kind in deps.items():
                if j == i:
                    continue
                a = ops[j]
                if a.dma:
                    op.deps.append(j)
                elif a.eng == op.eng:
                    if op.eng == "pe" and not op.dma:
                        continue
                    op.deps.append(j)
                    a.inc = True
                else:
                    op.deps.append(j)
                    a.inc = True
            for r in op.reads:
                readers.setdefault(r, []).append(i)
            for w in op.writes:
                last_w[w] = i
                readers[w] = []
        cnt = {e: 0 for e in self.ENGS}
        dcnt = {e: 0 for e in self.ENGS}
        self.dma_prev = {}
        hist = {e: [] for e in self.ENGS}
        for i, op in enumerate(ops):
            if op.dma:
                n = dcnt[op.eng]
                dcnt[op.eng] += 1
                op.sem_i = n % NDMA_SEM
                op.val = 16 * (n // NDMA_SEM + 1)
                hist[op.eng].append(i)
                if n >= NDMA_SEM:
                    op.deps.append(hist[op.eng][n - NDMA_SEM])
            elif op.inc:
                cnt[op.eng] += 1
                op.val = cnt[op.eng]

    def emit(self):
        self.analyse()
        nc = self.nc
        ops = self.ops
        import contextlib
        with contextlib.ExitStack() as st:
            esem = {e: st.enter_context(nc.semaphore("c_" + e)) for e in self.ENGS}
            dsem = {e: [st.enter_context(nc.semaphore("d_%s%d" % (e, k))) for k in range(NDMA_SEM)]
                    for e in ("sp", "pool", "act")}
            block = st.enter_context(nc.Block())

            def run(engname, eng):
                waited = {}
                for op in ops:
                    if op.eng != engname:
                        continue
                    for j in op.deps:
                        a = ops[j]
                        if a.dma:
                            sem, key = dsem[a.eng][a.sem_i], (a.eng, a.sem_i)
                        else:
                            sem, key = esem[a.eng], a.eng
                        if waited.get(key, 0) >= a.val:
                            continue
                        waited[key] = a.val
                        eng.wait_ge(sem, a.val)
                    if op.fn is None:
                        continue
                    inst = op.fn(eng)
                    if op.dma:
                        inst.then_inc(dsem[op.eng][op.sem_i], 16)
                    elif op.inc:
                        inst.then_inc(esem[op.eng], 1)

            @block.tensor
            def _(e):
                run("pe", e)

            @block.scalar
            def _(e):
                run("act", e)

            @block.vector
            def _(e):
                run("dve", e)

            @block.gpsimd
            def _(e):
                run("pool", e)

            @block.sync
            def _(e):
                run("sp", e)


class Arena:
    def __init__(self, t, nbytes):
        self.t = t
        self.nbytes = nbytes
        self.off = 0

    def alloc(self, shape_free, dtype, off=None):
        n = int(np.prod(shape_free))
        esz = 2 if dtype == BF16 else 4
        nb = (n * esz + 31) // 32 * 32
        if off is None:
            off = self.off
            self.off += nb
            assert self.off <= self.nbytes, ("SBUF arena overflow", self.off, self.nbytes)
        v = self.t[:, off // 4:(off + nb) // 4]
        if dtype == BF16:
            v = v.bitcast(BF16)
        v = v[:, 0:n]
        if len(shape_free) == 2:
            v = v.rearrange("p (a b) -> p a b", a=shape_free[0])
        elif len(shape_free) == 3:
            v = v.rearrange("p (a b c) -> p a b c", a=shape_free[0], b=shape_free[1])
        return v


def build_l0():
    nc = bass.Bass("TRN2", target_bir_lowering=False)
    c_d = nc.dram_tensor("c", [128, KC], F32, kind="ExternalInput").ap()
    wada = nc.dram_tensor("wada", [18, 128, KC, 128], F32, kind="ExternalInput").ap()
    bada_d = nc.dram_tensor("bada", [128, 18], F32, kind="ExternalInput").ap()
    out_d = nc.dram_tensor("ada", [128, 18], F32, kind="ExternalOutput").ap()
    import contextlib
    with contextlib.ExitStack() as st:
        NB = 64 * 1024
        arena_t = st.enter_context(nc.sbuf_tensor("arena", [128, NB // 4], F32))
        ps = st.enter_context(nc.psum_tensor("ps0", [128, 512], F32))
        A = Arena(arena_t, NB)
        P = Prog(nc)
        c_sb = A.alloc([KC], F32)
        c_bf = A.alloc([KC], BF16)
        ada = A.alloc([18], F32)
        bada = A.alloc([18], F32)
        wb = [A.alloc([KC, 128], BF16) for _ in range(4)]
        P.dma("sp", c_sb, c_d, writes=["c"])
        P.dma("sp", bada, bada_d, writes=["bada"])
        P.act(lambda e: e.activation(out=c_bf, in_=c_sb, func=AF.Silu), reads=["c"], writes=["cbf"])
        for t in range(18):
            slot = t % 4
            P.dma("pool", wb[slot], wada[t], writes=[("wb", slot)])

            def mm(e, t=t, slot=slot):
                inst = None
                for kc in range(KC):
                    inst = e.matmul(ps[:, t:t + 1], lhsT=wb[slot][:, kc, :], rhs=c_bf[:, kc:kc + 1],
                                    start=(kc == 0), stop=(kc == KC - 1))
                return inst
            P.pe(mm, reads=[("wb", slot), "cbf"], writes=["ps"])
        P.dve(lambda e: e.tensor_tensor(out=ada, in0=ps[:, 0:18], in1=bada, op=ALU.add),
              reads=["bada"], writes=["ps", "ada"])
        P.dma("sp", out_d, ada, reads=["ada"], writes=["out"])
        P.add("sp", None, reads=["out"])
        P.emit()
    return nc


def emit_ffn_in(P, A, PS, hT, actT, win_dram, wbuf, sg, tag):
    NT = TOK // 512
    n_in_tiles = FFC // 2
    for t in range(n_in_tiles):
        slot = t % 2
        wb = wbuf[slot]
        P.dma("pool", wb, win_dram[t], writes=[("wbuf", slot)])
        for b2 in range(2):
            ffb = 2 * t + b2
            for th in range(NT):
                i = (ffb * NT + th) % 2
                pg, pu = PS[i], PS[2 + i]
                tsl = slice(th * 512, (th + 1) * 512)

                def mm(e, wb=wb, b2=b2, tsl=tsl, pg=pg, pu=pu):
                    inst = None
                    for (pp, off) in ((pg, b2 * 256), (pu, b2 * 256 + 128)):
                        for kc in range(KC):
                            inst = e.matmul(pp[:, :], lhsT=wb[:, kc, off:off + 128], rhs=hT[:, kc, tsl],
                                            start=(kc == 0), stop=(kc == KC - 1))
                    return inst
                P.pe(mm, reads=[("wbuf", slot), tag + "hT"], writes=[("ps", i), ("ps", 2 + i)])
                P.act(lambda e, pg=pg, i=i: e.activation(out=sg[i], in_=pg[:, :], func=AF.Silu),
                      writes=[("ps", i), (tag + "sg", i)])
                P.dve(lambda e, pu=pu, i=i, ffb=ffb, tsl=tsl: e.tensor_tensor(
                    out=actT[:, ffb, tsl], in0=sg[i], in1=pu[:, :], op=ALU.mult),
                    reads=[(tag + "sg", i)], writes=[("ps", 2 + i), (tag + "actT", ffb, tsl.start)])


def emit_proj_ln(P, A, PS, rhsT, nK, rhs_keys, vT, vkeys_extra, wout_dram, wobuf, x_res, gate_vec, gate_keys, lng, lnb,
                 ones_mat, consume, tag, small_off, alias_keys):
    NT = TOK // 512
    o = [small_off]

    def al(shape):
        v = A.alloc(shape, F32, off=o[0])
        o[0] += int(np.prod(shape)) * 4
        return v
    xs = [al([512]) for _ in range(2)]
    tmp = [al([512]) for _ in range(2)]
    sq = [al([512]) for _ in range(2)]
    mean_sb = al([512])
    rstd = al([512])
    t1 = [al([512]) for _ in range(2)]
    small_keys = [(tag + k, i) for k in ("xs", "tmp", "sq", "t1") for i in range(2)] + [tag + "mean", tag + "rstd"]
    if alias_keys:
        P.dve(lambda e: e.memset(rstd[:, 0:1], 0.0), writes=list(alias_keys) + small_keys)
    n = [0]

    def proj_group(th, fb):
        tsl = slice(th * 512, (th + 1) * 512)
        if True:
            slot = n[0] % 2
            n[0] += 1
            wo = wobuf[slot]
            P.dma("pool", wo, wout_dram[fb], writes=[("wobuf", slot)])
            P.dma("sp", xs[slot], x_res[fb * 128:(fb + 1) * 128, tsl], reads=[(tag + "xres", fb, th)], writes=[(tag + "xs", slot)])
            py = PS[4 + slot]

            def mm(e, wo=wo, tsl=tsl, py=py):
                inst = None
                for kc in range(nK):
                    inst = e.matmul(py[:, :], lhsT=wo[:, kc, :], rhs=rhsT[:, kc, tsl],
                                    start=(kc == 0), stop=(kc == nK - 1))
                return inst
            P.pe(mm, reads=[("wobuf", slot)] + rhs_keys(th), writes=[("ps", 4 + slot)])
            P.act(lambda e, py=py, slot=slot, fb=fb: e.activation(out=tmp[slot], in_=py[:, :], func=AF.Identity,
                                                                 scale=gate_vec[:, fb:fb + 1]),
                  reads=gate_keys, writes=[("ps", 4 + slot), (tag + "tmp", slot)])
            P.dve(lambda e, fb=fb, slot=slot: e.scalar_tensor_tensor(
                out=vT[:, fb, :], in0=xs[slot], scalar=ALPHA, in1=tmp[slot], op0=ALU.mult, op1=ALU.add),
                reads=[(tag + "tmp", slot), (tag + "xs", slot)], writes=[(tag + "v", fb)] + vkeys_extra)
    for fb in range(KC):
        proj_group(0, fb)
    for th in range(NT):
        pm, pq = PS[6], PS[7]
        for kc in range(KC):
            i = kc % 2
            P.act(lambda e, i=i, kc=kc: e.activation(out=sq[i], in_=vT[:, kc, :], func=AF.Square),
                  reads=[(tag + "v", kc)], writes=[(tag + "sq", i)])
            P.pe(lambda e, kc=kc, pm=pm: e.matmul(
                pm[:, :], lhsT=ones_mat, rhs=vT[:, kc, :], start=(kc == 0), stop=(kc == KC - 1)),
                reads=[(tag + "v", kc), "ones"], writes=[("ps", 6)])
            P.pe(lambda e, kc=kc, i=i, pq=pq: e.matmul(
                pq[:, :], lhsT=ones_mat, rhs=sq[i], start=(kc == 0), stop=(kc == KC - 1)),
                reads=[(tag + "sq", i), "ones"], writes=[("ps", 7)])
        P.act(lambda e, pm=pm: e.activation(out=mean_sb, in_=pm[:, :], func=AF.Identity),
              writes=[("ps", 6), tag + "mean"])
        P.dve(lambda e: e.tensor_tensor(out=rstd, in0=mean_sb, in1=mean_sb, op=ALU.mult),
              reads=[tag + "mean"], writes=[tag + "rstd"])
        P.dve(lambda e, pq=pq: e.tensor_tensor(out=rstd, in0=pq[:, :], in1=rstd, op=ALU.subtract),
              reads=[tag + "rstd"], writes=[("ps", 7), tag + "rstd"])
        P.dve(lambda e: e.tensor_scalar(out=rstd, in0=rstd, scalar1=LN_EPS, scalar2=None, op0=ALU.add),
              reads=[tag + "rstd"], writes=[tag + "rstd"])
        P.act(lambda e: e.activation(out=rstd, in_=rstd, func=AF.Sqrt), reads=[tag + "rstd"], writes=[tag + "rstd"])
        P.dve(lambda e: e.reciprocal(out=rstd, in_=rstd), reads=[tag + "rstd"], writes=[tag + "rstd"])
        for kc in range(KC):
            i = kc % 2
            P.dve(lambda e, i=i, kc=kc: e.tensor_tensor(out=t1[i], in0=vT[:, kc, :], in1=mean_sb, op=ALU.subtract),
                  reads=[(tag + "v", kc), tag + "mean"], writes=[(tag + "t1", i)])
            P.dve(lambda e, i=i: e.tensor_tensor(out=t1[i], in0=t1[i], in1=rstd, op=ALU.mult),
                  reads=[(tag + "t1", i), tag + "rstd"], writes=[(tag + "t1", i)])
            P.act(lambda e, i=i, kc=kc: e.activation(out=vT[:, kc, :], in_=t1[i], func=AF.Identity,
                                                    scale=lng[:, kc:kc + 1], bias=lnb[:, kc:kc + 1]),
                  reads=[(tag + "t1", i), tag + "ln"], writes=[(tag + "v", kc)])
            consume(kc, th, vT[:, kc, :], (tag + "v", kc))
            if th + 1 < NT:
                proj_group(th + 1, kc)


def build_l1():
    nc = bass.Bass("TRN2", target_bir_lowering=False)
    xT = nc.dram_tensor("xT", [D, TOK], F32, kind="ExternalInput").ap()
    ada_d = nc.dram_tensor("ada", [128, 9 * KC], F32, kind="ExternalInput").ap()
    win = nc.dram_tensor("win", [FFC // 2, 128, KC, 512], F32, kind="ExternalInput").ap()
    wout = nc.dram_tensor("wout", [KC, 128, FFC, 128], F32, kind="ExternalInput").ap()
    lngb = nc.dram_tensor("lngb", [128, 2 * KC], F32, kind="ExternalInput").ap()
    x1T = nc.dram_tensor("x1T", [D, TOK], F32, kind="ExternalOutput").ap()
    h1T = nc.dram_tensor("h1T", [D, TOK], BF16, kind="ExternalOutput").ap()
    import contextlib
    with contextlib.ExitStack() as st:
        NB = 188 * 1024
        arena_t = st.enter_context(nc.sbuf_tensor("arena", [128, NB // 4], F32))
        PS = [st.enter_context(nc.psum_tensor("ps%d" % i, [128, 512], F32)) for i in range(8)]
        A = Arena(arena_t, NB)
        P = Prog(nc)
        wbuf_off = A.off
        wbuf = [A.alloc([KC, 512], BF16) for _ in range(2)]
        wobuf = [A.alloc([FFC, 128], BF16) for _ in range(2)]
        hT_off = A.off
        hT = A.alloc([KC, TOK], BF16)
        vT = A.alloc([KC, 512], F32, off=hT_off)
        act_off = A.off
        actT = A.alloc([FFC, TOK], BF16)
        ones_mat = A.alloc([128], F32)
        ln_sb = A.alloc([2 * KC], F32)
        P.dve(lambda e: e.memset(ones_mat, 1.0 / D), writes=["ones"])
        P.dma("sp", ln_sb, lngb, writes=["0ln"])
        ada = A.alloc([9 * KC], F32)
        P.dma("sp", ada, ada_d, writes=["0ada"])
        der = A.alloc([3 * KC], F32)
        P.dve(lambda e: e.tensor_scalar(out=der[:, 0:KC], in0=ada[:, KC:2 * KC], scalar1=1.0, scalar2=None, op0=ALU.add),
              reads=["0ada"], writes=["0der"])
        P.dve(lambda e: e.tensor_scalar(out=der[:, KC:2 * KC], in0=ada[:, 2 * KC:3 * KC], scalar1=0.5, scalar2=None,
                                        op0=ALU.mult), reads=["0ada"], writes=["0der"])
        P.dve(lambda e: e.tensor_scalar(out=der[:, 2 * KC:3 * KC], in0=ada[:, 4 * KC:5 * KC], scalar1=1.0, scalar2=None,
                                        op0=ALU.add), reads=["0ada"], writes=["0der"])
        xst = A.alloc([KC, TOK], F32, off=act_off)
        xT_v = xT.rearrange("(kc p) t -> p kc t", p=128)
        for g in range(4):
            P.dma("sp", xst[:, 4 * g:4 * g + 4, :], xT_v[:, 4 * g:4 * g + 4, :], writes=[("xst", g)])
        for kc in range(KC):
            P.dve(lambda e, kc=kc: e.tensor_scalar(
                out=hT[:, kc, :], in0=xst[:, kc, :], scalar1=der[:, kc:kc + 1],
                scalar2=ada[:, kc:kc + 1], op0=ALU.mult, op1=ALU.add),
                reads=[("xst", kc // 4), "0der", "0ada"], writes=["0hT"])
        bscr = A.alloc([1], F32)
        P.barrier(bscr)
        hout = [A.alloc([512], BF16) for _ in range(2)]
        cnt = [0]

        def consume(kc, th, xn, key):
            tsl = slice(th * 512, (th + 1) * 512)
            i = cnt[0] % 2
            cnt[0] += 1
            P.dma("sp", x1T[kc * 128:(kc + 1) * 128, tsl], xn, reads=[key], writes=[("out", "x1", kc, th)])
            P.dve(lambda e, i=i, kc=kc, xn=xn: e.tensor_scalar(
                out=hout[i], in0=xn, scalar1=der[:, 2 * KC + kc:2 * KC + kc + 1],
                scalar2=ada[:, 3 * KC + kc:3 * KC + kc + 1], op0=ALU.mult, op1=ALU.add),
                reads=[key, "0der", "0ada"], writes=[("hout", i)])
            P.dma("sp", h1T[kc * 128:(kc + 1) * 128, tsl], hout[i], reads=[("hout", i)], writes=[("out", "h1", kc, th)])

        sg = [A.alloc([512], F32) for _ in range(2)]
        emit_ffn_in(P, A, PS, hT, actT, win, wbuf, sg, "0")
        act_keys = lambda th: [("0actT", ffb, th * 512) for ffb in range(FFC)]
        emit_proj_ln(P, A, PS, actT, FFC, act_keys, vT, ["0hT"], wout, wobuf, xT, der[:, KC:2 * KC], ["0ada", "0der"],
                     ln_sb[:, 0:KC], ln_sb[:, KC:2 * KC], ones_mat, consume, "0", wbuf_off, [("wbuf", 0), ("wbuf", 1)])
        outs = [("out", n, kc, th) for n in ("x1", "h1") for kc in range(KC) for th in range(TOK // 512)]
        P.add("sp", None, reads=outs)
        P.emit()
    return nc


def fm(v):
    return np.ascontiguousarray(v.reshape(-1, 128).T)


def tile_w(w, fw):
    K, F = w.shape
    return np.ascontiguousarray(w.reshape(K // 128, 128, F // fw, fw).transpose(2, 1, 0, 3))


def ffn_in_tiles(w):
    g = w[:, :DFF].reshape(KC, 128, FFC, 128)
    u = w[:, DFF:].reshape(KC, 128, FFC, 128)
    gu = np.stack([g, u], axis=3)
    gu = gu.reshape(KC, 128, FFC // 2, 512)
    return np.ascontiguousarray(gu.transpose(2, 1, 0, 3))


def ada_tiles(w_ada, b_ada, vec_ids):
    cols = np.concatenate([np.arange(v * D, (v + 1) * D) for v in vec_ids])
    w = w_ada[:, cols]
    return tile_w(w, 512), fm(b_ada[cols])


NBLK = S // 512
WQC = 898


def build_l2(nblk=NBLK):
    nc = bass.Bass("TRN2", target_bir_lowering=False)
    hT_d = nc.dram_tensor("hT", [D, S], BF16, kind="ExternalInput").ap()
    wq_d = nc.dram_tensor("wq", [128, KC, WQC], F32, kind="ExternalInput").ap()
    cw_d = nc.dram_tensor("convw", [128, 12], F32, kind="ExternalInput").ap()
    sc_d = nc.dram_tensor("scal", [128, 8], F32, kind="ExternalInput").ap()
    lam_d = nc.dram_tensor("lamp", [128, 256], F32, kind="ExternalInput").ap()
    bias_d = nc.dram_tensor("biasT", [128, 256], F32, kind="ExternalInput").ap()
    cst_d = nc.dram_tensor("consts", [128, 384], F32, kind="ExternalInput").ap()
    attn_d = nc.dram_tensor("attnT", [128, S], BF16, kind="ExternalOutput").ap()
    dn_d = nc.dram_tensor("dnT", [128, S], BF16, kind="ExternalOutput").ap()
    hT_v = hT_d.rearrange("(kc p) t -> p kc t", p=128)
    import contextlib
    with contextlib.ExitStack() as st:
        NB = 188 * 1024
        arena_t = st.enter_context(nc.sbuf_tensor("arena", [128, NB // 4], F32))
        PS = [st.enter_context(nc.psum_tensor("ps%d" % i, [128, 512], F32)) for i in range(8)]
        A = Arena(arena_t, NB)
        P = Prog(nc)
        al = A.alloc
        wq = al([KC, WQC], BF16)
        cw = al([12], F32)
        sc = al([8], F32)
        lamp = al([256], F32)
        biasT = al([256], F32)
        cst = al([384], F32)
        ident = cst[:, 0:128]
        U = cst[:, 128:256]
        SL = cst[:, 256:384]
        identb = al([128], BF16)
        onesF = al([128], F32)
        negones = al([128], F32)
        ones128 = al([128], F32)
        onesb = al([128], BF16)
        one_col = al([1], F32)
        P.dma("pool", wq, wq_d, writes=["wq"])
        P.dma("sp", cw, cw_d, writes=["par"])
        P.dma("sp", sc, sc_d, writes=["par"])
        P.dma("sp", lamp, lam_d, writes=["par"])
        P.dma("sp", biasT, bias_d, writes=["par"])
        P.dma("sp", cst, cst_d, writes=["par"])
        P.dve(lambda e: e.memset(onesF, 1.0), writes=["cst2"])
        P.dve(lambda e: e.memset(negones, -1.0), writes=["cst2"])
        P.dve(lambda e: e.memset(ones128, 1.0 / 128), writes=["cst2"])
        P.dve(lambda e: e.memset(onesb, 1.0), writes=["cst2"])
        P.dve(lambda e: e.memset(one_col, 1.0), writes=["cst2"])
        zerob = al([128], BF16)
        zero512 = al([512], BF16)
        P.dve(lambda e: e.memset(zerob, 0.0), writes=["cst2"])
        P.dve(lambda e: e.memset(zero512, 0.0), writes=["cst2"])
        der = al([8], F32)
        P.act(lambda e: e.activation(out=der[:, 0:1], in_=sc[:, 0:1], func=AF.Exp), reads=["par"], writes=["der"])
        P.dve(lambda e: e.tensor_scalar(out=der[:, 0:1], in0=der[:, 0:1], scalar1=-1.0, scalar2=None, op0=ALU.mult),
              reads=["der"], writes=["der"])
        P.dve(lambda e: e.tensor_scalar(out=der[:, 1:2], in0=sc[:, 4:5], scalar1=0.8, scalar2=None, op0=ALU.mult),
              reads=["par"], writes=["der"])
        lt = al([128], F32)
        P.dve(lambda e: e.tensor_tensor(out=lt[:, 0:64], in0=lamp[:, 0:64], in1=lamp[:, 64:128], op=ALU.mult),
              reads=["par"], writes=["lt"])
        P.dve(lambda e: e.tensor_tensor(out=lt[:, 64:128], in0=lamp[:, 128:192], in1=lamp[:, 192:256], op=ALU.mult),
              reads=["par", "lt"], writes=["lt"])
        P.dve(lambda e: e.reduce_sum(out=der[:, 2:3], in_=lt[:, 0:64], axis=AX.X), reads=["lt", "der"], writes=["der"])
        P.dve(lambda e: e.reduce_sum(out=der[:, 3:4], in_=lt[:, 64:128], axis=AX.X), reads=["lt", "der"], writes=["der"])
        P.act(lambda e: e.activation(out=der[:, 2:4], in_=der[:, 2:4], func=AF.Exp), reads=["der"], writes=["der"])
        P.dve(lambda e: e.tensor_tensor(out=der[:, 4:5], in0=der[:, 3:4], in1=der[:, 2:3], op=ALU.subtract),
              reads=["der"], writes=["der"])
        P.dve(lambda e: e.tensor_scalar(out=der[:, 4:5], in0=der[:, 4:5], scalar1=-0.2, scalar2=None, op0=ALU.add),
              reads=["der"], writes=["der"])
        nea, subw, neglam = der[:, 0:1], der[:, 1:2], der[:, 4:5]
        dtb, b31, normw = sc[:, 1:2], sc[:, 2:3], sc[:, 3:4]
        qT = al([S], BF16)
        kT = al([S], BF16)
        Vtok = al([S // 128, 128], BF16)
        hblk = [al([KC, 512], BF16)]
        raw = [al([515], F32) for _ in range(3)]
        cs = [al([512], F32) for _ in range(3)]
        cs0 = [cs[0], al([512], F32)]
        zs2 = [al([512], F32) for _ in range(2)]
        ba2 = [al([512], F32) for _ in range(2)]
        cs1 = [cs[1], al([512], F32)]
        cs2 = [cs[2], al([512], F32)]
        tA = al([512], F32)
        tB = al([512], F32)
        vtmp = al([512], F32)
        Sst = al([128], F32)
        NCH = 4
        ch = []
        for c in range(NCH):
            d = {}
            for nm, w in (("kn", 128), ("v", 128), ("kbg", 128), ("kdec", 128), ("u", 128), ("o", 128),
                          ("wT", 128), ("sm", 16), ("gU", 128), ("dm", 256), ("X", 128), ("Y", 128),
                          ("X2", 128), ("Y2", 128), ("PT", 128), ("PT2", 128), ("qkT", 128)):
                d[nm] = al([w], F32)
            ch.append(d)
        seqt = [{nm: al([128], F32) for nm in ("vnew", "o1s")} for _ in range(2)]
        Pm = [[al([512], BF16) for _ in range(2)] for _ in range(2)]
        tb_ = [al([256], F32) for _ in range(2)]
        at0 = al([512], F32)
        at1 = al([512], F32)
        at2 = al([512], F32)
        ost = al([512], BF16)
        dst = al([512], BF16)
        for w in range(3):
            P.dve(lambda e, w=w: e.memset(raw[w][:, 0:3], 0.0), writes=[("raw", w)])
        P.dve(lambda e: e.memset(Sst, 0.0), writes=["S"])

        def K(c, nm):
            return ("ch", c, nm)

        def emit_ip(tb):
            tsl = slice(tb * 512, (tb + 1) * 512)
            csb = [cs0[tb % 2], cs1[tb % 2], cs2[tb % 2]]
            ba = ba2[tb % 2]
            zs = zs2[tb % 2]
            hb = hblk[0]
            hk = ("hblk", 0)
            P.dma("pool", hb, hT_v[:, :, tsl], writes=[hk])
            for ob in range(8):
                pa = PS[ob % 2]
                pk = ("ps", ob % 2)
                M = 128 if ob < 7 else 2

                def mm(e, ob=ob, pa=pa, M=M, hb=hb):
                    inst = None
                    for kc in range(KC):
                        inst = e.matmul(pa[0:M, :], lhsT=wq[:, kc, ob * 128:ob * 128 + M], rhs=hb[:, kc, :],
                                        start=(kc == 0), stop=(kc == KC - 1))
                    return inst
                P.pe(mm, reads=["wq", hk], writes=[pk], cost=3.6 if ob < 7 else 1.2)
                if ob == 0:
                    P.act(lambda e, pa=pa, tsl=tsl: e.activation(out=qT[:, tsl], in_=pa[:, :], func=AF.Identity),
                          writes=[pk, ("qT", tb)])
                elif ob == 1:
                    P.act(lambda e, pa=pa, tsl=tsl: e.activation(out=kT[:, tsl], in_=pa[:, :], func=AF.Identity),
                          writes=[pk, ("kT", tb)])
                elif ob == 2:
                    P.act(lambda e, pa=pa: e.activation(out=vtmp, in_=pa[:, :], func=AF.Identity), writes=[pk, "vtmp"])

                    def tr(e):
                        inst = None
                        for j in range(4):
                            inst = e.transpose(out=PS[2][:, j * 128:(j + 1) * 128], in_=vtmp[:, j * 128:(j + 1) * 128],
                                               identity=ident)
                        return inst
                    P.pe(tr, reads=["vtmp", "par"], writes=[("ps", 2)])
                    P.dve(lambda e, tb=tb: e.tensor_copy(
                        out=Vtok[:, tb * 4:(tb + 1) * 4, :].rearrange("p a b -> p (a b)"), in_=PS[2][:, 0:512]),
                        writes=[("ps", 2), ("V", tb)])
                elif ob in (3, 4, 5):
                    w = ob - 3
                    P.act(lambda e, pa=pa, w=w: e.activation(out=raw[w][:, 3:515], in_=pa[:, :], func=AF.Identity),
                          writes=[pk, ("raw", w)])
                    P.dve(lambda e, w=w: e.tensor_scalar(out=tA, in0=raw[w][:, 0:512], scalar1=cw[:, w * 4:w * 4 + 1],
                                                         scalar2=None, op0=ALU.mult),
                          reads=[("raw", w), "par"], writes=["tA"])
                    for i in range(1, 4):
                        P.dve(lambda e, w=w, i=i: e.scalar_tensor_tensor(
                            out=tA, in0=raw[w][:, i:i + 512], scalar=cw[:, w * 4 + i:w * 4 + i + 1], in1=tA,
                            op0=ALU.mult, op1=ALU.add), reads=[("raw", w), "par", "tA"], writes=["tA"])
                    P.dve(lambda e, w=w: e.tensor_copy(out=raw[w][:, 0:3], in_=raw[w][:, 512:515]),
                          reads=[("raw", w)], writes=[("raw", w)])
                    P.act(lambda e, w=w: e.activation(out=csb[w], in_=tA, func=AF.Silu), reads=["tA"], writes=[("cs", w, tb % 2)])
                    if w < 2:
                        P.act(lambda e, w=w: e.activation(out=tB, in_=csb[w], func=AF.Square), reads=[("cs", w, tb % 2)],
                              writes=["tB"])
                        P.pe(lambda e: e.matmul(PS[2][:, :], lhsT=onesF, rhs=tB, start=True, stop=True),
                             reads=["tB", "cst2"], writes=[("ps", 2)])
                        P.dve(lambda e: e.tensor_scalar(out=tB, in0=PS[2][:, :], scalar1=RMS_EPS, scalar2=None,
                                                        op0=ALU.add), writes=[("ps", 2), "tB"])
                        P.act(lambda e: e.activation(out=tB, in_=tB, func=AF.Sqrt), reads=["tB"], writes=["tB"])
                        P.dve(lambda e: e.reciprocal(out=tB, in_=tB), reads=["tB"], writes=["tB"])
                        sc_ = (128.0 ** -0.5) if w == 0 else 1.0
                        P.dve(lambda e, w=w, sc_=sc_: e.scalar_tensor_tensor(
                            out=csb[w], in0=csb[w], scalar=sc_, in1=tB, op0=ALU.mult, op1=ALU.mult),
                            reads=[("cs", w, tb % 2), "tB"], writes=[("cs", w, tb % 2)])
                elif ob == 6:
                    P.act(lambda e, pa=pa: e.activation(out=zs, in_=pa[:, :], func=AF.Silu), writes=[pk, ("zs", tb % 2)])
                else:
                    P.act(lambda e, pa=pa: e.activation(out=ba[0:2, :], in_=pa[0:2, :], func=AF.Identity),
                          writes=[pk, ("ba", tb % 2)])

        emit_ip(0)
        for tb in range(nblk):
            tsl = slice(tb * 512, (tb + 1) * 512)
            zs = zs2[tb % 2]
            zk = ("zs", tb % 2)
            qk_ = ("cs", 0, tb % 2)
            qn, kn_f, cv, ba = cs0[tb % 2], cs1[tb % 2], cs2[tb % 2], ba2[tb % 2]
            k1_, k2_, kb_ = ("cs", 1, tb % 2), ("cs", 2, tb % 2), ("ba", tb % 2)
            iP0 = len(P.ops)
            cl = [(c, c) for c in range(NCH)]
            NIT = 6

            def each(fn):
                for c, cb in cl:
                    fn(c, ch[c], slice(cb * 128, (cb + 1) * 128), PS[3 + c], ("ps", 3 + c),
                       PS[7][:, 16 * c:16 * c + 16], ("ps", 7))

            def s1(c, d, csl, ps, pk, ps2, pk2, qn=qn, kn_f=kn_f, cv=cv, ba=ba):
                def f(e):
                    e.transpose(out=ps[:, 0:128], in_=kn_f[:, csl], identity=ident)
                    e.transpose(out=ps[:, 128:256], in_=cv[:, csl], identity=ident)
                    e.matmul(ps[:, 256:384], lhsT=kn_f[:, csl], rhs=kn_f[:, csl], start=True, stop=True)
                    e.matmul(ps[:, 384:512], lhsT=kn_f[:, csl], rhs=qn[:, csl], start=True, stop=True)
                    return e.transpose(out=ps2[:, 4:6], in_=ba[0:2, csl], identity=ident[0:2, 0:2])
                P.pe(f, reads=[k1_, k2_, qk_, kb_, "par"], writes=[pk, pk2])
            each(s1)

            def s2(c, d, csl, ps, pk, ps2, pk2):
                sm = d["sm"]
                P.act(lambda e: e.activation(out=d["kn"], in_=ps[:, 0:128], func=AF.Identity), writes=[pk, K(c, "kn")])
                P.dve(lambda e: e.tensor_copy(out=d["v"], in_=ps[:, 128:256]), writes=[pk, K(c, "v")])
                P.act(lambda e: e.activation(out=sm[:, 0:1], in_=ps2[:, 4:5], func=AF.Sigmoid), writes=[pk2, K(c, "sm")])
                P.act(lambda e: e.activation(out=sm[:, 2:3], in_=ps2[:, 5:6], func=AF.Exp, bias=dtb),
                      reads=["par"], writes=[pk2, K(c, "sm")])
                P.act(lambda e: e.activation(out=sm[:, 2:3], in_=sm[:, 2:3], func=AF.Ln, bias=one_col),
                      reads=[K(c, "sm"), "cst2"], writes=[K(c, "sm")])
                P.dve(lambda e: e.tensor_tensor(out=sm[:, 3:4], in0=sm[:, 2:3], in1=nea, op=ALU.mult),
                      reads=[K(c, "sm"), "der"], writes=[K(c, "sm")])
                P.dve(lambda e: e.tensor_scalar(out=sm[:, 1:2], in0=sm[:, 0:1], scalar1=-1.0, scalar2=None, op0=ALU.mult),
                      reads=[K(c, "sm")], writes=[K(c, "sm")])
                P.dve(lambda e: e.tensor_scalar(out=d["gU"], in0=U, scalar1=sm[:, 3:4], scalar2=None, op0=ALU.mult),
                      reads=[K(c, "sm"), "par"], writes=[K(c, "gU")])
            each(s2)

            def s4(c, d, csl, ps, pk, ps2, pk2):
                g = d["sm"][:, 3:4]
                gU = d["gU"]

                def f(e):
                    e.matmul(ps2[:, 0:1], lhsT=U, rhs=g, start=True, stop=True)
                    e.matmul(ps2[:, 1:2], lhsT=SL, rhs=g, start=True, stop=True)
                    e.matmul(ps2[:, 2:3], lhsT=onesF, rhs=g, start=True, stop=True)
                    e.matmul(ps[:, 0:128], lhsT=gU, rhs=onesF, start=True, stop=False)
                    e.matmul(ps[:, 0:128], lhsT=negones, rhs=gU, start=False, stop=True)
                    e.matmul(ps[:, 128:256], lhsT=gU, rhs=negones, start=True, stop=False)
                    return e.matmul(ps[:, 128:256], lhsT=onesF, rhs=gU, start=False, stop=True)
                P.pe(f, reads=[K(c, "sm"), K(c, "gU"), "par", "cst2"], writes=[pk, pk2])
            each(s4)

            def s5(c, d, csl, ps, pk, ps2, pk2):
                sm = d["sm"]
                P.act(lambda e: e.activation(out=sm[:, 4:6], in_=ps2[:, 0:2], func=AF.Exp), writes=[pk2, K(c, "sm")])
                P.act(lambda e: e.activation(out=sm[:, 7:8], in_=ps2[:, 2:3], func=AF.Exp), writes=[pk2, K(c, "sm")])
                P.dve(lambda e: e.tensor_scalar(out=d["dm"], in0=ps[:, 0:256], scalar1=0.0, scalar2=None, op0=ALU.min),
                      writes=[pk, K(c, "dm")])
                P.act(lambda e: e.activation(out=d["dm"], in_=d["dm"], func=AF.Exp), reads=[K(c, "dm")], writes=[K(c, "dm")])
                P.dve(lambda e: e.tensor_tensor(out=d["dm"][:, 0:128], in0=d["dm"][:, 0:128], in1=SL, op=ALU.mult),
                      reads=[K(c, "dm"), "par"], writes=[K(c, "dm")])
                P.dve(lambda e: e.tensor_tensor(out=d["dm"][:, 128:256], in0=d["dm"][:, 128:256], in1=U, op=ALU.mult),
                      reads=[K(c, "dm"), "par"], writes=[K(c, "dm")])
                P.dve(lambda e: e.tensor_tensor(out=sm[:, 6:7], in0=sm[:, 0:1], in1=sm[:, 4:5], op=ALU.mult),
                      reads=[K(c, "sm")], writes=[K(c, "sm")])
                P.act(lambda e: e.activation(out=d["kbg"], in_=d["kn"], func=AF.Identity, scale=sm[:, 6:7]),
                      reads=[K(c, "sm"), K(c, "kn")], writes=[K(c, "kbg")])
                P.act(lambda e: e.activation(out=d["kdec"], in_=d["kn"], func=AF.Identity, scale=sm[:, 5:6]),
                      reads=[K(c, "sm"), K(c, "kn")], writes=[K(c, "kdec")])
                P.act(lambda e: e.activation(out=d["v"], in_=d["v"], func=AF.Identity, scale=sm[:, 0:1]),
                      reads=[K(c, "sm"), K(c, "v")], writes=[K(c, "v")])
            each(s5)

            def s6(c, d, csl, ps, pk, ps2, pk2):
                P.dve(lambda e: e.scalar_tensor_tensor(out=d["X"], in0=ps[:, 256:384], scalar=d["sm"][:, 1:2],
                                                       in1=d["dm"][:, 0:128], op0=ALU.mult, op1=ALU.mult),
                      reads=[K(c, "sm"), K(c, "dm")], writes=[pk, K(c, "X")])
                P.dve(lambda e: e.tensor_tensor(out=d["qkT"], in0=ps[:, 384:512], in1=d["dm"][:, 128:256], op=ALU.mult),
                      reads=[K(c, "dm")], writes=[pk, K(c, "qkT")])
            each(s6)

            def s8(c, d, csl, ps, pk, ps2, pk2):
                P.pe(lambda e: e.transpose(out=ps[:, 0:128], in_=d["X"], identity=ident), reads=[K(c, "X"), "par"],
                     writes=[pk])
                P.act(lambda e: e.activation(out=d["Y"], in_=ps[:, 0:128], func=AF.Identity), writes=[pk, K(c, "Y")])
                P.dve(lambda e: e.tensor_tensor(out=d["PT"], in0=ps[:, 0:128], in1=ident, op=ALU.add),
                      reads=["par"], writes=[pk, K(c, "PT")])
            each(s8)
            XB, YB, PB = ("X", "X2"), ("Y", "Y2"), ("PT", "PT2")

            def sa_pe(c, d, ps, pk, k):
                X, Y = d[XB[(k - 1) % 2]], d[YB[(k - 1) % 2]]

                def f(e):
                    inst = e.matmul(ps[:, 0:128], lhsT=Y, rhs=X, start=True, stop=True)
                    if k < NIT:
                        inst = e.matmul(ps[:, 128:256], lhsT=X, rhs=Y, start=True, stop=True)
                    return inst
                P.pe(f, reads=[K(c, XB[(k - 1) % 2]), K(c, YB[(k - 1) % 2])], writes=[pk])

            def sa_ev(c, d, ps, pk, k):
                P.act(lambda e: e.activation(out=d[XB[k % 2]], in_=ps[:, 0:128], func=AF.Identity),
                      writes=[pk, K(c, XB[k % 2])])
                if k < NIT:
                    P.dve(lambda e: e.tensor_copy(out=d[YB[k % 2]], in_=ps[:, 128:256]), writes=[pk, K(c, YB[k % 2])])

            def sb_pe(c, d, ps, pk, k):
                P.pe(lambda e: e.matmul(ps[:, 256:384], lhsT=d[XB[k % 2]], rhs=d[PB[(k - 1) % 2]], start=True, stop=True),
                     reads=[K(c, XB[k % 2]), K(c, PB[(k - 1) % 2])], writes=[pk])

            def sb_ev(c, d, ps, pk, k):
                P.dve(lambda e: e.tensor_tensor(out=d[PB[k % 2]], in0=ps[:, 256:384], in1=d[PB[(k - 1) % 2]], op=ALU.add),
                      reads=[K(c, PB[(k - 1) % 2])], writes=[pk, K(c, PB[k % 2])])
            each(lambda c, d, csl, ps, pk, ps2, pk2: sa_pe(c, d, ps, pk, 1))
            each(lambda c, d, csl, ps, pk, ps2, pk2: sa_ev(c, d, ps, pk, 1))
            for k in range(1, NIT + 1):
                def ph_pe(c, d, csl, ps, pk, ps2, pk2, k=k):
                    sb_pe(c, d, ps, pk, k)
                    if k < NIT:
                        sa_pe(c, d, ps, pk, k + 1)
                each(ph_pe)

                def ph_ev(c, d, csl, ps, pk, ps2, pk2, k=k):
                    sb_ev(c, d, ps, pk, k)
                    if k < NIT:
                        sa_ev(c, d, ps, pk, k + 1)
                each(ph_ev)
            PTF = PB[NIT % 2]

            def sf(c, d, csl, ps, pk, ps2, pk2):
                PT = d[PTF]

                def f(e):
                    e.matmul(ps[:, 0:128], lhsT=PT, rhs=d["v"], start=True, stop=True)
                    return e.matmul(ps[:, 128:256], lhsT=d["kbg"], rhs=PT, start=True, stop=True)
                P.pe(f, reads=[K(c, PTF), K(c, "v"), K(c, "kbg")], writes=[pk])
                P.act(lambda e: e.activation(out=d["u"], in_=ps[:, 0:128], func=AF.Identity), writes=[pk, K(c, "u")])
                P.dve(lambda e: e.tensor_copy(out=d["wT"], in_=ps[:, 128:256]), writes=[pk, K(c, "wT")])
            each(sf)
            iP1 = len(P.ops)
            if tb + 1 < nblk:
                emit_ip(tb + 1)
                P.interleave(iP0, iP1, len(P.ops))
            iA = len(P.ops)
            pa, pb, pt_ = PS[5][:, 0:256], PS[5][:, 256:512], PS[6]
            for c, cb in cl:
                d = ch[c]
                csl = slice(cb * 128, (cb + 1) * 128)
                sm = d["sm"]
                sq_ = seqt[c % 2]

                def f1(e, d=d, csl=csl, qn=qn):
                    e.matmul(pa[:, 0:128], lhsT=d["wT"], rhs=Sst, start=True, stop=True)
                    return e.matmul(pa[:, 128:256], lhsT=qn[:, csl], rhs=Sst, start=True, stop=True)
                P.pe(f1, reads=[K(c, "wT"), "S", qk_], writes=[("ps", 5)])
                P.dve(lambda e, d=d, sq_=sq_: e.tensor_tensor(out=sq_["vnew"], in0=d["u"], in1=pa[:, 0:128], op=ALU.subtract),
                      reads=[K(c, "u")], writes=[("ps", 5), ("vnew", c % 2)])
                P.dve(lambda e, sm=sm, sq_=sq_: e.tensor_scalar(out=sq_["o1s"], in0=pa[:, 128:256], scalar1=sm[:, 4:5],
                                                                scalar2=None, op0=ALU.mult),
                      reads=[K(c, "sm")], writes=[("ps", 5), ("o1s", c % 2)])

                def f2(e, d=d, sq_=sq_):
                    e.matmul(pb[:, 0:128], lhsT=d["qkT"], rhs=sq_["vnew"], start=True, stop=True)
                    return e.matmul(pb[:, 128:256], lhsT=d["kdec"], rhs=sq_["vnew"], start=True, stop=True)
                P.pe(f2, reads=[K(c, "qkT"), ("vnew", c % 2), K(c, "kdec")], writes=[("ps", 5)])
                P.dve(lambda e, sm=sm: e.scalar_tensor_tensor(out=Sst, in0=Sst, scalar=sm[:, 7:8], in1=pb[:, 128:256],
                                                              op0=ALU.mult, op1=ALU.add),
                      reads=[K(c, "sm"), "S"], writes=[("ps", 5), "S"])
                P.dve(lambda e, d=d, sq_=sq_: e.tensor_tensor(out=d["o"], in0=sq_["o1s"], in1=pb[:, 0:128], op=ALU.add),
                      reads=[("o1s", c % 2)], writes=[("ps", 5), K(c, "o")])
            def each1(fn):
                for c, cb in cl:
                    fn(c, ch[c])

            def e1(c, d):
                sm = d["sm"]
                P.dve(lambda e: e.memset(sm[:, 8:9], 0.0), reads=[K(c, "sm")], writes=[K(c, "sm")])
                P.act(lambda e: e.activation(out=d["kn"], in_=d["o"], func=AF.Square, accum_out=sm[:, 8:9]),
                      reads=[K(c, "o")], writes=[K(c, "kn"), K(c, "sm")])
            each1(e1)

            def e2(c, d):
                sm = d["sm"]
                P.dve(lambda e: e.tensor_scalar(out=sm[:, 8:9], in0=sm[:, 8:9], scalar1=1.0 / 128, scalar2=RMS_EPS,
                                                op0=ALU.mult, op1=ALU.add), reads=[K(c, "sm")], writes=[K(c, "sm")])
            each1(e2)

            def e3(c, d):
                sm = d["sm"]
                P.act(lambda e: e.activation(out=sm[:, 8:9], in_=sm[:, 8:9], func=AF.Sqrt), reads=[K(c, "sm")],
                      writes=[K(c, "sm")])
            each1(e3)

            def e4(c, d):
                sm = d["sm"]
                P.dve(lambda e: e.reciprocal(out=sm[:, 8:9], in_=sm[:, 8:9]), reads=[K(c, "sm")], writes=[K(c, "sm")])
            each1(e4)

            def e5(c, d):
                sm = d["sm"]
                P.act(lambda e: e.activation(out=d["o"], in_=d["o"], func=AF.Identity, scale=sm[:, 8:9]),
                      reads=[K(c, "sm"), K(c, "o")], writes=[K(c, "o")])
            each1(e5)

            def e6(c, d):
                P.pe(lambda e: e.transpose(out=pt_[:, c * 128:(c + 1) * 128], in_=d["o"], identity=ident),
                     reads=[K(c, "o"), "par"], writes=[("ps", 6)])
            each1(e6)
            P.dve(lambda e, zs=zs: e.scalar_tensor_tensor(out=dst, in0=pt_[:, :], scalar=normw, in1=zs, op0=ALU.mult,
                                                          op1=ALU.mult), reads=[zk, "par"], writes=[("ps", 6), "dst"])
            P.dma("sp", dn_d[:, tsl], dst, reads=["dst"], writes=[("out", "dn", tb)])
            iB = len(P.ops)
            nj = 4 * tb + 4
            p_s = (PS[0], PS[1])
            p_o = (PS[2], PS[3])
            k_s = (("ps", 0), ("ps", 1))
            k_o = (("ps", 2), ("ps", 3))
            PL = (PS[4], PS[7])
            KL = (("ps", 4), ("ps", 7))
            def att_front(j):
                a = j - 4 * tb
                q0 = max(a, 0) * 128
                jsl = slice(j * 128, (j + 1) * 128)
                qsl = slice(tb * 512 + q0, tb * 512 + 512)
                for m in range(2):
                    rows = slice(64 * m, 64 * m + 64)
                    pm_ = Pm[m][j % 2]
                    pmk = ("Pm", m, j % 2)
                    P.pe(lambda e, m=m, rows=rows, jsl=jsl, qsl=qsl, q0=q0: e.matmul(
                        p_s[m][:, q0:512], lhsT=kT[rows, jsl], rhs=qT[rows, qsl], start=True, stop=True),
                        reads=[("kT", j // 4), ("qT", tb)], writes=[k_s[m]])
                    band_lo = q0 if a >= -1 else 512
                    band_hi = min(512, (a + 2) * 128) if a >= -1 else 512
                    if a >= -1:
                        bc0 = 0 if a >= 0 else 128
                        wdt = band_hi - band_lo
                        P.dve(lambda e, m=m, band_lo=band_lo, band_hi=band_hi, bc0=bc0, wdt=wdt: e.scalar_tensor_tensor(
                            out=tb_[m][:, 0:wdt], in0=p_s[m][:, band_lo:band_hi], scalar=0.125,
                            in1=biasT[:, bc0:bc0 + wdt], op0=ALU.mult, op1=ALU.add),
                            reads=["par"], writes=[k_s[m], ("tb", m)])
                        P.act(lambda e, m=m, band_lo=band_lo, band_hi=band_hi, wdt=wdt, pm_=pm_: e.activation(
                            out=pm_[:, band_lo:band_hi], in_=tb_[m][:, 0:wdt], func=AF.Exp),
                            reads=[("tb", m)], writes=[pmk])
                    if band_hi < 512 or a < -1:
                        lo = band_hi if a >= -1 else 0
                        P.act(lambda e, m=m, lo=lo, pm_=pm_: e.activation(out=pm_[:, lo:512], in_=p_s[m][:, lo:512],
                                                                        func=AF.Exp, scale=0.125, bias=b31),
                              reads=["par"], writes=[k_s[m], pmk])

            def att_back(j):
                a = j - 4 * tb
                q0 = max(a, 0) * 128
                for m in range(2):
                    pm_ = Pm[m][j % 2]
                    pmk = ("Pm", m, j % 2)
                    P.pe(lambda e, m=m, j=j, q0=q0, pm_=pm_: e.matmul(
                        p_o[m][:, q0:512], lhsT=Vtok[:, j, :], rhs=pm_[:, q0:512], start=(j == 0), stop=False),
                        reads=[("V", j // 4), pmk], writes=[k_o[m]])
                    P.pe(lambda e, m=m, j=j, q0=q0, pm_=pm_: e.matmul(
                        PL[m][:, q0:512], lhsT=onesb, rhs=pm_[:, q0:512], start=(j == 0), stop=False),
                        reads=["cst2", pmk], writes=[KL[m]])
            for step in range(nj + 1):
                if step < nj:
                    att_front(step)
                if step >= 1:
                    att_back(step - 1)
            for m in range(2):
                P.pe(lambda e, m=m: e.matmul(p_o[m][:, :], lhsT=zerob, rhs=zero512, start=False, stop=True),
                     reads=["cst2"], writes=[k_o[m]])
                P.pe(lambda e, m=m: e.matmul(PL[m][:, :], lhsT=zerob, rhs=zero512, start=False, stop=True),
                     reads=["cst2"], writes=[KL[m]])
            for m, t, r in ((0, at0, at2), (1, at1, at2)):
                P.dve(lambda e, m=m, r=r: e.reciprocal(out=r, in_=PL[m][:, :]), writes=[KL[m], ("atr", 0)])
                P.dve(lambda e, m=m, t=t, r=r: e.tensor_tensor(out=t, in0=p_o[m][:, :], in1=r, op=ALU.mult),
                      reads=[("atr", 0)], writes=[k_o[m], ("at", m)])
            P.dve(lambda e: e.scalar_tensor_tensor(out=at0, in0=at1, scalar=neglam, in1=at0, op0=ALU.mult, op1=ALU.add),
                  reads=[("at", 1), "der"], writes=[("at", 0)])
            P.act(lambda e: e.activation(out=at2, in_=at0, func=AF.Square), reads=[("at", 0)], writes=[("atr", 0)])
            P.pe(lambda e: e.matmul(PS[0][:, :], lhsT=ones128, rhs=at2, start=True, stop=True),
                 reads=[("atr", 0), "cst2"], writes=[("ps", 0)])
            P.dve(lambda e: e.tensor_scalar(out=at2, in0=PS[0][:, :], scalar1=LN_EPS, scalar2=None, op0=ALU.add),
                  writes=[("ps", 0), ("atr", 0)])
            P.act(lambda e: e.activation(out=at2, in_=at2, func=AF.Sqrt), reads=[("atr", 0)], writes=[("atr", 0)])
            P.dve(lambda e: e.reciprocal(out=at2, in_=at2), reads=[("atr", 0)], writes=[("atr", 0)])
            P.dve(lambda e: e.tensor_tensor(out=at0, in0=at0, in1=at2, op=ALU.mult), reads=[("atr", 0), ("at", 0)],
                  writes=[("at", 0)])
            P.act(lambda e: e.activation(out=ost, in_=at0, func=AF.Identity, scale=subw), reads=[("at", 0), "der"],
                  writes=["ost"])
            P.dma("sp", attn_d[:, tsl], ost, reads=["ost"], writes=[("out", "at", tb)])
            iC = len(P.ops)
            P.interleave(iA, iB, iC)
        outs = [("out", n, tb) for n in ("dn", "at") for tb in range(nblk)]
        P.add("sp", None, reads=outs)
        P.emit()
    return nc


def t5_bucket_np(rel):
    n = np.maximum(rel, 0)
    max_exact = 16
    nf = np.maximum(n, max_exact).astype(np.float32)
    large = max_exact + (np.log(nf / np.float32(max_exact)) / np.float32(math.log(128 / max_exact))
                         * np.float32(32 - max_exact)).astype(np.int32)
    large = np.minimum(large, 31)
    return np.where(n < max_exact, n, large)


def l2_inputs(h, hT_all, w_in, conv_w, a_log, dt_bias, dn_norm_w, diff_lambda, subln_w, rel_bias):
    cols = np.concatenate([np.arange(o + h * 128, o + (h + 1) * 128) for o in range(0, 7 * 1024, 1024)]
                          + [np.array([7168 + h, 7176 + h])])
    wq = np.ascontiguousarray(w_in[:, cols].reshape(KC, 128, WQC).transpose(1, 0, 2))
    cw = np.zeros((128, 12), np.float32)
    for w in range(3):
        cw[:, w * 4:(w + 1) * 4] = conv_w[:, w * 1024 + h * 128:w * 1024 + (h + 1) * 128].T
    sc = np.zeros((128, 8), np.float32)
    sc[:, 0] = a_log[h]
    sc[:, 1] = dt_bias[h]
    sc[:, 2] = rel_bias[31, h]
    sc[:, 3] = dn_norm_w
    sc[:, 4] = subln_w
    lamp = np.ascontiguousarray(np.broadcast_to(diff_lambda.reshape(1, 256), (128, 256)))
    kk = np.arange(128)[:, None]
    qq = np.arange(256)[None, :]
    rel = qq - kk
    bt = np.where(rel >= 0, rel_bias[t5_bucket_np(rel), h], np.float32(-30000.0)).astype(np.float32)
    cst = np.zeros((128, 384), np.float32)
    cst[:, 0:128] = np.eye(128, dtype=np.float32)
    a = np.arange(128)
    cst[:, 128:256] = (a[:, None] <= a[None, :])
    cst[:, 256:384] = (a[:, None] > a[None, :])
    return {"hT": hT_all, "wq": wq, "convw": cw, "scal": sc, "lamp": lamp, "biasT": bt, "consts": cst}


def build_l3():
    nc = bass.Bass("TRN2", target_bir_lowering=False)
    ada_d = nc.dram_tensor("ada", [128, 9 * KC], F32, kind="ExternalInput").ap()
    aT_d = nc.dram_tensor("aT", [1024, TOK], BF16, kind="ExternalInput").ap()
    bT_d = nc.dram_tensor("bT", [1024, TOK], BF16, kind="ExternalInput").ap()
    h1_d = nc.dram_tensor("h1T", [D, TOK], BF16, kind="ExternalInput").ap()
    x1_d = nc.dram_tensor("x1T", [D, TOK], F32, kind="ExternalInput").ap()
    wg_d = nc.dram_tensor("wg", [KC, 128, KC, 256], F32, kind="ExternalInput").ap()
    wab_d = nc.dram_tensor("wab", [KC, 128, KC, 128], F32, kind="ExternalInput").ap()
    wo_d = nc.dram_tensor("wo", [KC, 128, KC, 128], F32, kind="ExternalInput").ap()
    win = nc.dram_tensor("win", [FFC // 2, 128, KC, 512], F32, kind="ExternalInput").ap()
    wout = nc.dram_tensor("wout", [KC, 128, FFC, 128], F32, kind="ExternalInput").ap()
    lngb = nc.dram_tensor("lngb", [128, 4 * KC], F32, kind="ExternalInput").ap()
    x2s = nc.dram_tensor("x2s", [D, TOK], F32).ap()
    outT = nc.dram_tensor("outT", [D, TOK], F32, kind="ExternalOutput").ap()
    import contextlib
    with contextlib.ExitStack() as st:
        NB = 188 * 1024
        arena_t = st.enter_context(nc.sbuf_tensor("arena", [128, NB // 4], F32))
        PS = [st.enter_context(nc.psum_tensor("ps%d" % i, [128, 512], F32)) for i in range(8)]
        A = Arena(arena_t, NB)
        P = Prog(nc)
        wbuf_off = A.off
        wbuf = [A.alloc([KC, 512], BF16) for _ in range(2)]
        wobuf_off = A.off
        wobuf = [A.alloc([FFC, 128], BF16) for _ in range(2)]
        hT_off = A.off
        h2T = A.alloc([KC, TOK], BF16)
        vT3 = A.alloc([KC, 512], F32, off=hT_off)
        act_off = A.off
        actT = A.alloc([FFC, TOK], BF16)
        ones_mat = A.alloc([128], F32)
        ln_sb = A.alloc([4 * KC], F32)
        ada = A.alloc([9 * KC], F32)
        der = A.alloc([2 * KC], F32)
        bscr = A.alloc([1], F32)
        sg = [A.alloc([512], F32) for _ in range(2)]
        h1T = A.alloc([KC, TOK], BF16, off=act_off)
        aT = A.alloc([8, TOK], BF16, off=act_off + 32 * 1024)
        bT = A.alloc([8, TOK], BF16, off=act_off + 48 * 1024)
        wgb = [A.alloc([KC, 256], BF16, off=act_off + 64 * 1024 + i * 8192) for i in range(2)]
        wabb = [A.alloc([KC, 128], BF16, off=act_off + 80 * 1024 + i * 4096) for i in range(2)]
        mergedT = A.alloc([KC, TOK], BF16, off=wbuf_off)
        vT2 = A.alloc([KC, 512], F32, off=act_off)
        P.dve(lambda e: e.memset(ones_mat, 1.0 / D), writes=["ones"])
        P.dma("sp", ln_sb, lngb, writes=["1ln", "2ln"])
        P.dma("sp", ada, ada_d, writes=["ada"])
        P.dve(lambda e: e.tensor_scalar(out=der[:, 0:KC], in0=ada[:, 7 * KC:8 * KC], scalar1=1.0, scalar2=None, op0=ALU.add),
              reads=["ada"], writes=["der"])
        P.dve(lambda e: e.tensor_scalar(out=der[:, KC:2 * KC], in0=ada[:, 8 * KC:9 * KC], scalar1=0.5, scalar2=None,
                                        op0=ALU.mult), reads=["ada"], writes=["der"])
        P.dma("sp", h1T, h1_d.rearrange("(kc p) t -> p kc t", p=128), writes=["h1T"])
        P.dma("sp", aT, aT_d.rearrange("(kc p) t -> p kc t", p=128), writes=["aT"])
        P.dma("sp", bT, bT_d.rearrange("(kc p) t -> p kc t", p=128), writes=["bT"])
        s1o = wobuf_off
        sga = [A.alloc([512], F32, off=s1o + i * 2048) for i in range(2)]
        m1 = [A.alloc([512], F32, off=s1o + 4096 + i * 2048) for i in range(2)]
        n = 0
        for fb in range(KC):
            slot = fb % 2
            P.dma("pool", wgb[slot], wg_d[fb], writes=[("wgb", slot)])
            P.dma("pool", wabb[slot], wab_d[fb], writes=[("wabb", slot)])
            for th in range(TOK // 512):
                tsl = slice(th * 512, (th + 1) * 512)
                i = n % 2
                n += 1
                pga, pgb, pya, pyb = PS[i], PS[2 + i], PS[4 + i], PS[6 + i]

                def mm(e, slot=slot, tsl=tsl, pga=pga, pgb=pgb, pya=pya, pyb=pyb):
                    inst = None
                    for kc in range(KC):
                        inst = e.matmul(pga[:, :], lhsT=wgb[slot][:, kc, 0:128], rhs=h1T[:, kc, tsl],
                                        start=(kc == 0), stop=(kc == KC - 1))
                    for kc in range(KC):
                        inst = e.matmul(pgb[:, :], lhsT=wgb[slot][:, kc, 128:256], rhs=h1T[:, kc, tsl],
                                        start=(kc == 0), stop=(kc == KC - 1))
                    for kc in range(8):
                        inst = e.matmul(pya[:, :], lhsT=wabb[slot][:, kc, :], rhs=aT[:, kc, tsl],
                                        start=(kc == 0), stop=(kc == 7))
                    for kc in range(8):
                        inst = e.matmul(pyb[:, :], lhsT=wabb[slot][:, 8 + kc, :], rhs=bT[:, kc, tsl],
                                        start=(kc == 0), stop=(kc == 7))
                    return inst
                P.pe(mm, reads=[("wgb", slot), ("wabb", slot), "h1T", "aT", "bT"],
                     writes=[("ps", i), ("ps", 2 + i), ("ps", 4 + i), ("ps", 6 + i)])
                P.act(lambda e, i=i, pga=pga: e.activation(out=sga[i], in_=pga[:, :], func=AF.Sigmoid),
                      writes=[("ps", i), ("sga", i)])
                P.dve(lambda e, i=i, pya=pya: e.tensor_tensor(out=m1[i], in0=sga[i], in1=pya[:, :], op=ALU.mult),
                      reads=[("sga", i)], writes=[("ps", 4 + i), ("m1", i)])
                P.act(lambda e, i=i, pgb=pgb: e.activation(out=sga[i], in_=pgb[:, :], func=AF.Sigmoid),
                      reads=[("m1", i)], writes=[("ps", 2 + i), ("sga", i)])
                P.dve(lambda e, i=i, pyb=pyb: e.tensor_tensor(out=sga[i], in0=sga[i], in1=pyb[:, :], op=ALU.mult),
                      reads=[("sga", i)], writes=[("ps", 6 + i), ("sga", i)])
                P.dve(lambda e, i=i, fb=fb, tsl=tsl: e.tensor_tensor(out=mergedT[:, fb, tsl], in0=m1[i], in1=sga[i],
                                                                      op=ALU.add),
                      reads=[("sga", i), ("m1", i)], writes=[("mg", fb, th)])
        P.barrier(bscr)
        hout = [A.alloc([512], F32, off=act_off + 64 * 1024 + i * 2048) for i in range(2)]
        cnt = [0]

        def consume2(kc, th, xn, key):
            tsl = slice(th * 512, (th + 1) * 512)
            P.dma("sp", x2s[kc * 128:(kc + 1) * 128, tsl], xn, reads=[key], writes=[("2xres", kc, th)])
            P.dve(lambda e, kc=kc, xn=xn, tsl=tsl: e.tensor_scalar(
                out=h2T[:, kc, tsl], in0=xn, scalar1=der[:, kc:kc + 1], scalar2=ada[:, 6 * KC + kc:6 * KC + kc + 1],
                op0=ALU.mult, op1=ALU.add), reads=[key, "der", "ada"], writes=["2hT"])
        mg_keys = lambda th: [("mg", fb, th) for fb in range(KC)]
        wo2 = [A.alloc([KC, 128], BF16, off=act_off + 80 * 1024 + i * 4096) for i in range(2)]
        emit_proj_ln(P, A, PS, mergedT, KC, mg_keys, vT2, [], wo_d, wo2, x1_d, ada[:, 5 * KC:6 * KC], ["ada"],
                     ln_sb[:, 0:KC], ln_sb[:, KC:2 * KC], ones_mat, consume2, "1", wobuf_off, [])
        P.barrier(bscr)
        def consume3(kc, th, xn, key):
            tsl = slice(th * 512, (th + 1) * 512)
            P.dma("sp", outT[kc * 128:(kc + 1) * 128, tsl], xn, reads=[key], writes=[("out", kc, th)])
        emit_ffn_in(P, A, PS, h2T, actT, win, wbuf, sg, "2")
        act_keys = lambda th: [("2actT", ffb, th * 512) for ffb in range(FFC)]
        emit_proj_ln(P, A, PS, actT, FFC, act_keys, vT3, ["2hT"], wout, wobuf, x2s, der[:, KC:2 * KC], ["der"],
                     ln_sb[:, 2 * KC:3 * KC], ln_sb[:, 3 * KC:4 * KC], ones_mat, consume3, "2", wbuf_off,
                     [("wbuf", 0), ("wbuf", 1)])
        outs = [("out", kc, th) for kc in range(KC) for th in range(TOK // 512)]
        P.add("sp", None, reads=outs)
        P.emit()
    return nc


_CACHE = {}


def _prog(name, fn):
    if name not in _CACHE:
        _CACHE[name] = fn()
    return _CACHE[name]


def kernel(x, c, w_ada, b_ada, ln_g, ln_b, w_ffn_in, w_ffn_out, w_in, conv_w, dn_a_log, dn_dt_bias, dn_norm_w,
           diff_lambda, diff_subln_w, rel_bias, w_branch_a, w_branch_b, w_out):
    f32 = lambda a: np.ascontiguousarray(np.asarray(a, dtype=np.float32))
    x, c, w_ada, b_ada, ln_g, ln_b = f32(x), f32(c), f32(w_ada), f32(b_ada), f32(ln_g), f32(ln_b)
    w_ffn_in, w_ffn_out, w_in, conv_w = f32(w_ffn_in), f32(w_ffn_out), f32(w_in), f32(conv_w)
    cores = list(range(NCORES))
    cfm = fm(c[0])
    wt = tile_w(w_ada[0], 128)
    bfm = fm(b_ada[0])
    res = run_bass_kernel_spmd(_prog("l0", build_l0), [
        {"c": cfm, "wada": np.ascontiguousarray(wt[18 * r:18 * (r + 1)]), "bada": np.ascontiguousarray(bfm[:, 18 * r:18 * (r + 1)])}
        for r in cores], core_ids=cores)
    ada = np.ascontiguousarray(np.concatenate([np.asarray(res.results[r]["ada"]) for r in cores], axis=1))
    del wt
    win0 = ffn_in_tiles(w_ffn_in[0, 0])
    wout0 = tile_w(w_ffn_out[0, 0], 128)
    lngb0 = np.concatenate([fm(ln_g[0, 0]), fm(ln_b[0, 0])], axis=1)
    xs = x[0]
    res = run_bass_kernel_spmd(_prog("l1", build_l1), [
        {"xT": np.ascontiguousarray(xs[r * TOK:(r + 1) * TOK].T), "ada": ada, "win": win0, "wout": wout0, "lngb": lngb0}
        for r in cores], core_ids=cores)
    x1T = [np.asarray(res.results[r]["x1T"]) for r in cores]
    h1T = [np.asarray(res.results[r]["h1T"]) for r in cores]
    del win0, wout0
    hT_all = np.ascontiguousarray(np.concatenate(h1T, axis=1))
    res = run_bass_kernel_spmd(_prog("l2", build_l2), [
        l2_inputs(h, hT_all, w_in[0], conv_w[0], f32(dn_a_log)[0], f32(dn_dt_bias)[0], f32(dn_norm_w)[0],
                  f32(diff_lambda)[0], f32(diff_subln_w)[0], f32(rel_bias)) for h in cores], core_ids=cores)
    aT_all = np.concatenate([np.asarray(res.results[h]["attnT"]) for h in cores], axis=0)
    bT_all = np.concatenate([np.asarray(res.results[h]["dnT"]) for h in cores], axis=0)
    del hT_all
    ga = w_in[0][:, 7184:7184 + D].reshape(KC, 128, KC, 128)
    gb = w_in[0][:, 7184 + D:7184 + 2 * D].reshape(KC, 128, KC, 128)
    wg = np.ascontiguousarray(np.concatenate([ga, gb], axis=3).transpose(2, 1, 0, 3))
    wab = tile_w(np.concatenate([f32(w_branch_a)[0], f32(w_branch_b)[0]], axis=0), 128)
    wo = tile_w(f32(w_out)[0], 128)
    win1 = ffn_in_tiles(w_ffn_in[0, 1])
    wout1 = tile_w(w_ffn_out[0, 1], 128)
    lngb1 = np.concatenate([fm(ln_g[0, 1]), fm(ln_b[0, 1]), fm(ln_g[0, 2]), fm(ln_b[0, 2])], axis=1)
    res = run_bass_kernel_spmd(_prog("l3", build_l3), [
        {"ada": ada, "aT": np.ascontiguousarray(aT_all[:, r * TOK:(r + 1) * TOK]),
         "bT": np.ascontiguousarray(bT_all[:, r * TOK:(r + 1) * TOK]), "h1T": h1T[r], "x1T": x1T[r],
         "wg": wg, "wab": wab, "wo": wo, "win": win1, "wout": wout1, "lngb": lngb1} for r in cores], core_ids=cores)
    out = np.concatenate([np.asarray(res.results[r]["outT"]).T for r in cores], axis=0)
    return np.ascontiguousarray(out.reshape(1, S, D).astype(np.float32))
```

```python
import math
import numpy as np
import ml_dtypes
import concourse.bass as bass
import concourse.mybir as mybir
from concourse.bass_utils import run_bass_kernel_spmd

F32 = mybir.dt.float32
BF16 = mybir.dt.bfloat16
AF = mybir.ActivationFunctionType
ALU = mybir.AluOpType
AX = mybir.AxisListType

NCORES = 8
D = 2048
S = 8192
TOK = S // NCORES
KC = D // 128
DFF = 5632
FFC = DFF // 128
ALPHA = 2.0 ** 0.25
LN_EPS = 1e-5
RMS_EPS = 1e-6
NDMA_SEM = 12


class Op:
    __slots__ = ("eng", "fn", "reads", "writes", "dma", "deps", "inc", "val", "sem_i", "barrier", "cost")

    def __init__(self, eng, fn, reads, writes, dma):
        self.eng, self.fn, self.reads, self.writes, self.dma = eng, fn, tuple(reads), tuple(writes), dma
        self.deps = []
        self.inc = False
        self.val = 0
        self.sem_i = 0
        self.barrier = False
        self.cost = None


class Prog:
    ENGS = ("pe", "act", "dve", "pool", "sp")

    def __init__(self, nc):
        self.nc = nc
        self.ops = []

    def add(self, eng, fn, reads=(), writes=(), dma=False, cost=None):
        op = Op(eng, fn, reads, writes, dma)
        op.cost = cost
        self.ops.append(op)

    def pe(self, fn, reads=(), writes=(), cost=None):
        self.add("pe", fn, reads, writes, cost=cost)

    def act(self, fn, reads=(), writes=()):
        self.add("act", fn, reads, writes)

    def dve(self, fn, reads=(), writes=()):
        self.add("dve", fn, reads, writes)

    def pool(self, fn, reads=(), writes=()):
        self.add("pool", fn, reads, writes)

    def dma(self, q, out, in_, reads=(), writes=()):
        self.add(q, lambda e: e.dma_start(out=out, in_=in_), reads, writes, dma=True)

    COST = {"pe": 0.3, "act": 0.45, "dve": 0.4, "pool": 0.5, "sp": 0.1}

    def interleave(self, i0, i1, i2):
        streams = [self.ops[i0:i1], self.ops[i1:i2]]
        HOP = 0.9
        deps = []
        for ops in streams:
            last_w, readers, dl = {}, {}, []
            for i, op in enumerate(ops):
                d = set()
                for r in op.reads:
                    if r in last_w:
                        d.add(last_w[r])
                for w in op.writes:
                    if w in last_w:
                        d.add(last_w[w])
                    d.update(readers.get(w, ()))
                d.discard(i)
                dl.append(d)
                for r in op.reads:
                    readers.setdefault(r, []).append(i)
                for w in op.writes:
                    last_w[w] = i
                    readers[w] = []
            deps.append(dl)
        fin = [[0.0] * len(st) for st in streams]
        ptr = [0, 0]
        free = {e: 0.0 for e in self.ENGS}
        out = []

        def start_time(si):
            i = ptr[si]
            op = streams[si][i]
            t = free[op.eng]
            for j in deps[si][i]:
                lat = HOP if streams[si][j].eng != op.eng or streams[si][j].dma else 0.15
                t = max(t, fin[si][j] + lat)
            return t
        while ptr[0] < len(streams[0]) or ptr[1] < len(streams[1]):
            cands = [si for si in (0, 1) if ptr[si] < len(streams[si])]
            best = min(cands, key=lambda si: (start_time(si), si))
            i = ptr[best]
            op = streams[best][i]
            t0 = start_time(best)
            cost = getattr(op, "cost", None) or self.COST[op.eng]
            if op.dma:
                free[op.eng] = t0 + 0.1
                fin[best][i] = t0 + 2.0
            else:
                free[op.eng] = t0 + cost
                fin[best][i] = t0 + cost
            out.append(op)
            ptr[best] += 1
        self.ops[i0:i2] = out

    def barrier(self, scratch):
        op = Op("dve", lambda e: e.memset(scratch, 0.0), (), (), False)
        op.barrier = True
        self.ops.append(op)

    def analyse(self):
        last_w = {}
        readers = {}
        ops = self.ops
        last_on = {}
        dma_hist = {e: [] for e in self.ENGS}
        pending = {}
        for i, op in enumerate(ops):
            if op.barrier:
                for e2, j in last_on.items():
                    if e2 != "dve" or True:
                        if j != i:
                            op.deps.append(j)
                            ops[j].inc = True
                for e2 in self.ENGS:
                    for j in dma_hist[e2][-NDMA_SEM:]:
                        op.deps.append(j)
                op.inc = True
                for e2 in self.ENGS:
                    pending[e2] = i
                pending.pop("dve")
                last_w.clear()
                readers.clear()
                last_on["dve"] = i
                continue
            if op.eng in pending:
                op.deps.append(pending.pop(op.eng))
            if op.dma:
                dma_hist[op.eng].append(i)
            else:
                last_on[op.eng] = i
            deps = {}
            for r in op.reads:
                j = last_w.get(r)
                if j is not None:
                    deps[j] = "raw"
            for w in op.writes:
                j = last_w.get(w)
                if j is not None and j not in deps:
                    deps[j] = "waw"
                for j in readers.get(w, ()):
                    if j not in deps:
                        deps[j] = "war"
            for j, kind in deps.items():
                if j == i:
                    continue
                a = ops[j]
                if a.dma:
                    op.deps.append(j)
                elif a.eng == op.eng:
                    if op.eng == "pe" and not op.dma:
                        continue
                    op.deps.append(j)
                    a.inc = True
                else:
                    op.deps.append(j)
                    a.inc = True
            for r in op.reads:
                readers.setdefault(r, []).append(i)
            for w in op.writes:
                last_w[w] = i
                readers[w] = []
        cnt = {e: 0 for e in self.ENGS}
        dcnt = {e: 0 for e in self.ENGS}
        self.dma_prev = {}
        hist = {e: [] for e in self.ENGS}
        for i, op in enumerate(ops):
            if op.dma:
                n = dcnt[op.eng]
                dcnt[op.eng] += 1
                op.sem_i = n % NDMA_SEM
                op.val = 16 * (n // NDMA_SEM + 1)
                hist[op.eng].append(i)
                if n >= NDMA_SEM:
                    op.deps.append(hist[op.eng][n - NDMA_SEM])
            elif op.inc:
                cnt[op.eng] += 1
                op.val = cnt[op.eng]

    def emit(self):
        self.analyse()
        nc = self.nc
        ops = self.ops
        import contextlib
        with contextlib.ExitStack() as st:
            esem = {e: st.enter_context(nc.semaphore("c_" + e)) for e in self.ENGS}
            dsem = {e: [st.enter_context(nc.semaphore("d_%s%d" % (e, k))) for k in range(NDMA_SEM)]
                    for e in ("sp", "pool", "act")}
            block = st.enter_context(nc.Block())

            def run(engname, eng):
                waited = {}
                for op in ops:
                    if op.eng != engname:
                        continue
                    for j in op.deps:
                        a = ops[j]
                        if a.dma:
                            sem, key = dsem[a.eng][a.sem_i], (a.eng, a.sem_i)
                        else:
                            sem, key = esem[a.eng], a.eng
                        if waited.get(key, 0) >= a.val:
                            continue
                        waited[key] = a.val
                        eng.wait_ge(sem, a.val)
                    if op.fn is None:
                        continue
                    inst = op.fn(eng)
                    if op.dma:
                        inst.then_inc(dsem[op.eng][op.sem_i], 16)
                    elif op.inc:
                        inst.then_inc(esem[op.eng], 1)

            @block.tensor
            def _(e):
                run("pe", e)

            @block.scalar
            def _(e):
                run("act", e)

            @block.vector
            def _(e):
                run("dve", e)

            @block.gpsimd
            def _(e):
                run("pool", e)

            @block.sync
            def _(e):
                run("sp", e)


class Arena:
    def __init__(self, t, nbytes):
        self.t = t
        self.nbytes = nbytes
        self.off = 0

    def alloc(self, shape_free, dtype, off=None):
        n = int(np.prod(shape_free))
        esz = 2 if dtype == BF16 else 4
        nb = (n * esz + 31) // 32 * 32
        if off is None:
            off = self.off
            self.off += nb
            assert self.off <= self.nbytes, ("SBUF arena overflow", self.off, self.nbytes)
        v = self.t[:, off // 4:(off + nb) // 4]
        if dtype == BF16:
            v = v.bitcast(BF16)
        v = v[:, 0:n]
        if len(shape_free) == 2:
            v = v.rearrange("p (a b) -> p a b", a=shape_free[0])
        elif len(shape_free) == 3:
            v = v.rearrange("p (a b c) -> p a b c", a=shape_free[0], b=shape_free[1])
        return v


def build_l0():
    nc = bass.Bass("TRN2", target_bir_lowering=False)
    c_d = nc.dram_tensor("c", [128, KC], F32, kind="ExternalInput").ap()
    wada = nc.dram_tensor("wada", [18, 128, KC, 128], F32, kind="ExternalInput").ap()
    bada_d = nc.dram_tensor("bada", [128, 18], F32, kind="ExternalInput").ap()
    out_d = nc.dram_tensor("ada", [128, 18], F32, kind="ExternalOutput").ap()
    import contextlib
    with contextlib.ExitStack() as st:
        NB = 64 * 1024
        arena_t = st.enter_context(nc.sbuf_tensor("arena", [128, NB // 4], F32))
        ps = st.enter_context(nc.psum_tensor("ps0", [128, 512], F32))
        A = Arena(arena_t, NB)
        P = Prog(nc)
        c_sb = A.alloc([KC], F32)
        c_bf = A.alloc([KC], BF16)
        ada = A.alloc([18], F32)
        bada = A.alloc([18], F32)
        wb = [A.alloc([KC, 128], BF16) for _ in range(4)]
        P.dma("sp", c_sb, c_d, writes=["c"])
        P.dma("sp", bada, bada_d, writes=["bada"])
        P.act(lambda e: e.activation(out=c_bf, in_=c_sb, func=AF.Silu), reads=["c"], writes=["cbf"])
        for t in range(18):
            slot = t % 4
            P.dma("pool", wb[slot], wada[t], writes=[("wb", slot)])

            def mm(e, t=t, slot=slot):
                inst = None
                for kc in range(KC):
                    inst = e.matmul(ps[:, t:t + 1], lhsT=wb[slot][:, kc, :], rhs=c_bf[:, kc:kc + 1],
                                    start=(kc == 0), stop=(kc == KC - 1))
                return inst
            P.pe(mm, reads=[("wb", slot), "cbf"], writes=["ps"])
        P.dve(lambda e: e.tensor_tensor(out=ada, in0=ps[:, 0:18], in1=bada, op=ALU.add),
              reads=["bada"], writes=["ps", "ada"])
        P.dma("sp", out_d, ada, reads=["ada"], writes=["out"])
        P.add("sp", None, reads=["out"])
        P.emit()
    return nc


def emit_ffn_in(P, A, PS, hT, actT, win_dram, wbuf, sg, tag):
    NT = TOK // 512
    n_in_tiles = FFC // 2
    for t in range(n_in_tiles):
        slot = t % 2
        wb = wbuf[slot]
        P.dma("pool", wb, win_dram[t], writes=[("wbuf", slot)])
        for b2 in range(2):
            ffb = 2 * t + b2
            for th in range(NT):
                i = (ffb * NT + th) % 2
                pg, pu = PS[i], PS[2 + i]
                tsl = slice(th * 512, (th + 1) * 512)

                def mm(e, wb=wb, b2=b2, tsl=tsl, pg=pg, pu=pu):
                    inst = None
                    for (pp, off) in ((pg, b2 * 256), (pu, b2 * 256 + 128)):
                        for kc in range(KC):
                            inst = e.matmul(pp[:, :], lhsT=wb[:, kc, off:off + 128], rhs=hT[:, kc, tsl],
                                            start=(kc == 0), stop=(kc == KC - 1))
                    return inst
                P.pe(mm, reads=[("wbuf", slot), tag + "hT"], writes=[("ps", i), ("ps", 2 + i)])
                P.act(lambda e, pg=pg, i=i: e.activation(out=sg[i], in_=pg[:, :], func=AF.Silu),
                      writes=[("ps", i), (tag + "sg", i)])
                P.dve(lambda e, pu=pu, i=i, ffb=ffb, tsl=tsl: e.tensor_tensor(
                    out=actT[:, ffb, tsl], in0=sg[i], in1=pu[:, :], op=ALU.mult),
                    reads=[(tag + "sg", i)], writes=[("ps", 2 + i), (tag + "actT", ffb, tsl.start)])


def emit_proj_ln(P, A, PS, rhsT, nK, rhs_keys, vT, vkeys_extra, wout_dram, wobuf, x_res, gate_vec, gate_keys, lng, lnb,
                 ones_mat, consume, tag, small_off, alias_keys):
    NT = TOK // 512
    o = [small_off]

    def al(shape):
        v = A.alloc(shape, F32, off=o[0])
        o[0] += int(np.prod(shape)) * 4
        return v
    xs = [al([512]) for _ in range(2)]
    tmp = [al([512]) for _ in range(2)]
    sq = [al([512]) for _ in range(2)]
    mean_sb = al([512])
    rstd = al([512])
    t1 = [al([512]) for _ in range(2)]
    small_keys = [(tag + k, i) for k in ("xs", "tmp", "sq", "t1") for i in range(2)] + [tag + "mean", tag + "rstd"]
    if alias_keys:
        P.dve(lambda e: e.memset(rstd[:, 0:1], 0.0), writes=list(alias_keys) + small_keys)
    n = [0]

    def proj_group(th, fb):
        tsl = slice(th * 512, (th + 1) * 512)
        if True:
            slot = n[0] % 2
            n[0] += 1
            wo = wobuf[slot]
            P.dma("pool", wo, wout_dram[fb], writes=[("wobuf", slot)])
            P.dma("sp", xs[slot], x_res[fb * 128:(fb + 1) * 128, tsl], reads=[(tag + "xres", fb, th)], writes=[(tag + "xs", slot)])
            py = PS[4 + slot]

            def mm(e, wo=wo, tsl=tsl, py=py):
                inst = None
                for kc in range(nK):
                    inst = e.matmul(py[:, :], lhsT=wo[:, kc, :], rhs=rhsT[:, kc, tsl],
                                    start=(kc == 0), stop=(kc == nK - 1))
                return inst
            P.pe(mm, reads=[("wobuf", slot)] + rhs_keys(th), writes=[("ps", 4 + slot)])
            P.act(lambda e, py=py, slot=slot, fb=fb: e.activation(out=tmp[slot], in_=py[:, :], func=AF.Identity,
                                                                 scale=gate_vec[:, fb:fb + 1]),
                  reads=gate_keys, writes=[("ps", 4 + slot), (tag + "tmp", slot)])
            P.dve(lambda e, fb=fb, slot=slot: e.scalar_tensor_tensor(
                out=vT[:, fb, :], in0=xs[slot], scalar=ALPHA, in1=tmp[slot], op0=ALU.mult, op1=ALU.add),
                reads=[(tag + "tmp", slot), (tag + "xs", slot)], writes=[(tag + "v", fb)] + vkeys_extra)
    for fb in range(KC):
        proj_group(0, fb)
    for th in range(NT):
        pm, pq = PS[6], PS[7]
        for kc in range(KC):
            i = kc % 2
            P.act(lambda e, i=i, kc=kc: e.activation(out=sq[i], in_=vT[:, kc, :], func=AF.Square),
                  reads=[(tag + "v", kc)], writes=[(tag + "sq", i)])
            P.pe(lambda e, kc=kc, pm=pm: e.matmul(
                pm[:, :], lhsT=ones_mat, rhs=vT[:, kc, :], start=(kc == 0), stop=(kc == KC - 1)),
                reads=[(tag + "v", kc), "ones"], writes=[("ps", 6)])
            P.pe(lambda e, kc=kc, i=i, pq=pq: e.matmul(
                pq[:, :], lhsT=ones_mat, rhs=sq[i], start=(kc == 0), stop=(kc == KC - 1)),
                reads=[(tag + "sq", i), "ones"], writes=[("ps", 7)])
        P.act(lambda e, pm=pm: e.activation(out=mean_sb, in_=pm[:, :], func=AF.Identity),
              writes=[("ps", 6), tag + "mean"])
        P.dve(lambda e: e.tensor_tensor(out=rstd, in0=mean_sb, in1=mean_sb, op=ALU.mult),
              reads=[tag + "mean"], writes=[tag + "rstd"])
        P.dve(lambda e, pq=pq: e.tensor_tensor(out=rstd, in0=pq[:, :], in1=rstd, op=ALU.subtract),
              reads=[tag + "rstd"], writes=[("ps", 7), tag + "rstd"])
        P.dve(lambda e: e.tensor_scalar(out=rstd, in0=rstd, scalar1=LN_EPS, scalar2=None, op0=ALU.add),
              reads=[tag + "rstd"], writes=[tag + "rstd"])
        P.act(lambda e: e.activation(out=rstd, in_=rstd, func=AF.Sqrt), reads=[tag + "rstd"], writes=[tag + "rstd"])
        P.dve(lambda e: e.reciprocal(out=rstd, in_=rstd), reads=[tag + "rstd"], writes=[tag + "rstd"])
        for kc in range(KC):
            i = kc % 2
            P.dve(lambda e, i=i, kc=kc: e.tensor_tensor(out=t1[i], in0=vT[:, kc, :], in1=mean_sb, op=ALU.subtract),
                  reads=[(tag + "v", kc), tag + "mean"], writes=[(tag + "t1", i)])
            P.dve(lambda e, i=i: e.tensor_tensor(out=t1[i], in0=t1[i], in1=rstd, op=ALU.mult),
                  reads=[(tag + "t1", i), tag + "rstd"], writes=[(tag + "t1", i)])
            P.act(lambda e, i=i, kc=kc: e.activation(out=vT[:, kc, :], in_=t1[i], func=AF.Identity,
                                                    scale=lng[:, kc:kc + 1], bias=lnb[:, kc:kc + 1]),
                  reads=[(tag + "t1", i), tag + "ln"], writes=[(tag + "v", kc)])
            consume(kc, th, vT[:, kc, :], (tag + "v", kc))
            if th + 1 < NT:
                proj_group(th + 1, kc)


def build_l1():
    nc = bass.Bass("TRN2", target_bir_lowering=False)
    xT = nc.dram_tensor("xT", [D, TOK], F32, kind="ExternalInput").ap()
    ada_d = nc.dram_tensor("ada", [128, 9 * KC], F32, kind="ExternalInput").ap()
    win = nc.dram_tensor("win", [FFC // 2, 128, KC, 512], F32, kind="ExternalInput").ap()
    wout = nc.dram_tensor("wout", [KC, 128, FFC, 128], F32, kind="ExternalInput").ap()
    lngb = nc.dram_tensor("lngb", [128, 2 * KC], F32, kind="ExternalInput").ap()
    x1T = nc.dram_tensor("x1T", [D, TOK], F32, kind="ExternalOutput").ap()
    h1T = nc.dram_tensor("h1T", [D, TOK], BF16, kind="ExternalOutput").ap()
    import contextlib
    with contextlib.ExitStack() as st:
        NB = 188 * 1024
        arena_t = st.enter_context(nc.sbuf_tensor("arena", [128, NB // 4], F32))
        PS = [st.enter_context(nc.psum_tensor("ps%d" % i, [128, 512], F32)) for i in range(8)]
        A = Arena(arena_t, NB)
        P = Prog(nc)
        wbuf_off = A.off
        wbuf = [A.alloc([KC, 512], BF16) for _ in range(2)]
        wobuf = [A.alloc([FFC, 128], BF16) for _ in range(2)]
        hT_off = A.off
        hT = A.alloc([KC, TOK], BF16)
        vT = A.alloc([KC, 512], F32, off=hT_off)
        act_off = A.off
        actT = A.alloc([FFC, TOK], BF16)
        ones_mat = A.alloc([128], F32)
        ln_sb = A.alloc([2 * KC], F32)
        P.dve(lambda e: e.memset(ones_mat, 1.0 / D), writes=["ones"])
        P.dma("sp", ln_sb, lngb, writes=["0ln"])
        ada = A.alloc([9 * KC], F32)
        P.dma("sp", ada, ada_d, writes=["0ada"])
        der = A.alloc([3 * KC], F32)
        P.dve(lambda e: e.tensor_scalar(out=der[:, 0:KC], in0=ada[:, KC:2 * KC], scalar1=1.0, scalar2=None, op0=ALU.add),
              reads=["0ada"], writes=["0der"])
        P.dve(lambda e: e.tensor_scalar(out=der[:, KC:2 * KC], in0=ada[:, 2 * KC:3 * KC], scalar1=0.5, scalar2=None,
                                        op0=ALU.mult), reads=["0ada"], writes=["0der"])
        P.dve(lambda e: e.tensor_scalar(out=der[:, 2 * KC:3 * KC], in0=ada[:, 4 * KC:5 * KC], scalar1=1.0, scalar2=None,
                                        op0=ALU.add), reads=["0ada"], writes=["0der"])
        xst = A.alloc([KC, TOK], F32, off=act_off)
        xT_v = xT.rearrange("(kc p) t -> p kc t", p=128)
        for g in range(4):
            P.dma("sp", xst[:, 4 * g:4 * g + 4, :], xT_v[:, 4 * g:4 * g + 4, :], writes=[("xst", g)])
        for kc in range(KC):
            P.dve(lambda e, kc=kc: e.tensor_scalar(
                out=hT[:, kc, :], in0=xst[:, kc, :], scalar1=der[:, kc:kc + 1],
                scalar2=ada[:, kc:kc + 1], op0=ALU.mult, op1=ALU.add),
                reads=[("xst", kc // 4), "0der", "0ada"], writes=["0hT"])
        bscr = A.alloc([1], F32)
        P.barrier(bscr)
        hout = [A.alloc([512], BF16) for _ in range(2)]
        cnt = [0]

        def consume(kc, th, xn, key):
            tsl = slice(th * 512, (th + 1) * 512)
            i = cnt[0] % 2
            cnt[0] += 1
            P.dma("sp", x1T[kc * 128:(kc + 1) * 128, tsl], xn, reads=[key], writes=[("out", "x1", kc, th)])
            P.dve(lambda e, i=i, kc=kc, xn=xn: e.tensor_scalar(
                out=hout[i], in0=xn, scalar1=der[:, 2 * KC + kc:2 * KC + kc + 1],
                scalar2=ada[:, 3 * KC + kc:3 * KC + kc + 1], op0=ALU.mult, op1=ALU.add),
                reads=[key, "0der", "0ada"], writes=[("hout", i)])
            P.dma("sp", h1T[kc * 128:(kc + 1) * 128, tsl], hout[i], reads=[("hout", i)], writes=[("out", "h1", kc, th)])

        sg = [A.alloc([512], F32) for _ in range(2)]
        emit_ffn_in(P, A, PS, hT, actT, win, wbuf, sg, "0")
        act_keys = lambda th: [("0actT", ffb, th * 512) for ffb in range(FFC)]
        emit_proj_ln(P, A, PS, actT, FFC, act_keys, vT, ["0hT"], wout, wobuf, xT, der[:, KC:2 * KC], ["0ada", "0der"],
                     ln_sb[:, 0:KC], ln_sb[:, KC:2 * KC], ones_mat, consume, "0", wbuf_off, [("wbuf", 0), ("wbuf", 1)])
        outs = [("out", n, kc, th) for n in ("x1", "h1") for kc in range(KC) for th in range(TOK // 512)]
        P.add("sp", None, reads=outs)
        P.emit()
    return nc


def fm(v):
    return np.ascontiguousarray(v.reshape(-1, 128).T)


def tile_w(w, fw):
    K, F = w.shape
    return np.ascontiguousarray(w.reshape(K // 128, 128, F // fw, fw).transpose(2, 1, 0, 3))


def ffn_in_tiles(w):
    g = w[:, :DFF].reshape(KC, 128, FFC, 128)
    u = w[:, DFF:].reshape(KC, 128, FFC, 128)
    gu = np.stack([g, u], axis=3)
    gu = gu.reshape(KC, 128, FFC // 2, 512)
    return np.ascontiguousarray(gu.transpose(2, 1, 0, 3))


def ada_tiles(w_ada, b_ada, vec_ids):
    cols = np.concatenate([np.arange(v * D, (v + 1) * D) for v in vec_ids])
    w = w_ada[:, cols]
    return tile_w(w, 512), fm(b_ada[cols])


NBLK = S // 512
WQC = 898


def build_l2(nblk=NBLK):
    nc = bass.Bass("TRN2", target_bir_lowering=False)
    hT_d = nc.dram_tensor("hT", [D, S], BF16, kind="ExternalInput").ap()
    wq_d = nc.dram_tensor("wq", [128, KC, WQC], F32, kind="ExternalInput").ap()
    cw_d = nc.dram_tensor("convw", [128, 12], F32, kind="ExternalInput").ap()
    sc_d = nc.dram_tensor("scal", [128, 8], F32, kind="ExternalInput").ap()
    lam_d = nc.dram_tensor("lamp", [128, 256], F32, kind="ExternalInput").ap()
    bias_d = nc.dram_tensor("biasT", [128, 256], F32, kind="ExternalInput").ap()
    cst_d = nc.dram_tensor("consts", [128, 384], F32, kind="ExternalInput").ap()
    attn_d = nc.dram_tensor("attnT", [128, S], BF16, kind="ExternalOutput").ap()
    dn_d = nc.dram_tensor("dnT", [128, S], BF16, kind="ExternalOutput").ap()
    hT_v = hT_d.rearrange("(kc p) t -> p kc t", p=128)
    import contextlib
    with contextlib.ExitStack() as st:
        NB = 188 * 1024
        arena_t = st.enter_context(nc.sbuf_tensor("arena", [128, NB // 4], F32))
        PS = [st.enter_context(nc.psum_tensor("ps%d" % i, [128, 512], F32)) for i in range(8)]
        A = Arena(arena_t, NB)
        P = Prog(nc)
        al = A.alloc
        wq = al([KC, WQC], BF16)
        cw = al([12], F32)
        sc = al([8], F32)
        lamp = al([256], F32)
        biasT = al([256], F32)
        cst = al([384], F32)
        ident = cst[:, 0:128]
        U = cst[:, 128:256]
        SL = cst[:, 256:384]
        identb = al([128], BF16)
        onesF = al([128], F32)
        negones = al([128], F32)
        ones128 = al([128], F32)
        onesb = al([128], BF16)
        one_col = al([1], F32)
        P.dma("pool", wq, wq_d, writes=["wq"])
        P.dma("sp", cw, cw_d, writes=["par"])
        P.dma("sp", sc, sc_d, writes=["par"])
        P.dma("sp", lamp, lam_d, writes=["par"])
        P.dma("sp", biasT, bias_d, writes=["par"])
        P.dma("sp", cst, cst_d, writes=["par"])
        P.dve(lambda e: e.memset(onesF, 1.0), writes=["cst2"])
        P.dve(lambda e: e.memset(negones, -1.0), writes=["cst2"])
        P.dve(lambda e: e.memset(ones128, 1.0 / 128), writes=["cst2"])
        P.dve(lambda e: e.memset(onesb, 1.0), writes=["cst2"])
        P.dve(lambda e: e.memset(one_col, 1.0), writes=["cst2"])
        zerob = al([128], BF16)
        zero512 = al([512], BF16)
        P.dve(lambda e: e.memset(zerob, 0.0), writes=["cst2"])
        P.dve(lambda e: e.memset(zero512, 0.0), writes=["cst2"])
        der = al([8], F32)
        P.act(lambda e: e.activation(out=der[:, 0:1], in_=sc[:, 0:1], func=AF.Exp), reads=["par"], writes=["der"])
        P.dve(lambda e: e.tensor_scalar(out=der[:, 0:1], in0=der[:, 0:1], scalar1=-1.0, scalar2=None, op0=ALU.mult),
              reads=["der"], writes=["der"])
        P.dve(lambda e: e.tensor_scalar(out=der[:, 1:2], in0=sc[:, 4:5], scalar1=0.8, scalar2=None, op0=ALU.mult),
              reads=["par"], writes=["der"])
        lt = al([128], F32)
        P.dve(lambda e: e.tensor_tensor(out=lt[:, 0:64], in0=lamp[:, 0:64], in1=lamp[:, 64:128], op=ALU.mult),
              reads=["par"], writes=["lt"])
        P.dve(lambda e: e.tensor_tensor(out=lt[:, 64:128], in0=lamp[:, 128:192], in1=lamp[:, 192:256], op=ALU.mult),
              reads=["par", "lt"], writes=["lt"])
        P.dve(lambda e: e.reduce_sum(out=der[:, 2:3], in_=lt[:, 0:64], axis=AX.X), reads=["lt", "der"], writes=["der"])
        P.dve(lambda e: e.reduce_sum(out=der[:, 3:4], in_=lt[:, 64:128], axis=AX.X), reads=["lt", "der"], writes=["der"])
        P.act(lambda e: e.activation(out=der[:, 2:4], in_=der[:, 2:4], func=AF.Exp), reads=["der"], writes=["der"])
        P.dve(lambda e: e.tensor_tensor(out=der[:, 4:5], in0=der[:, 3:4], in1=der[:, 2:3], op=ALU.subtract),
              reads=["der"], writes=["der"])
        P.dve(lambda e: e.tensor_scalar(out=der[:, 4:5], in0=der[:, 4:5], scalar1=-0.2, scalar2=None, op0=ALU.add),
              reads=["der"], writes=["der"])
        nea, subw, neglam = der[:, 0:1], der[:, 1:2], der[:, 4:5]
        dtb, b31, normw = sc[:, 1:2], sc[:, 2:3], sc[:, 3:4]
        qT = al([S], BF16)
        kT = al([S], BF16)
        Vtok = al([S // 128, 128], BF16)
        hblk = [al([KC, 512], BF16)]
        raw = [al([515], F32) for _ in range(3)]
        cs = [al([512], F32) for _ in range(3)]
        cs0 = [cs[0], al([512], F32)]
        zs2 = [al([512], F32) for _ in range(2)]
        ba2 = [al([512], F32) for _ in range(2)]
        cs1 = [cs[1], al([512], F32)]
        cs2 = [cs[2], al([512], F32)]
        tA = al([512], F32)
        tB = al([512], F32)
        vtmp = al([512], F32)
        Sst = al([128], F32)
        NCH = 4
        ch = []
        for c in range(NCH):
            d = {}
            for nm, w in (("kn", 128), ("v", 128), ("kbg", 128), ("kdec", 128), ("u", 128), ("o", 128),
                          ("wT", 128), ("sm", 16), ("gU", 128), ("dm", 256), ("X", 128), ("Y", 128),
                          ("X2", 128), ("Y2", 128), ("PT", 128), ("PT2", 128), ("qkT", 128)):
                d[nm] = al([w], F32)
            ch.append(d)
        seqt = [{nm: al([128], F32) for nm in ("vnew", "o1s")} for _ in range(2)]
        Pm = [[al([512], BF16) for _ in range(2)] for _ in range(2)]
        tb_ = [al([256], F32) for _ in range(2)]
        at0 = al([512], F32)
        at1 = al([512], F32)
        at2 = al([512], F32)
        ost = al([512], BF16)
        dst = al([512], BF16)
        for w in range(3):
            P.dve(lambda e, w=w: e.memset(raw[w][:, 0:3], 0.0), writes=[("raw", w)])
        P.dve(lambda e: e.memset(Sst, 0.0), writes=["S"])

        def K(c, nm):
            return ("ch", c, nm)

        def emit_ip(tb):
            tsl = slice(tb * 512, (tb + 1) * 512)
            csb = [cs0[tb % 2], cs1[tb % 2], cs2[tb % 2]]
            ba = ba2[tb % 2]
            zs = zs2[tb % 2]
            hb = hblk[0]
            hk = ("hblk", 0)
            P.dma("pool", hb, hT_v[:, :, tsl], writes=[hk])
            for ob in range(8):
                pa = PS[ob % 2]
                pk = ("ps", ob % 2)
                M = 128 if ob < 7 else 2

                def mm(e, ob=ob, pa=pa, M=M, hb=hb):
                    inst = None
                    for kc in range(KC):
                        inst = e.matmul(pa[0:M, :], lhsT=wq[:, kc, ob * 128:ob * 128 + M], rhs=hb[:, kc, :],
                                        start=(kc == 0), stop=(kc == KC - 1))
                    return inst
                P.pe(mm, reads=["wq", hk], writes=[pk], cost=3.6 if ob < 7 else 1.2)
                if ob == 0:
                    P.act(lambda e, pa=pa, tsl=tsl: e.activation(out=qT[:, tsl], in_=pa[:, :], func=AF.Identity),
                          writes=[pk, ("qT", tb)])
                elif ob == 1:
                    P.act(lambda e, pa=pa, tsl=tsl: e.activation(out=kT[:, tsl], in_=pa[:, :], func=AF.Identity),
                          writes=[pk, ("kT", tb)])
                elif ob == 2:
                    P.act(lambda e, pa=pa: e.activation(out=vtmp, in_=pa[:, :], func=AF.Identity), writes=[pk, "vtmp"])

                    def tr(e):
                        inst = None
                        for j in range(4):
                            inst = e.transpose(out=PS[2][:, j * 128:(j + 1) * 128], in_=vtmp[:, j * 128:(j + 1) * 128],
                                               identity=ident)
                        return inst
                    P.pe(tr, reads=["vtmp", "par"], writes=[("ps", 2)])
                    P.dve(lambda e, tb=tb: e.tensor_copy(
                        out=Vtok[:, tb * 4:(tb + 1) * 4, :].rearrange("p a b -> p (a b)"), in_=PS[2][:, 0:512]),
                        writes=[("ps", 2), ("V", tb)])
                elif ob in (3, 4, 5):
                    w = ob - 3
                    P.act(lambda e, pa=pa, w=w: e.activation(out=raw[w][:, 3:515], in_=pa[:, :], func=AF.Identity),
                          writes=[pk, ("raw", w)])
                    P.dve(lambda e, w=w: e.tensor_scalar(out=tA, in0=raw[w][:, 0:512], scalar1=cw[:, w * 4:w * 4 + 1],
                                                         scalar2=None, op0=ALU.mult),
                          reads=[("raw", w), "par"], writes=["tA"])
                    for i in range(1, 4):
                        P.dve(lambda e, w=w, i=i: e.scalar_tensor_tensor(
                            out=tA, in0=raw[w][:, i:i + 512], scalar=cw[:, w * 4 + i:w * 4 + i + 1], in1=tA,
                            op0=ALU.mult, op1=ALU.add), reads=[("raw", w), "par", "tA"], writes=["tA"])
                    P.dve(lambda e, w=w: e.tensor_copy(out=raw[w][:, 0:3], in_=raw[w][:, 512:515]),
                          reads=[("raw", w)], writes=[("raw", w)])
                    P.act(lambda e, w=w: e.activation(out=csb[w], in_=tA, func=AF.Silu), reads=["tA"], writes=[("cs", w, tb % 2)])
                    if w < 2:
                        P.act(lambda e, w=w: e.activation(out=tB, in_=csb[w], func=AF.Square), reads=[("cs", w, tb % 2)],
                              writes=["tB"])
                        P.pe(lambda e: e.matmul(PS[2][:, :], lhsT=onesF, rhs=tB, start=True, stop=True),
                             reads=["tB", "cst2"], writes=[("ps", 2)])
                        P.dve(lambda e: e.tensor_scalar(out=tB, in0=PS[2][:, :], scalar1=RMS_EPS, scalar2=None,
                                                        op0=ALU.add), writes=[("ps", 2), "tB"])
                        P.act(lambda e: e.activation(out=tB, in_=tB, func=AF.Sqrt), reads=["tB"], writes=["tB"])
                        P.dve(lambda e: e.reciprocal(out=tB, in_=tB), reads=["tB"], writes=["tB"])
                        sc_ = (128.0 ** -0.5) if w == 0 else 1.0
                        P.dve(lambda e, w=w, sc_=sc_: e.scalar_tensor_tensor(
                            out=csb[w], in0=csb[w], scalar=sc_, in1=tB, op0=ALU.mult, op1=ALU.mult),
                            reads=[("cs", w, tb % 2), "tB"], writes=[("cs", w, tb % 2)])
                elif ob == 6:
                    P.act(lambda e, pa=pa: e.activation(out=zs, in_=pa[:, :], func=AF.Silu), writes=[pk, ("zs", tb % 2)])
                else:
                    P.act(lambda e, pa=pa: e.activation(out=ba[0:2, :], in_=pa[0:2, :], func=AF.Identity),
                          writes=[pk, ("ba", tb % 2)])

        emit_ip(0)
        for tb in range(nblk):
            tsl = slice(tb * 512, (tb + 1) * 512)
            zs = zs2[tb % 2]
            zk = ("zs", tb % 2)
            qk_ = ("cs", 0, tb % 2)
            qn, kn_f, cv, ba = cs0[tb % 2], cs1[tb % 2], cs2[tb % 2], ba2[tb % 2]
            k1_, k2_, kb_ = ("cs", 1, tb % 2), ("cs", 2, tb % 2), ("ba", tb % 2)
            iP0 = len(P.ops)
            cl = [(c, c) for c in range(NCH)]
            NIT = 6

            def each(fn):
                for c, cb in cl:
                    fn(c, ch[c], slice(cb * 128, (cb + 1) * 128), PS[3 + c], ("ps", 3 + c),
                       PS[7][:, 16 * c:16 * c + 16], ("ps", 7))

            def s1(c, d, csl, ps, pk, ps2, pk2, qn=qn, kn_f=kn_f, cv=cv, ba=ba):
                def f(e):
                    e.transpose(out=ps[:, 0:128], in_=kn_f[:, csl], identity=ident)
                    e.transpose(out=ps[:, 128:256], in_=cv[:, csl], identity=ident)
                    e.matmul(ps[:, 256:384], lhsT=kn_f[:, csl], rhs=kn_f[:, csl], start=True, stop=True)
                    e.matmul(ps[:, 384:512], lhsT=kn_f[:, csl], rhs=qn[:, csl], start=True, stop=True)
                    return e.transpose(out=ps2[:, 4:6], in_=ba[0:2, csl], identity=ident[0:2, 0:2])
                P.pe(f, reads=[k1_, k2_, qk_, kb_, "par"], writes=[pk, pk2])
            each(s1)

            def s2(c, d, csl, ps, pk, ps2, pk2):
                sm = d["sm"]
                P.act(lambda e: e.activation(out=d["kn"], in_=ps[:, 0:128], func=AF.Identity), writes=[pk, K(c, "kn")])
                P.dve(lambda e: e.tensor_copy(out=d["v"], in_=ps[:, 128:256]), writes=[pk, K(c, "v")])
                P.act(lambda e: e.activation(out=sm[:, 0:1], in_=ps2[:, 4:5], func=AF.Sigmoid), writes=[pk2, K(c, "sm")])
                P.act(lambda e: e.activation(out=sm[:, 2:3], in_=ps2[:, 5:6], func=AF.Exp, bias=dtb),
                      reads=["par"], writes=[pk2, K(c, "sm")])
                P.act(lambda e: e.activation(out=sm[:, 2:3], in_=sm[:, 2:3], func=AF.Ln, bias=one_col),
                      reads=[K(c, "sm"), "cst2"], writes=[K(c, "sm")])
                P.dve(lambda e: e.tensor_tensor(out=sm[:, 3:4], in0=sm[:, 2:3], in1=nea, op=ALU.mult),
                      reads=[K(c, "sm"), "der"], writes=[K(c, "sm")])
                P.dve(lambda e: e.tensor_scalar(out=sm[:, 1:2], in0=sm[:, 0:1], scalar1=-1.0, scalar2=None, op0=ALU.mult),
                      reads=[K(c, "sm")], writes=[K(c, "sm")])
                P.dve(lambda e: e.tensor_scalar(out=d["gU"], in0=U, scalar1=sm[:, 3:4], scalar2=None, op0=ALU.mult),
                      reads=[K(c, "sm"), "par"], writes=[K(c, "gU")])
            each(s2)

            def s4(c, d, csl, ps, pk, ps2, pk2):
                g = d["sm"][:, 3:4]
                gU = d["gU"]

                def f(e):
                    e.matmul(ps2[:, 0:1], lhsT=U, rhs=g, start=True, stop=True)
                    e.matmul(ps2[:, 1:2], lhsT=SL, rhs=g, start=True, stop=True)
                    e.matmul(ps2[:, 2:3], lhsT=onesF, rhs=g, start=True, stop=True)
                    e.matmul(ps[:, 0:128], lhsT=gU, rhs=onesF, start=True, stop=False)
                    e.matmul(ps[:, 0:128], lhsT=negones, rhs=gU, start=False, stop=True)
                    e.matmul(ps[:, 128:256], lhsT=gU, rhs=negones, start=True, stop=False)
                    return e.matmul(ps[:, 128:256], lhsT=onesF, rhs=gU, start=False, stop=True)
                P.pe(f, reads=[K(c, "sm"), K(c, "gU"), "par", "cst2"], writes=[pk, pk2])
            each(s4)

            def s5(c, d, csl, ps, pk, ps2, pk2):
                sm = d["sm"]
                P.act(lambda e: e.activation(out=sm[:, 4:6], in_=ps2[:, 0:2], func=AF.Exp), writes=[pk2, K(c, "sm")])
                P.act(lambda e: e.activation(out=sm[:, 7:8], in_=ps2[:, 2:3], func=AF.Exp), writes=[pk2, K(c, "sm")])
                P.dve(lambda e: e.tensor_scalar(out=d["dm"], in0=ps[:, 0:256], scalar1=0.0, scalar2=None, op0=ALU.min),
                      writes=[pk, K(c, "dm")])
                P.act(lambda e: e.activation(out=d["dm"], in_=d["dm"], func=AF.Exp), reads=[K(c, "dm")], writes=[K(c, "dm")])
                P.dve(lambda e: e.tensor_tensor(out=d["dm"][:, 0:128], in0=d["dm"][:, 0:128], in1=SL, op=ALU.mult),
                      reads=[K(c, "dm"), "par"], writes=[K(c, "dm")])
                P.dve(lambda e: e.tensor_tensor(out=d["dm"][:, 128:256], in0=d["dm"][:, 128:256], in1=U, op=ALU.mult),
                      reads=[K(c, "dm"), "par"], writes=[K(c, "dm")])
                P.dve(lambda e: e.tensor_tensor(out=sm[:, 6:7], in0=sm[:, 0:1], in1=sm[:, 4:5], op=ALU.mult),
                      reads=[K(c, "sm")], writes=[K(c, "sm")])
                P.act(lambda e: e.activation(out=d["kbg"], in_=d["kn"], func=AF.Identity, scale=sm[:, 6:7]),
                      reads=[K(c, "sm"), K(c, "kn")], writes=[K(c, "kbg")])
                P.act(lambda e: e.activation(out=d["kdec"], in_=d["kn"], func=AF.Identity, scale=sm[:, 5:6]),
                      reads=[K(c, "sm"), K(c, "kn")], writes=[K(c, "kdec")])
                P.act(lambda e: e.activation(out=d["v"], in_=d["v"], func=AF.Identity, scale=sm[:, 0:1]),
                      reads=[K(c, "sm"), K(c, "v")], writes=[K(c, "v")])
            each(s5)

            def s6(c, d, csl, ps, pk, ps2, pk2):
                P.dve(lambda e: e.scalar_tensor_tensor(out=d["X"], in0=ps[:, 256:384], scalar=d["sm"][:, 1:2],
                                                       in1=d["dm"][:, 0:128], op0=ALU.mult, op1=ALU.mult),
                      reads=[K(c, "sm"), K(c, "dm")], writes=[pk, K(c, "X")])
                P.dve(lambda e: e.tensor_tensor(out=d["qkT"], in0=ps[:, 384:512], in1=d["dm"][:, 128:256], op=ALU.mult),
                      reads=[K(c, "dm")], writes=[pk, K(c, "qkT")])
            each(s6)

            def s8(c, d, csl, ps, pk, ps2, pk2):
                P.pe(lambda e: e.transpose(out=ps[:, 0:128], in_=d["X"], identity=ident), reads=[K(c, "X"), "par"],
                     writes=[pk])
                P.act(lambda e: e.activation(out=d["Y"], in_=ps[:, 0:128], func=AF.Identity), writes=[pk, K(c, "Y")])
                P.dve(lambda e: e.tensor_tensor(out=d["PT"], in0=ps[:, 0:128], in1=ident, op=ALU.add),
                      reads=["par"], writes=[pk, K(c, "PT")])
            each(s8)
            XB, YB, PB = ("X", "X2"), ("Y", "Y2"), ("PT", "PT2")

            def sa_pe(c, d, ps, pk, k):
                X, Y = d[XB[(k - 1) % 2]], d[YB[(k - 1) % 2]]

                def f(e):
                    inst = e.matmul(ps[:, 0:128], lhsT=Y, rhs=X, start=True, stop=True)
                    if k < NIT:
                        inst = e.matmul(ps[:, 128:256], lhsT=X, rhs=Y, start=True, stop=True)
                    return inst
                P.pe(f, reads=[K(c, XB[(k - 1) % 2]), K(c, YB[(k - 1) % 2])], writes=[pk])

            def sa_ev(c, d, ps, pk, k):
                P.act(lambda e: e.activation(out=d[XB[k % 2]], in_=ps[:, 0:128], func=AF.Identity),
                      writes=[pk, K(c, XB[k % 2])])
                if k < NIT:
                    P.dve(lambda e: e.tensor_copy(out=d[YB[k % 2]], in_=ps[:, 128:256]), writes=[pk, K(c, YB[k % 2])])

            def sb_pe(c, d, ps, pk, k):
                P.pe(lambda e: e.matmul(ps[:, 256:384], lhsT=d[XB[k % 2]], rhs=d[PB[(k - 1) % 2]], start=True, stop=True),
                     reads=[K(c, XB[k % 2]), K(c, PB[(k - 1) % 2])], writes=[pk])

            def sb_ev(c, d, ps, pk, k):
                P.dve(lambda e: e.tensor_tensor(out=d[PB[k % 2]], in0=ps[:, 256:384], in1=d[PB[(k - 1) % 2]], op=ALU.add),
                      reads=[K(c, PB[(k - 1) % 2])], writes=[pk, K(c, PB[k % 2])])
            each(lambda c, d, csl, ps, pk, ps2, pk2: sa_pe(c, d, ps, pk, 1))
            each(lambda c, d, csl, ps, pk, ps2, pk2: sa_ev(c, d, ps, pk, 1))
            for k in range(1, NIT + 1):
                def ph_pe(c, d, csl, ps, pk, ps2, pk2, k=k):
                    sb_pe(c, d, ps, pk, k)
                    if k < NIT:
                        sa_pe(c, d, ps, pk, k + 1)
                each(ph_pe)

                def ph_ev(c, d, csl, ps, pk, ps2, pk2, k=k):
                    sb_ev(c, d, ps, pk, k)
                    if k < NIT:
                        sa_ev(c, d, ps, pk, k + 1)
                each(ph_ev)
            PTF = PB[NIT % 2]

            def sf(c, d, csl, ps, pk, ps2, pk2):
                PT = d[PTF]

                def f(e):
                    e.matmul(ps[:, 0:128], lhsT=PT, rhs=d["v"], start=True, stop=True)
                    return e.matmul(ps[:, 128:256], lhsT=d["kbg"], rhs=PT, start=True, stop=True)
                P.pe(f, reads=[K(c, PTF), K(c, "v"), K(c, "kbg")], writes=[pk])
                P.act(lambda e: e.activation(out=d["u"], in_=ps[:, 0:128], func=AF.Identity), writes=[pk, K(c, "u")])
                P.dve(lambda e: e.tensor_copy(out=d["wT"], in_=ps[:, 128:256]), writes=[pk, K(c, "wT")])
            each(sf)
            iP1 = len(P.ops)
            if tb + 1 < nblk:
                emit_ip(tb + 1)
                P.interleave(iP0, iP1, len(P.ops))
            iA = len(P.ops)
            pa, pb, pt_ = PS[5][:, 0:256], PS[5][:, 256:512], PS[6]
            for c, cb in cl:
                d = ch[c]
                csl = slice(cb * 128, (cb + 1) * 128)
                sm = d["sm"]
                sq_ = seqt[c % 2]

                def f1(e, d=d, csl=csl, qn=qn):
                    e.matmul(pa[:, 0:128], lhsT=d["wT"], rhs=Sst, start=True, stop=True)
                    return e.matmul(pa[:, 128:256], lhsT=qn[:, csl], rhs=Sst, start=True, stop=True)
                P.pe(f1, reads=[K(c, "wT"), "S", qk_], writes=[("ps", 5)])
                P.dve(lambda e, d=d, sq_=sq_: e.tensor_tensor(out=sq_["vnew"], in0=d["u"], in1=pa[:, 0:128], op=ALU.subtract),
                      reads=[K(c, "u")], writes=[("ps", 5), ("vnew", c % 2)])
                P.dve(lambda e, sm=sm, sq_=sq_: e.tensor_scalar(out=sq_["o1s"], in0=pa[:, 128:256], scalar1=sm[:, 4:5],
                                                                scalar2=None, op0=ALU.mult),
                      reads=[K(c, "sm")], writes=[("ps", 5), ("o1s", c % 2)])

                def f2(e, d=d, sq_=sq_):
                    e.matmul(pb[:, 0:128], lhsT=d["qkT"], rhs=sq_["vnew"], start=True, stop=True)
                    return e.matmul(pb[:, 128:256], lhsT=d["kdec"], rhs=sq_["vnew"], start=True, stop=True)
                P.pe(f2, reads=[K(c, "qkT"), ("vnew", c % 2), K(c, "kdec")], writes=[("ps", 5)])
                P.dve(lambda e, sm=sm: e.scalar_tensor_tensor(out=Sst, in0=Sst, scalar=sm[:, 7:8], in1=pb[:, 128:256],
                                                              op0=ALU.mult, op1=ALU.add),
                      reads=[K(c, "sm"), "S"], writes=[("ps", 5), "S"])
                P.dve(lambda e, d=d, sq_=sq_: e.tensor_tensor(out=d["o"], in0=sq_["o1s"], in1=pb[:, 0:128], op=ALU.add),
                      reads=[("o1s", c % 2)], writes=[("ps", 5), K(c, "o")])
            def each1(fn):
                for c, cb in cl:
                    fn(c, ch[c])

            def e1(c, d):
                sm = d["sm"]
                P.dve(lambda e: e.memset(sm[:, 8:9], 0.0), reads=[K(c, "sm")], writes=[K(c, "sm")])
                P.act(lambda e: e.activation(out=d["kn"], in_=d["o"], func=AF.Square, accum_out=sm[:, 8:9]),
                      reads=[K(c, "o")], writes=[K(c, "kn"), K(c, "sm")])
            each1(e1)

            def e2(c, d):
                sm = d["sm"]
                P.dve(lambda e: e.tensor_scalar(out=sm[:, 8:9], in0=sm[:, 8:9], scalar1=1.0 / 128, scalar2=RMS_EPS,
                                                op0=ALU.mult, op1=ALU.add), reads=[K(c, "sm")], writes=[K(c, "sm")])
            each1(e2)

            def e3(c, d):
                sm = d["sm"]
                P.act(lambda e: e.activation(out=sm[:, 8:9], in_=sm[:, 8:9], func=AF.Sqrt), reads=[K(c, "sm")],
                      writes=[K(c, "sm")])
            each1(e3)

            def e4(c, d):
                sm = d["sm"]
                P.dve(lambda e: e.reciprocal(out=sm[:, 8:9], in_=sm[:, 8:9]), reads=[K(c, "sm")], writes=[K(c, "sm")])
            each1(e4)

            def e5(c, d):
                sm = d["sm"]
                P.act(lambda e: e.activation(out=d["o"], in_=d["o"], func=AF.Identity, scale=sm[:, 8:9]),
                      reads=[K(c, "sm"), K(c, "o")], writes=[K(c, "o")])
            each1(e5)

            def e6(c, d):
                P.pe(lambda e: e.transpose(out=pt_[:, c * 128:(c + 1) * 128], in_=d["o"], identity=ident),
                     reads=[K(c, "o"), "par"], writes=[("ps", 6)])
            each1(e6)
            P.dve(lambda e, zs=zs: e.scalar_tensor_tensor(out=dst, in0=pt_[:, :], scalar=normw, in1=zs, op0=ALU.mult,
                                                          op1=ALU.mult), reads=[zk, "par"], writes=[("ps", 6), "dst"])
            P.dma("sp", dn_d[:, tsl], dst, reads=["dst"], writes=[("out", "dn", tb)])
            iB = len(P.ops)
            nj = 4 * tb + 4
            p_s = (PS[0], PS[1])
            p_o = (PS[2], PS[3])
            k_s = (("ps", 0), ("ps", 1))
            k_o = (("ps", 2), ("ps", 3))
            PL = (PS[4], PS[7])
            KL = (("ps", 4), ("ps", 7))
            def att_front(j):
                a = j - 4 * tb
                q0 = max(a, 0) * 128
                jsl = slice(j * 128, (j + 1) * 128)
                qsl = slice(tb * 512 + q0, tb * 512 + 512)
                for m in range(2):
                    rows = slice(64 * m, 64 * m + 64)
                    pm_ = Pm[m][j % 2]
                    pmk = ("Pm", m, j % 2)
                    P.pe(lambda e, m=m, rows=rows, jsl=jsl, qsl=qsl, q0=q0: e.matmul(
                        p_s[m][:, q0:512], lhsT=kT[rows, jsl], rhs=qT[rows, qsl], start=True, stop=True),
                        reads=[("kT", j // 4), ("qT", tb)], writes=[k_s[m]])
                    band_lo = q0 if a >= -1 else 512
                    band_hi = min(512, (a + 2) * 128) if a >= -1 else 512
                    if a >= -1:
                        bc0 = 0 if a >= 0 else 128
                        wdt = band_hi - band_lo
                        P.dve(lambda e, m=m, band_lo=band_lo, band_hi=band_hi, bc0=bc0, wdt=wdt: e.scalar_tensor_tensor(
                            out=tb_[m][:, 0:wdt], in0=p_s[m][:, band_lo:band_hi], scalar=0.125,
                            in1=biasT[:, bc0:bc0 + wdt], op0=ALU.mult, op1=ALU.add),
                            reads=["par"], writes=[k_s[m], ("tb", m)])
                        P.act(lambda e, m=m, band_lo=band_lo, band_hi=band_hi, wdt=wdt, pm_=pm_: e.activation(
                            out=pm_[:, band_lo:band_hi], in_=tb_[m][:, 0:wdt], func=AF.Exp),
                            reads=[("tb", m)], writes=[pmk])
                    if band_hi < 512 or a < -1:
                        lo = band_hi if a >= -1 else 0
                        P.act(lambda e, m=m, lo=lo, pm_=pm_: e.activation(out=pm_[:, lo:512], in_=p_s[m][:, lo:512],
                                                                        func=AF.Exp, scale=0.125, bias=b31),
                              reads=["par"], writes=[k_s[m], pmk])

            def att_back(j):
                a = j - 4 * tb
                q0 = max(a, 0) * 128
                for m in range(2):
                    pm_ = Pm[m][j % 2]
                    pmk = ("Pm", m, j % 2)
                    P.pe(lambda e, m=m, j=j, q0=q0, pm_=pm_: e.matmul(
                        p_o[m][:, q0:512], lhsT=Vtok[:, j, :], rhs=pm_[:, q0:512], start=(j == 0), stop=False),
                        reads=[("V", j // 4), pmk], writes=[k_o[m]])
                    P.pe(lambda e, m=m, j=j, q0=q0, pm_=pm_: e.matmul(
                        PL[m][:, q0:512], lhsT=onesb, rhs=pm_[:, q0:512], start=(j == 0), stop=False),
                        reads=["cst2", pmk], writes=[KL[m]])
            for step in range(nj + 1):
                if step < nj:
                    att_front(step)
                if step >= 1:
                    att_back(step - 1)
            for m in range(2):
                P.pe(lambda e, m=m: e.matmul(p_o[m][:, :], lhsT=zerob, rhs=zero512, start=False, stop=True),
                     reads=["cst2"], writes=[k_o[m]])
                P.pe(lambda e, m=m: e.matmul(PL[m][:, :], lhsT=zerob, rhs=zero512, start=False, stop=True),
                     reads=["cst2"], writes=[KL[m]])
            for m, t, r in ((0, at0, at2), (1, at1, at2)):
                P.dve(lambda e, m=m, r=r: e.reciprocal(out=r, in_=PL[m][:, :]), writes=[KL[m], ("atr", 0)])
                P.dve(lambda e, m=m, t=t, r=r: e.tensor_tensor(out=t, in0=p_o[m][:, :], in1=r, op=ALU.mult),
                      reads=[("atr", 0)], writes=[k_o[m], ("at", m)])
            P.dve(lambda e: e.scalar_tensor_tensor(out=at0, in0=at1, scalar=neglam, in1=at0, op0=ALU.mult, op1=ALU.add),
                  reads=[("at", 1), "der"], writes=[("at", 0)])
            P.act(lambda e: e.activation(out=at2, in_=at0, func=AF.Square), reads=[("at", 0)], writes=[("atr", 0)])
            P.pe(lambda e: e.matmul(PS[0][:, :], lhsT=ones128, rhs=at2, start=True, stop=True),
                 reads=[("atr", 0), "cst2"], writes=[("ps", 0)])
            P.dve(lambda e: e.tensor_scalar(out=at2, in0=PS[0][:, :], scalar1=LN_EPS, scalar2=None, op0=ALU.add),
                  writes=[("ps", 0), ("atr", 0)])
            P.act(lambda e: e.activation(out=at2, in_=at2, func=AF.Sqrt), reads=[("atr", 0)], writes=[("atr", 0)])
            P.dve(lambda e: e.reciprocal(out=at2, in_=at2), reads=[("atr", 0)], writes=[("atr", 0)])
            P.dve(lambda e: e.tensor_tensor(out=at0, in0=at0, in1=at2, op=ALU.mult), reads=[("atr", 0), ("at", 0)],
                  writes=[("at", 0)])
            P.act(lambda e: e.activation(out=ost, in_=at0, func=AF.Identity, scale=subw), reads=[("at", 0), "der"],
                  writes=["ost"])
            P.dma("sp", attn_d[:, tsl], ost, reads=["ost"], writes=[("out", "at", tb)])
            iC = len(P.ops)
            P.interleave(iA, iB, iC)
        outs = [("out", n, tb) for n in ("dn", "at") for tb in range(nblk)]
        P.add("sp", None, reads=outs)
        P.emit()
    return nc


def t5_bucket_np(rel):
    n = np.maximum(rel, 0)
    max_exact = 16
    nf = np.maximum(n, max_exact).astype(np.float32)
    large = max_exact + (np.log(nf / np.float32(max_exact)) / np.float32(math.log(128 / max_exact))
                         * np.float32(32 - max_exact)).astype(np.int32)
    large = np.minimum(large, 31)
    return np.where(n < max_exact, n, large)


def l2_inputs(h, hT_all, w_in, conv_w, a_log, dt_bias, dn_norm_w, diff_lambda, subln_w, rel_bias):
    cols = np.concatenate([np.arange(o + h * 128, o + (h + 1) * 128) for o in range(0, 7 * 1024, 1024)]
                          + [np.array([7168 + h, 7176 + h])])
    wq = np.ascontiguousarray(w_in[:, cols].reshape(KC, 128, WQC).transpose(1, 0, 2))
    cw = np.zeros((128, 12), np.float32)
    for w in range(3):
        cw[:, w * 4:(w + 1) * 4] = conv_w[:, w * 1024 + h * 128:w * 1024 + (h + 1) * 128].T
    sc = np.zeros((128, 8), np.float32)
    sc[:, 0] = a_log[h]
    sc[:, 1] = dt_bias[h]
    sc[:, 2] = rel_bias[31, h]
    sc[:, 3] = dn_norm_w
    sc[:, 4] = subln_w
    lamp = np.ascontiguousarray(np.broadcast_to(diff_lambda.reshape(1, 256), (128, 256)))
    kk = np.arange(128)[:, None]
    qq = np.arange(256)[None, :]
    rel = qq - kk
    bt = np.where(rel >= 0, rel_bias[t5_bucket_np(rel), h], np.float32(-30000.0)).astype(np.float32)
    cst = np.zeros((128, 384), np.float32)
    cst[:, 0:128] = np.eye(128, dtype=np.float32)
    a = np.arange(128)
    cst[:, 128:256] = (a[:, None] <= a[None, :])
    cst[:, 256:384] = (a[:, None] > a[None, :])
    return {"hT": hT_all, "wq": wq, "convw": cw, "scal": sc, "lamp": lamp, "biasT": bt, "consts": cst}


def build_l3():
    nc = bass.Bass("TRN2", target_bir_lowering=False)
    ada_d = nc.dram_tensor("ada", [128, 9 * KC], F32, kind="ExternalInput").ap()
    aT_d = nc.dram_tensor("aT", [1024, TOK], BF16, kind="ExternalInput").ap()
    bT_d = nc.dram_tensor("bT", [1024, TOK], BF16, kind="ExternalInput").ap()
    h1_d = nc.dram_tensor("h1T", [D, TOK], BF16, kind="ExternalInput").ap()
    x1_d = nc.dram_tensor("x1T", [D, TOK], F32, kind="ExternalInput").ap()
    wg_d = nc.dram_tensor("wg", [KC, 128, KC, 256], F32, kind="ExternalInput").ap()
    wab_d = nc.dram_tensor("wab", [KC, 128, KC, 128], F32, kind="ExternalInput").ap()
    wo_d = nc.dram_tensor("wo", [KC, 128, KC, 128], F32, kind="ExternalInput").ap()
    win = nc.dram_tensor("win", [FFC // 2, 128, KC, 512], F32, kind="ExternalInput").ap()
    wout = nc.dram_tensor("wout", [KC, 128, FFC, 128], F32, kind="ExternalInput").ap()
    lngb = nc.dram_tensor("lngb", [128, 4 * KC], F32, kind="ExternalInput").ap()
    x2s = nc.dram_tensor("x2s", [D, TOK], F32).ap()
    outT = nc.dram_tensor("outT", [D, TOK], F32, kind="ExternalOutput").ap()
    import contextlib
    with contextlib.ExitStack() as st:
        NB = 188 * 1024
        arena_t = st.enter_context(nc.sbuf_tensor("arena", [128, NB // 4], F32))
        PS = [st.enter_context(nc.psum_tensor("ps%d" % i, [128, 512], F32)) for i in range(8)]
        A = Arena(arena_t, NB)
        P = Prog(nc)
        wbuf_off = A.off
        wbuf = [A.alloc([KC, 512], BF16) for _ in range(2)]
        wobuf_off = A.off
        wobuf = [A.alloc([FFC, 128], BF16) for _ in range(2)]
        hT_off = A.off
        h2T = A.alloc([KC, TOK], BF16)
        vT3 = A.alloc([KC, 512], F32, off=hT_off)
        act_off = A.off
        actT = A.alloc([FFC, TOK], BF16)
        ones_mat = A.alloc([128], F32)
        ln_sb = A.alloc([4 * KC], F32)
        ada = A.alloc([9 * KC], F32)
        der = A.alloc([2 * KC], F32)
        bscr = A.alloc([1], F32)
        sg = [A.alloc([512], F32) for _ in range(2)]
        h1T = A.alloc([KC, TOK], BF16, off=act_off)
        aT = A.alloc([8, TOK], BF16, off=act_off + 32 * 1024)
        bT = A.alloc([8, TOK], BF16, off=act_off + 48 * 1024)
        wgb = [A.alloc([KC, 256], BF16, off=act_off + 64 * 1024 + i * 8192) for i in range(2)]
        wabb = [A.alloc([KC, 128], BF16, off=act_off + 80 * 1024 + i * 4096) for i in range(2)]
        mergedT = A.alloc([KC, TOK], BF16, off=wbuf_off)
        vT2 = A.alloc([KC, 512], F32, off=act_off)
        P.dve(lambda e: e.memset(ones_mat, 1.0 / D), writes=["ones"])
        P.dma("sp", ln_sb, lngb, writes=["1ln", "2ln"])
        P.dma("sp", ada, ada_d, writes=["ada"])
        P.dve(lambda e: e.tensor_scalar(out=der[:, 0:KC], in0=ada[:, 7 * KC:8 * KC], scalar1=1.0, scalar2=None, op0=ALU.add),
              reads=["ada"], writes=["der"])
        P.dve(lambda e: e.tensor_scalar(out=der[:, KC:2 * KC], in0=ada[:, 8 * KC:9 * KC], scalar1=0.5, scalar2=None,
                                        op0=ALU.mult), reads=["ada"], writes=["der"])
        for th in range(TOK // 512):
            tsl = slice(th * 512, (th + 1) * 512)
            P.dma("sp", h1T[:, :, tsl], h1_d.rearrange("(kc p) t -> p kc t", p=128)[:, :, tsl], writes=[("h1T", th)])
            P.dma("act", aT[:, :, tsl], aT_d.rearrange("(kc p) t -> p kc t", p=128)[:, :, tsl], writes=[("aT", th)])
            P.dma("act", bT[:, :, tsl], bT_d.rearrange("(kc p) t -> p kc t", p=128)[:, :, tsl], writes=[("bT", th)])
        s1o = wobuf_off
        sga = [A.alloc([512], F32, off=s1o + i * 2048) for i in range(2)]
        m1 = [A.alloc([512], F32, off=s1o + 4096 + i * 2048) for i in range(2)]
        n = 0
        for fb in range(KC):
            slot = fb % 2
            P.dma("pool", wgb[slot], wg_d[fb], writes=[("wgb", slot)])
            P.dma("pool", wabb[slot], wab_d[fb], writes=[("wabb", slot)])
            for th in range(TOK // 512):
                tsl = slice(th * 512, (th + 1) * 512)
                i = n % 2
                n += 1
                pga, pgb, pya, pyb = PS[i], PS[2 + i], PS[4 + i], PS[6 + i]

                def mm(e, slot=slot, tsl=tsl, pga=pga, pgb=pgb, pya=pya, pyb=pyb):
                    inst = None
                    for kc in range(KC):
                        inst = e.matmul(pga[:, :], lhsT=wgb[slot][:, kc, 0:128], rhs=h1T[:, kc, tsl],
                                        start=(kc == 0), stop=(kc == KC - 1))
                    for kc in range(KC):
                        inst = e.matmul(pgb[:, :], lhsT=wgb[slot][:, kc, 128:256], rhs=h1T[:, kc, tsl],
                                        start=(kc == 0), stop=(kc == KC - 1))
                    for kc in range(8):
                        inst = e.matmul(pya[:, :], lhsT=wabb[slot][:, kc, :], rhs=aT[:, kc, tsl],
                                        start=(kc == 0), stop=(kc == 7))
                    for kc in range(8):
                        inst = e.matmul(pyb[:, :], lhsT=wabb[slot][:, 8 + kc, :], rhs=bT[:, kc, tsl],
                                        start=(kc == 0), stop=(kc == 7))
                    return inst
                P.pe(mm, reads=[("wgb", slot), ("wabb", slot), ("h1T", th), ("aT", th), ("bT", th)],
                     writes=[("ps", i), ("ps", 2 + i), ("ps", 4 + i), ("ps", 6 + i)])
                P.act(lambda e, i=i, pga=pga: e.activation(out=sga[i], in_=pga[:, :], func=AF.Sigmoid),
                      writes=[("ps", i), ("sga", i)])
                P.dve(lambda e, i=i, pya=pya: e.tensor_tensor(out=m1[i], in0=sga[i], in1=pya[:, :], op=ALU.mult),
                      reads=[("sga", i)], writes=[("ps", 4 + i), ("m1", i)])
                P.act(lambda e, i=i, pgb=pgb: e.activation(out=sga[i], in_=pgb[:, :], func=AF.Sigmoid),
                      reads=[("m1", i)], writes=[("ps", 2 + i), ("sga", i)])
                P.dve(lambda e, i=i, pyb=pyb: e.tensor_tensor(out=sga[i], in0=sga[i], in1=pyb[:, :], op=ALU.mult),
                      reads=[("sga", i)], writes=[("ps", 6 + i), ("sga", i)])
                P.dve(lambda e, i=i, fb=fb, tsl=tsl: e.tensor_tensor(out=mergedT[:, fb, tsl], in0=m1[i], in1=sga[i],
                                                                      op=ALU.add),
                      reads=[("sga", i), ("m1", i)], writes=[("mg", fb, th)])
        P.barrier(bscr)
        hout = [A.alloc([512], F32, off=act_off + 64 * 1024 + i * 2048) for i in range(2)]
        cnt = [0]

        def consume2(kc, th, xn, key):
            tsl = slice(th * 512, (th + 1) * 512)
            P.dma("sp", x2s[kc * 128:(kc + 1) * 128, tsl], xn, reads=[key], writes=[("2xres", kc, th)])
            P.dve(lambda e, kc=kc, xn=xn, tsl=tsl: e.tensor_scalar(
                out=h2T[:, kc, tsl], in0=xn, scalar1=der[:, kc:kc + 1], scalar2=ada[:, 6 * KC + kc:6 * KC + kc + 1],
                op0=ALU.mult, op1=ALU.add), reads=[key, "der", "ada"], writes=["2hT"])
        mg_keys = lambda th: [("mg", fb, th) for fb in range(KC)]
        wo2 = [A.alloc([KC, 128], BF16, off=act_off + 80 * 1024 + i * 4096) for i in range(2)]
        emit_proj_ln(P, A, PS, mergedT, KC, mg_keys, vT2, [], wo_d, wo2, x1_d, ada[:, 5 * KC:6 * KC], ["ada"],
                     ln_sb[:, 0:KC], ln_sb[:, KC:2 * KC], ones_mat, consume2, "1", wobuf_off, [])
        P.barrier(bscr)
        def consume3(kc, th, xn, key):
            tsl = slice(th * 512, (th + 1) * 512)
            P.dma("sp", outT[kc * 128:(kc + 1) * 128, tsl], xn, reads=[key], writes=[("out", kc, th)])
        emit_ffn_in(P, A, PS, h2T, actT, win, wbuf, sg, "2")
        act_keys = lambda th: [("2actT", ffb, th * 512) for ffb in range(FFC)]
        emit_proj_ln(P, A, PS, actT, FFC, act_keys, vT3, ["2hT"], wout, wobuf, x2s, der[:, KC:2 * KC], ["der"],
                     ln_sb[:, 2 * KC:3 * KC], ln_sb[:, 3 * KC:4 * KC], ones_mat, consume3, "2", wbuf_off,
                     [("wbuf", 0), ("wbuf", 1)])
        outs = [("out", kc, th) for kc in range(KC) for th in range(TOK // 512)]
        P.add("sp", None, reads=outs)
        P.emit()
    return nc


_CACHE = {}


def _prog(name, fn):
    if name not in _CACHE:
        _CACHE[name] = fn()
    return _CACHE[name]


def kernel(x, c, w_ada, b_ada, ln_g, ln_b, w_ffn_in, w_ffn_out, w_in, conv_w, dn_a_log, dn_dt_bias, dn_norm_w,
           diff_lambda, diff_subln_w, rel_bias, w_branch_a, w_branch_b, w_out):
    f32 = lambda a: np.ascontiguousarray(np.asarray(a, dtype=np.float32))
    x, c, w_ada, b_ada, ln_g, ln_b = f32(x), f32(c), f32(w_ada), f32(b_ada), f32(ln_g), f32(ln_b)
    w_ffn_in, w_ffn_out, w_in, conv_w = f32(w_ffn_in), f32(w_ffn_out), f32(w_in), f32(conv_w)
    cores = list(range(NCORES))
    cfm = fm(c[0])
    wt = tile_w(w_ada[0], 128)
    bfm = fm(b_ada[0])
    res = run_bass_kernel_spmd(_prog("l0", build_l0), [
        {"c": cfm, "wada": np.ascontiguousarray(wt[18 * r:18 * (r + 1)]), "bada": np.ascontiguousarray(bfm[:, 18 * r:18 * (r + 1)])}
        for r in cores], core_ids=cores)
    ada = np.ascontiguousarray(np.concatenate([np.asarray(res.results[r]["ada"]) for r in cores], axis=1))
    del wt
    win0 = ffn_in_tiles(w_ffn_in[0, 0])
    wout0 = tile_w(w_ffn_out[0, 0], 128)
    lngb0 = np.concatenate([fm(ln_g[0, 0]), fm(ln_b[0, 0])], axis=1)
    xs = x[0]
    res = run_bass_kernel_spmd(_prog("l1", build_l1), [
        {"xT": np.ascontiguousarray(xs[r * TOK:(r + 1) * TOK].T), "ada": ada, "win": win0, "wout": wout0, "lngb": lngb0}
        for r in cores], core_ids=cores)
    x1T = [np.asarray(res.results[r]["x1T"]) for r in cores]
    h1T = [np.asarray(res.results[r]["h1T"]) for r in cores]
    del win0, wout0
    hT_all = np.ascontiguousarray(np.concatenate(h1T, axis=1))
    res = run_bass_kernel_spmd(_prog("l2", build_l2), [
        l2_inputs(h, hT_all, w_in[0], conv_w[0], f32(dn_a_log)[0], f32(dn_dt_bias)[0], f32(dn_norm_w)[0],
                  f32(diff_lambda)[0], f32(diff_subln_w)[0], f32(rel_bias)) for h in cores], core_ids=cores)
    aT_all = np.concatenate([np.asarray(res.results[h]["attnT"]) for h in cores], axis=0)
    bT_all = np.concatenate([np.asarray(res.results[h]["dnT"]) for h in cores], axis=0)
    del hT_all
    ga = w_in[0][:, 7184:7184 + D].reshape(KC, 128, KC, 128)
    gb = w_in[0][:, 7184 + D:7184 + 2 * D].reshape(KC, 128, KC, 128)
    wg = np.ascontiguousarray(np.concatenate([ga, gb], axis=3).transpose(2, 1, 0, 3))
    wab = tile_w(np.concatenate([f32(w_branch_a)[0], f32(w_branch_b)[0]], axis=0), 128)
    wo = tile_w(f32(w_out)[0], 128)
    win1 = ffn_in_tiles(w_ffn_in[0, 1])
    wout1 = tile_w(w_ffn_out[0, 1], 128)
    lngb1 = np.concatenate([fm(ln_g[0, 1]), fm(ln_b[0, 1]), fm(ln_g[0, 2]), fm(ln_b[0, 2])], axis=1)
    res = run_bass_kernel_spmd(_prog("l3", build_l3), [
        {"ada": ada, "aT": np.ascontiguousarray(aT_all[:, r * TOK:(r + 1) * TOK]),
         "bT": np.ascontiguousarray(bT_all[:, r * TOK:(r + 1) * TOK]), "h1T": h1T[r], "x1T": x1T[r],
         "wg": wg, "wab": wab, "wo": wo, "win": win1, "wout": wout1, "lngb": lngb1} for r in cores], core_ids=cores)
    out = np.concatenate([np.asarray(res.results[r]["outT"]).T for r in cores], axis=0)
    return np.ascontiguousarray(out.reshape(1, S, D).astype(np.float32))
```
